# Optimizing a Trainium2 kernel written in Bass

```python
import math
import jax
import jax.numpy as jnp
from jax import lax
import numpy as np

D_MODEL = 2048
BATCH = 4
SEQ = 4096
DEPTH = 4

N_META = 16
CHUNK = 64
BLOCK = 128
PAD = BLOCK - N_META
CONV_W = 4
EPS = 1e-6
NEG = -1e30

DN_HEADS = 6
DN_DK = 128
DN_DV = 128
GLA_HEADS = 5
GLA_DK = 64
GLA_DV = 128
GLA_RANK = 16
GLA_NORMALIZER = 16.0
FOX_HEADS = 5
FOX_D = 128

DN_W = DN_HEADS * DN_DV
DN_KW = DN_HEADS * DN_DK
GLA_KW = GLA_HEADS * GLA_DK
GLA_W = GLA_HEADS * GLA_DV
FOX_W = FOX_HEADS * FOX_D
MIX_W = DN_W + GLA_W + FOX_W
DN_CONV_CH = 2 * DN_KW + DN_W

COL_SIZES = (DN_KW, DN_KW, DN_W, DN_W, DN_HEADS, DN_HEADS,
             GLA_KW, GLA_KW, GLA_W, GLA_W, GLA_RANK,
             FOX_W, FOX_W, FOX_W, FOX_W, FOX_HEADS)
IN_COLS = sum(COL_SIZES)

kernel_name = 'hymba_style_deltanet_gla_fox_hybrid'


def rms_norm(x, w):
    xf = x.astype(jnp.float32)
    y = xf * lax.rsqrt(jnp.mean(xf * xf, axis=-1, keepdims=True) + EPS)
    return (y * w.astype(jnp.float32)).astype(x.dtype)


def l2_normalize(t):
    return t * lax.rsqrt(jnp.sum(t * t, axis=-1, keepdims=True) + EPS)


def split_cols(p):
    parts, start = [], 0
    for n in COL_SIZES:
        parts.append(p[..., start:start + n])
        start += n
    return parts


def to_heads(t, n_heads):
    b, s, w = t.shape
    return t.reshape(b, s, n_heads, w // n_heads).transpose(0, 2, 1, 3)


def to_chunks(t):
    return t.reshape(t.shape[:2] + (t.shape[2] // CHUNK, CHUNK) + t.shape[3:])


def from_chunks(o):
    o = jnp.moveaxis(o, 0, 2)
    b, h, n, c, d = o.shape
    return o.reshape(b, h, n * c, d).transpose(0, 2, 1, 3)


def causal_depthwise_conv(x, w):
    return lax.conv_general_dilated(
        x, w[:, None, :], window_strides=(1,), padding=[(CONV_W - 1, 0)],
        dimension_numbers=('NWC', 'WIO', 'NWC'), feature_group_count=x.shape[-1])


def gated_head_norm(o, z, w):
    o = o * lax.rsqrt(jnp.mean(o * o, axis=-1, keepdims=True) + EPS) * w
    return o.reshape(z.shape) * jax.nn.silu(z)


def gated_delta_rule_chunked(q, k, v, log_a, beta):
    qc, kc, vc = to_chunks(q), to_chunks(k), to_chunks(v)
    g = jnp.cumsum(to_chunks(log_a), axis=-1)
    bc = to_chunks(beta)
    causal = jnp.tril(jnp.ones((CHUNK, CHUNK), dtype=bool))
    strict = jnp.tril(jnp.ones((CHUNK, CHUNK), dtype=bool), -1)
    decay = jnp.exp(jnp.where(causal, g[..., :, None] - g[..., None, :], -jnp.inf))
    kb = kc * bc[..., None]
    a_mat = jnp.where(strict, jnp.einsum('bhncd,bhnsd->bhncs', kb, kc) * decay, 0.0)
    eye = jnp.eye(CHUNK, dtype=jnp.float32)
    rhs = jnp.concatenate([vc * bc[..., None], kb * jnp.exp(g)[..., None]], axis=-1)
    sol = lax.linalg.triangular_solve(eye + a_mat, rhs, left_side=True, lower=True,
                                      unit_diagonal=True)
    u, w = sol[..., :DN_DV], sol[..., DN_DV:]
    qk = jnp.where(causal, jnp.einsum('bhncd,bhnsd->bhncs', qc, kc) * decay, 0.0)
    q_dec = qc * jnp.exp(g)[..., None]
    k_dec = kc * jnp.exp(g[..., -1:] - g)[..., None]
    g_last = jnp.exp(g[..., -1])
    xs = tuple(jnp.moveaxis(t, 2, 0) for t in (qk, q_dec, k_dec, u, w, g_last))

    def step(s, inp):
        qk_c, qd, kd, u_c, w_c, gl = inp
        v_new = u_c - jnp.einsum('bhck,bhkv->bhcv', w_c, s)
        o = jnp.einsum('bhck,bhkv->bhcv', qd, s) + jnp.einsum('bhcs,bhsv->bhcv', qk_c, v_new)
        s = s * gl[..., None, None] + jnp.einsum('bhck,bhcv->bhkv', kd, v_new)
        return s, o

    b, h = q.shape[:2]
    s0 = jnp.zeros((b, h, DN_DK, DN_DV), jnp.float32)
    _, o = lax.scan(step, s0, xs)
    return from_chunks(o)


def gla_chunked(q, k, v, log_g):
    qc, kc, vc = to_chunks(q), to_chunks(k), to_chunks(v)
    g = jnp.cumsum(to_chunks(log_g), axis=3)
    q_dec = qc * jnp.exp(g)
    k_dec = kc * jnp.exp(g[..., -1:, :] - g)
    g_last = jnp.exp(g[..., -1, :])
    xs = tuple(jnp.moveaxis(t, 2, 0) for t in (qc, kc, vc, g, q_dec, k_dec, g_last))
    causal = jnp.tril(jnp.ones((CHUNK, CHUNK), dtype=bool))[:, :, None]

    def step(s, inp):
        q_c, k_c, v_c, g_c, qd, kd, gl = inp
        dec = jnp.exp(jnp.where(causal, g_c[:, :, :, None, :] - g_c[:, :, None, :, :], -jnp.inf))
        attn = jnp.einsum('bhid,bhjd,bhijd->bhij', q_c, k_c, dec)
        o = jnp.einsum('bhid,bhdv->bhiv', qd, s) + jnp.einsum('bhij,bhjv->bhiv', attn, v_c)
        s = s * gl[..., :, None] + jnp.einsum('bhjd,bhjv->bhdv', kd, v_c)
        return s, o

    b, h = q.shape[:2]
    s0 = jnp.zeros((b, h, GLA_DK, GLA_DV), jnp.float32)
    _, o = lax.scan(step, s0, xs)
    return from_chunks(o)


def forgetting_attention(q, k, v, log_f):
    b, h, p, d = q.shape
    nb = p // BLOCK
    c = jnp.cumsum(log_f, axis=-1)
    key_pos = jnp.arange(p)
    qb = jnp.moveaxis(q.reshape(b, h, nb, BLOCK, d), 2, 0)
    cb = jnp.moveaxis(c.reshape(b, h, nb, BLOCK), 2, 0)

    def one_block(args):
        i, q_i, c_i = args
        q_pos = i * BLOCK + jnp.arange(BLOCK)
        logits = jnp.einsum('bhqd,bhkd->bhqk', q_i, k) + c_i[..., :, None] - c[:, :, None, :]
        mask = (key_pos[None, :] <= q_pos[:, None]) & (key_pos[None, :] >= PAD)
        probs = jax.nn.softmax(jnp.where(mask, logits, NEG), axis=-1)
        return jnp.einsum('bhqk,bhkd->bhqd', probs, v)

    o = lax.map(one_block, (jnp.arange(nb), qb, cb))
    return jnp.moveaxis(o, 0, 2).reshape(b, h, p, d).transpose(0, 2, 1, 3)


def hybrid_layer(x, norm_w, w_in, conv_w, a_log, dt_bias, dn_norm_w, w_gk2, b_gk,
                 gla_norm_w, b_f, w_out):
    f32 = jnp.float32
    h = rms_norm(x, norm_w)
    h = jnp.pad(h, ((0, 0), (PAD, 0), (0, 0)))
    proj = (h @ w_in).astype(f32)
    (dq, dk, dv, dz, db, da, gq, gk, gv, gz, gl, fq, fk, fv, fz, ff) = split_cols(proj)

    qkv = jax.nn.silu(causal_depthwise_conv(jnp.concatenate([dq, dk, dv], -1), conv_w.astype(f32)))
    cq, ck, cv = qkv[..., :DN_KW], qkv[..., DN_KW:2 * DN_KW], qkv[..., 2 * DN_KW:]
    q_dn = l2_normalize(to_heads(cq, DN_HEADS)) * DN_DK ** -0.5
    k_dn = l2_normalize(to_heads(ck, DN_HEADS))
    v_dn = to_heads(cv, DN_HEADS)
    beta = jax.nn.sigmoid(db).transpose(0, 2, 1)
    log_a = (-jnp.exp(a_log.astype(f32)) * jax.nn.softplus(da + dt_bias.astype(f32))).transpose(0, 2, 1)
    o_dn = gated_delta_rule_chunked(q_dn, k_dn, v_dn, log_a, beta)
    o_dn = gated_head_norm(o_dn, dz, dn_norm_w.astype(f32))

    log_g = jax.nn.log_sigmoid(gl @ w_gk2.astype(f32) + b_gk.astype(f32)) / GLA_NORMALIZER
    o_gla = gla_chunked(to_heads(gq, GLA_HEADS) * GLA_DK ** -0.5, to_heads(gk, GLA_HEADS),
                        to_heads(gv, GLA_HEADS), to_heads(log_g, GLA_HEADS))
    o_gla = gated_head_norm(o_gla, gz, gla_norm_w.astype(f32))

    log_f = jax.nn.log_sigmoid(ff + b_f.astype(f32)).transpose(0, 2, 1)
    o_fox = forgetting_attention(to_heads(fq, FOX_HEADS) * FOX_D ** -0.5, to_heads(fk, FOX_HEADS),
                                 to_heads(fv, FOX_HEADS), log_f)
    o_fox = o_fox.reshape(fz.shape) * jax.nn.silu(fz)

    o = jnp.concatenate([o_dn, o_gla, o_fox], axis=-1)[:, PAD:].astype(x.dtype)
    return x + o @ w_out


def setup_inputs(seed: int = 0) -> dict:
    key = jax.random.key(seed)
    ks = jax.random.split(key, 16)
    f32 = jnp.float32
    x = jax.random.normal(ks[0], (BATCH, SEQ, D_MODEL), f32)
    meta_tokens = jax.random.normal(ks[1], (N_META, D_MODEL), f32)
    norm_w = 1.0 + 0.02 * jax.random.normal(ks[2], (DEPTH, D_MODEL), f32)
    w_in = jax.random.normal(ks[3], (DEPTH, D_MODEL, IN_COLS), f32) * D_MODEL ** -0.5
    conv_w = jax.random.normal(ks[4], (DEPTH, CONV_W, DN_CONV_CH), f32) * CONV_W ** -0.5
    a_log = jnp.log(jax.random.uniform(ks[5], (DEPTH, DN_HEADS), f32, 1.0, 16.0))
    dt = jnp.exp(jax.random.uniform(ks[6], (DEPTH, DN_HEADS), f32, math.log(1e-3), math.log(1e-1)))
    dt_bias = dt + jnp.log(-jnp.expm1(-dt))
    dn_norm_w = 1.0 + 0.02 * jax.random.normal(ks[7], (DEPTH, DN_DV), f32)
    w_gk2 = jax.random.normal(ks[8], (DEPTH, GLA_RANK, GLA_KW), f32) * GLA_RANK ** -0.5
    b_gk = 0.1 * jax.random.normal(ks[9], (DEPTH, GLA_KW), f32)
    gla_norm_w = 1.0 + 0.02 * jax.random.normal(ks[10], (DEPTH, GLA_DV), f32)
    b_f = 2.0 + 0.5 * jax.random.normal(ks[11], (DEPTH, FOX_HEADS), f32)
    w_out = jax.random.normal(ks[12], (DEPTH, MIX_W, D_MODEL), f32) * (MIX_W ** -0.5) * (2 * DEPTH) ** -0.5
    final_norm_w = 1.0 + 0.02 * jax.random.normal(ks[13], (D_MODEL,), f32)
    return {'x': x, 'meta_tokens': meta_tokens, 'norm_w': norm_w, 'w_in': w_in, 'conv_w': conv_w,
            'a_log': a_log, 'dt_bias': dt_bias, 'dn_norm_w': dn_norm_w, 'w_gk2': w_gk2,
            'b_gk': b_gk, 'gla_norm_w': gla_norm_w, 'b_f': b_f, 'w_out': w_out,
            'final_norm_w': final_norm_w}


def reference(x, meta_tokens, norm_w, w_in, conv_w, a_log, dt_bias, dn_norm_w, w_gk2, b_gk,
              gla_norm_w, b_f, w_out, final_norm_w):
    b = x.shape[0]
    meta = jnp.broadcast_to(meta_tokens.astype(x.dtype)[None], (b, N_META, D_MODEL))
    h = jnp.concatenate([meta, x], axis=1)
    for l in range(DEPTH):
        h = hybrid_layer(h, norm_w[l], w_in[l], conv_w[l], a_log[l], dt_bias[l], dn_norm_w[l],
                         w_gk2[l], b_gk[l], gla_norm_w[l], b_f[l], w_out[l])
    return rms_norm(h, final_norm_w)[:, N_META:]
```

```python
import contextlib
import numpy as np
import concourse.bass as bass
import concourse.mybir as mybir
from concourse.bass_utils import run_bass_kernel_spmd

F32 = mybir.dt.float32
BF16 = mybir.dt.bfloat16
AF = mybir.ActivationFunctionType
ALU = mybir.AluOpType

D = 2048
KC = 16
IN_COLS = 7585
EPS = 1e-6
C_DQ, C_DK, C_DV, C_DZ, C_DB, C_DA = 0, 768, 1536, 2304, 3072, 3078
C_GQ, C_GK, C_GV, C_GZ, C_GL = 3084, 3404, 3724, 4364, 5004
C_FQ, C_FK, C_FV, C_FZ, C_FF = 5020, 5660, 6300, 6940, 7580

FM_CHUNKS = []
for i in range(18):
    FM_CHUNKS.append((128, [(0, 128 * i, 128)]))
FM_CHUNKS.append((128, [(0, C_GQ, 128)]))
FM_CHUNKS.append((128, [(0, C_GQ + 128, 128)]))
FM_CHUNKS.append((64, [(0, C_GQ + 256, 64)]))
FM_CHUNKS.append((128, [(0, C_GK, 128)]))
FM_CHUNKS.append((128, [(0, C_GK + 128, 128)]))
FM_CHUNKS.append((64, [(0, C_GK + 256, 64)]))
FM_CHUNKS.append((64, [(0, C_GL, 16), (32, C_DB, 12)]))
for i in range(5):
    FM_CHUNKS.append((128, [(0, C_FQ + 128 * i, 128)]))
for i in range(5):
    FM_CHUNKS.append((128, [(0, C_FK + 128 * i, 128)]))
NFM = len(FM_CHUNKS)
FM_GROUPS = [[0, 1, 2, 3], [4, 5, 6, 7], [8, 9, 10, 11], [12, 13, 14, 15], [16, 17],
             [18, 19, 20, 21], [22, 23, 24], [25, 26, 27, 28], [29, 30, 31, 32], [33, 34]]
CH_GQ, CH_GK, CH_SM, CH_FQ, CH_FK = 18, 21, 24, 25, 30

TM_SEGS = [(0, C_DZ, 768), (768, C_GV, 1280), (2048, C_FV, 1280), (3328, C_DB, 12), (3340, C_FF, 5)]
TMW = 3345
M_DZ, M_GV, M_GZ, M_FV, M_FZ, M_DB, M_DA, M_FF = 0, 768, 1408, 2048, 2688, 3328, 3334, 3340
O_DN, O_GLA, O_FOX = 0, 768, 1408


class Sched:
    COMPUTE = ('pe', 'act', 'dve', 'pool')

    def __init__(self, nc, stack, n_dma_sems=20):
        self.nc = nc
        self.E = {'pe': nc.tensor, 'act': nc.scalar, 'dve': nc.vector, 'pool': nc.gpsimd, 'sp': nc.sync}
        self.psem = {e: stack.enter_context(nc.semaphore('prog_' + e)) for e in self.COMPUTE}
        self.cnt = {e: 0 for e in self.COMPUTE}
        self.queues = ('sp', 'act', 'pool')
        self.dsem = {q: [stack.enter_context(nc.semaphore('d%s%d' % (q, i))) for i in range(n_dma_sems)]
                     for q in self.queues}
        self.dval = {q: [0] * n_dma_sems for q in self.queues}
        self.dnext = {q: 0 for q in self.queues}
        self.waited = {}
        self.xw = {}
        self.pw = {}
        self.rd = {}
        self.n_wait = 0
        self.n_ins = 0

    def _wait(self, eng, src, val, same_ok):
        if src[0] == 'c':
            e = src[1]
            if e == eng and (same_ok or e == 'pe'):
                return
            sem = self.psem[e]
        else:
            sem = self.dsem[src[1]][src[2]]
        key = (eng, src)
        if self.waited.get(key, 0) >= val:
            return
        self.waited[key] = val
        self.E[eng].wait_ge(sem, val)
        self.n_wait += 1

    def _deps(self, eng, reads, writes, pwrites):
        for k in reads:
            for m in (self.xw.get(k), self.pw.get(k)):
                if m:
                    for src, v in m.items():
                        self._wait(eng, src, v, False)
        for k in writes:
            for m in (self.xw.get(k), self.pw.get(k)):
                if m:
                    for src, v in m.items():
                        self._wait(eng, src, v, False)
            m = self.rd.get(k)
            if m:
                for src, v in m.items():
                    self._wait(eng, src, v, True)
        for k in pwrites:
            m = self.xw.get(k)
            if m:
                for src, v in m.items():
                    self._wait(eng, src, v, False)
            m = self.rd.get(k)
            if m:
                for src, v in m.items():
                    self._wait(eng, src, v, True)

    def _record(self, src, val, reads, writes, pwrites):
        for k in reads:
            if k not in writes:
                self.rd.setdefault(k, {})[src] = val
        for k in writes:
            self.xw[k] = {src: val}
            self.pw[k] = {}
            self.rd[k] = {}
        for k in pwrites:
            self.pw.setdefault(k, {})[src] = val

    def op(self, eng, fn, reads=(), writes=(), pwrites=()):
        self._deps(eng, reads, writes, pwrites)
        ins = fn()
        self.cnt[eng] += 1
        ins.then_inc(self.psem[eng], 1)
        self._record(('c', eng), self.cnt[eng], reads, writes, pwrites)
        self.n_ins += 1
        return ins

    def dma(self, q, out, in_, reads=(), writes=(), pwrites=(), **kw):
        eng = q
        self._deps(eng, reads, writes, pwrites)
        i = self.dnext[q]
        self.dnext[q] = (i + 1) % len(self.dsem[q])
        if self.dval[q][i] > 0:
            self._wait(eng, ('d', q, i), self.dval[q][i], False)
        ins = self.E[eng].dma_start(out=out, in_=in_, **kw)
        self.dval[q][i] += 16
        ins.then_inc(self.dsem[q][i], 16)
        self._record(('d', q, i), self.dval[q][i], reads, writes, pwrites)
        self.n_ins += 1
        return ins

    def fence(self):
        for eng in ('pe', 'act', 'dve', 'pool', 'sp'):
            self.finish(eng)

    def finish(self, eng='sp'):
        for e in self.COMPUTE:
            if self.cnt[e] > 0:
                self._wait(eng, ('c', e), self.cnt[e], False)
        for q in self.queues:
            for i in range(len(self.dsem[q])):
                if self.dval[q][i] > 0:
                    self._wait(eng, ('d', q, i), self.dval[q][i], False)


class Prog:
    def __init__(self, S_LEN, L, debug=False, stop_after=None):
        self.S_LEN, self.L, self.debug = S_LEN, L, debug
        self.NB = S_LEN // 128 + 1
        self.P = self.NB * 128
        self.NCH = self.NB * 2
        self.stop_after = stop_after
        nc = self.nc = bass.Bass("TRN2", target_bir_lowering=False)
        P = self.P
        I = lambda n, s: nc.dram_tensor(n, s, F32, kind="ExternalInput").ap()
        self.x = I("x", [S_LEN, D])
        self.meta = I("meta_tokens", [16, D])
        self.norm_w = I("norm_w", [L, D])
        self.w_in = I("w_in", [L, D, IN_COLS])
        self.conv_w = I("conv_w", [L, 4, 2304])
        self.a_log = I("a_log", [L, 6])
        self.dt_bias = I("dt_bias", [L, 6])
        self.dn_norm_w = I("dn_norm_w", [L, 128])
        self.w_gk2 = I("w_gk2", [L, 16, 320])
        self.b_gk = I("b_gk", [L, 320])
        self.gla_norm_w = I("gla_norm_w", [L, 128])
        self.b_f = I("b_f", [L, 5])
        self.w_out = I("w_out", [L, D, D])
        self.final_norm_w = I("final_norm_w", [1, D])
        self.out = nc.dram_tensor("out", [S_LEN, D], F32, kind="ExternalOutput").ap()
        kind = "ExternalOutput" if debug else "Internal"
        self.hres = nc.dram_tensor("hres", [P, D], F32, kind=kind).ap()
        self.projT = nc.dram_tensor("projT", [NFM * 128, P], F32, kind=kind).ap()
        self.projM = nc.dram_tensor("projM", [P, TMW], F32, kind=kind).ap()
        self.og = nc.dram_tensor("og", [P, D], BF16, kind=kind).ap()
        if debug:
            self.hT_dbg = nc.dram_tensor("hT_dbg", [128, KC * P], BF16, kind="ExternalOutput").ap()
        self.uid = 0

    def key(self, s):
        self.uid += 1
        return '%s#%d' % (s, self.uid)

    def build(self):
        nc = self.nc
        with contextlib.ExitStack() as st:
            self.S = Sched(nc, st)
            self.consts(st)
            self.init_hres()
            for l in range(self.L):
                self.phase_a(l)
                self.S.fence()
                if self.stop_after == ('a', l):
                    break
                self.phase_b(l)
                if self.stop_after == ('b', l):
                    break
                self.phase_c(l)
                self.S.fence()
            self.S.finish()
        return nc

    def consts(self, st):
        nc, S = self.nc, self.S
        T = lambda n, s, d=F32: st.enter_context(nc.sbuf_tensor(n, s, d))
        self.ident_b = T("ident_b", [128, 128], BF16)
        self.ident_f = T("ident_f", [128, 128], F32)
        self.ones_f = T("ones_f", [128, 128], F32)
        self.tri_f = T("tri_f", [128, 128], F32)
        self.tri_b = T("tri_b", [128, 128], BF16)
        self.bd_b = T("bd_b", [128, 128], BF16)
        self.bd_f = T("bd_f", [128, 128], F32)
        self.zeros_f = T("zeros_f", [128, 2048], F32)
        g = nc.gpsimd
        for t in (self.ident_b, self.ident_f):
            S.op('pool', lambda: g.memset(t[:], 1.0), writes=[t.name])
            S.op('pool', lambda: g.affine_select(out=t[:], in_=t[:], pattern=[[-1, 128]], compare_op=ALU.is_equal,
                                                 fill=0.0, base=0, channel_multiplier=1), reads=[t.name], writes=[t.name])
        S.op('pool', lambda: g.memset(self.ones_f[:], 1.0), writes=['ones_f'])
        S.op('pool', lambda: g.memset(self.zeros_f[:], 0.0), writes=['zeros_f'])
        for t in (self.tri_f, self.tri_b, self.bd_f, self.bd_b):
            S.op('pool', lambda: g.memset(t[:], 1.0), writes=[t.name])
            S.op('pool', lambda: g.affine_select(out=t[:], in_=t[:], pattern=[[1, 128]], compare_op=ALU.is_ge,
                                                 fill=0.0, base=0, channel_multiplier=-1), reads=[t.name], writes=[t.name])
        for t in (self.bd_f, self.bd_b):
            S.op('pool', lambda: g.memset(t[0:64, 64:128], 0.0), reads=[t.name], writes=[t.name])

    def hk(self, b):
        return 'hres_%d' % b

    def ogk(self, b):
        return 'og_%d' % b

    def init_hres(self):
        S = self.S
        S.dma('sp', self.hres[0:112, :], self.zeros_f[0:112, :], reads=['zeros_f'], pwrites=[self.hk(0)])
        S.dma('sp', self.hres[112:128, :], self.meta[:, :], pwrites=[self.hk(0)])
        for b in range(1, self.NB):
            S.dma('sp' if b % 2 else 'act', self.hres[b * 128:(b + 1) * 128, :], self.x[(b - 1) * 128:b * 128, :], writes=[self.hk(b)])

    def phase_a(self, l):
        nc, S, NB, P = self.nc, self.S, self.NB, self.P
        with contextlib.ExitStack() as st:
            T = lambda n, s, d=F32: st.enter_context(nc.sbuf_tensor(self.key(n), s, d))
            hT = T("hT", [128, KC, P], BF16)
            kh = self.key('hT')
            hkeys = [kh + '_%d' % b for b in range(NB)]
            with contextlib.ExitStack() as st1:
                T1 = lambda n, s, d=F32: st1.enter_context(nc.sbuf_tensor(self.key(n), s, d))
                nw = T1("nw", [128, D])
                xb = [T1("xb%d" % i, [128, D]) for i in range(2)]
                hn = [T1("hn%d" % i, [128, D], BF16) for i in range(2)]
                junk = T1("junk", [128, D], BF16)
                st_ = T1("stat", [128, 3 * NB])
                pT = [st1.enter_context(nc.psum_tensor(self.key("pTa%d" % i), [128, D], BF16)) for i in range(2)]
                S.dma('sp', nw[:], self.norm_w[l:l + 1, :].partition_broadcast(128), writes=[nw.name])
                for b in range(NB):
                    x_ = xb[b % 2]
                    h_ = hn[b % 2]
                    p_ = pT[b % 2]
                    S.dma('sp' if b % 2 == 0 else 'act', x_[:], self.hres[b * 128:(b + 1) * 128, :], reads=[self.hk(b)], writes=[x_.name])
                    ks = [self.key('st') for _ in range(3)]
                    S.op('act', lambda: nc.scalar.activation(out=junk[:], in_=x_[:], func=AF.Square,
                                                             accum_out=st_[:, 3 * b:3 * b + 1]), reads=[x_.name], writes=[junk.name, ks[0]])
                    S.op('act', lambda: nc.scalar.activation(out=st_[:, 3 * b + 1:3 * b + 2], in_=st_[:, 3 * b:3 * b + 1], func=AF.Ln,
                                                             scale=1.0 / D, bias=EPS), reads=[ks[0]], writes=[ks[1]])
                    S.op('act', lambda: nc.scalar.activation(out=st_[:, 3 * b + 2:3 * b + 3], in_=st_[:, 3 * b + 1:3 * b + 2], func=AF.Exp,
                                                             scale=-0.5), reads=[ks[1]], writes=[ks[2]])
                    S.op('dve', lambda: nc.vector.scalar_tensor_tensor(out=h_[:], in0=x_[:], scalar=st_[:, 3 * b + 2:3 * b + 3], in1=nw[:],
                                                                       op0=ALU.mult, op1=ALU.mult), reads=[x_.name, ks[2], nw.name], writes=[h_.name])
                    for kc in range(KC):
                        S.op('pe', lambda: nc.tensor.transpose(p_[:, kc * 128:(kc + 1) * 128], h_[:, kc * 128:(kc + 1) * 128], self.ident_b[:]),
                             reads=[h_.name, 'ident_b'], writes=[p_.name])
                    pv = p_[:].rearrange("p (k t) -> p k t", t=128)
                    S.op('act', lambda: nc.scalar.copy(out=hT[:, 0:8, b * 128:(b + 1) * 128], in_=pv[:, 0:8, :]), reads=[p_.name], writes=[hkeys[b] + 'a'])
                    S.op('dve', lambda: nc.vector.tensor_copy(out=hT[:, 8:16, b * 128:(b + 1) * 128], in_=pv[:, 8:16, :]), reads=[p_.name], writes=[hkeys[b] + 'b'])
            S.fence()
            if self.debug:
                S.dma('sp', self.hT_dbg[:, :], hT[:].rearrange("p k t -> p (k t)"), reads=[k + s_ for k in hkeys for s_ in 'ab'], writes=['hT_dbg'])
            wt = [T("wt%d" % i, [128, KC, 512], BF16) for i in range(2)]
            stg = [T("stg%d" % i, [128, 512]) for i in range(4)]
            ps = [st.enter_context(nc.psum_tensor(self.key("psA%d" % i), [128, 512], F32)) for i in range(4)]
            wsrc = self.w_in
            gi = 0
            cnt = 0
            tgs = [(t0, min(512, P - t0)) for t0 in range(0, P, 512)]

            def load_w(w, segs, need_zero):
                S.op('pool', lambda: nc.gpsimd.memset(w[:, 0, 0:2] if not need_zero else w[:], 0.0), writes=[w.name])
                for (d0, s0, n) in segs:
                    S.dma('pool', w[:, :, d0:d0 + n], wsrc[l, :, s0:s0 + n].rearrange("(k p) c -> p k c", p=128), pwrites=[w.name])

            for grp in FM_GROUPS:
                w = wt[gi % 2]
                gi += 1
                segs = []
                need_zero = False
                for j, ci in enumerate(grp):
                    M, sg = FM_CHUNKS[ci]
                    tot = sum(n for _, _, n in sg)
                    if tot != M:
                        need_zero = True
                    for (d0, s0, n) in sg:
                        if segs and segs[-1][0] + segs[-1][2] == j * 128 + d0 and segs[-1][1] + segs[-1][2] == s0:
                            segs[-1] = (segs[-1][0], segs[-1][1], segs[-1][2] + n)
                        else:
                            segs.append((j * 128 + d0, s0, n))
                load_w(w, segs, need_zero)
                for (t0, tn) in tgs:
                    b0, b1 = t0 // 128, (t0 + tn) // 128
                    hk = [hkeys[b] + s for b in range(b0, b1) for s in 'ab']
                    for j, ci in enumerate(grp):
                        M = FM_CHUNKS[ci][0]
                        p_ = ps[cnt % 4]
                        s_ = stg[cnt % 4]
                        for kc in range(KC):
                            S.op('pe', lambda: nc.tensor.matmul(p_[0:M, 0:tn], lhsT=w[:, kc, j * 128:j * 128 + M], rhs=hT[:, kc, t0:t0 + tn],
                                                                start=(kc == 0), stop=(kc == KC - 1)),
                                 reads=[w.name] + hk, writes=[p_.name])
                        if cnt % 2 == 0:
                            S.op('act', lambda: nc.scalar.copy(out=s_[0:M, 0:tn], in_=p_[0:M, 0:tn]), reads=[p_.name], writes=[s_.name])
                        else:
                            S.op('dve', lambda: nc.vector.tensor_copy(out=s_[0:M, 0:tn], in_=p_[0:M, 0:tn]), reads=[p_.name], writes=[s_.name])
                        S.dma('sp', self.projT[ci * 128:ci * 128 + M, t0:t0 + tn], s_[0:M, 0:tn], reads=[s_.name], pwrites=['projT'])
                        cnt += 1
            for c0 in range(0, TMW, 512):
                cn = min(512, TMW - c0)
                w = wt[gi % 2]
                gi += 1
                segs = []
                for (d0, s0, n) in TM_SEGS:
                    lo, hi = max(c0, d0), min(c0 + cn, d0 + n)
                    if lo < hi:
                        segs.append((lo - c0, s0 + (lo - d0), hi - lo))
                load_w(w, segs, False)
                for b in range(NB):
                    p_ = ps[cnt % 4]
                    s_ = stg[cnt % 4]
                    hk = [hkeys[b] + 'a', hkeys[b] + 'b']
                    for kc in range(KC):
                        S.op('pe', lambda: nc.tensor.matmul(p_[:, 0:cn], lhsT=hT[:, kc, b * 128:(b + 1) * 128], rhs=w[:, kc, 0:cn],
                                                            start=(kc == 0), stop=(kc == KC - 1)),
                             reads=[w.name] + hk, writes=[p_.name])
                    if cnt % 2 == 0:
                        S.op('act', lambda: nc.scalar.copy(out=s_[:, 0:cn], in_=p_[:, 0:cn]), reads=[p_.name], writes=[s_.name])
                    else:
                        S.op('dve', lambda: nc.vector.tensor_copy(out=s_[:, 0:cn], in_=p_[:, 0:cn]), reads=[p_.name], writes=[s_.name])
                    S.dma('sp', self.projM[b * 128:(b + 1) * 128, c0:c0 + cn], s_[:, 0:cn], reads=[s_.name], pwrites=['projM'])
                    cnt += 1

    def phase_c(self, l):
        nc, S, NB, P = self.nc, self.S, self.NB, self.P
        last = (l == self.L - 1)
        with contextlib.ExitStack() as st:
            T = lambda n, s, d=F32: st.enter_context(nc.sbuf_tensor(self.key(n), s, d))
            wo = T("wo", [128, KC, D], BF16)
            for q in range(4):
                S.dma('pool', wo[:, 4 * q:4 * q + 4, :], self.w_out[l, 512 * q:512 * (q + 1), :].rearrange("(k p) c -> p k c", p=128),
                      writes=[wo.name + str(q)])
            wkeys = [wo.name + str(q) for q in range(4)]
            ob = [T("ob%d" % i, [128, D], BF16) for i in range(2)]
            hb = [T("hb%d" % i, [128, D]) for i in range(2)]
            oT = [T("oT%d" % i, [128, KC, 128], BF16) for i in range(2)]
            hnew = [T("hnew%d" % i, [128, D]) for i in range(2)]
            junk = T("junkc", [128, D], BF16)
            st_ = T("statc", [128, 3 * NB])
            pT = [st.enter_context(nc.psum_tensor(self.key("pTc%d" % i), [128, D], BF16)) for i in range(2)]
            ps = [st.enter_context(nc.psum_tensor(self.key("psC%d" % i), [128, 512], F32)) for i in range(4)]
            if last:
                fw = T("fw", [128, D])
                S.dma('sp', fw[:], self.final_norm_w[0:1, :].partition_broadcast(128), writes=[fw.name])
                yo = [T("yo%d" % i, [128, D]) for i in range(2)]
            for b in range(NB):
                if last and b == 0:
                    continue
                o_, h_, t_, n_, p_ = ob[b % 2], hb[b % 2], oT[b % 2], hnew[b % 2], pT[b % 2]
                S.dma('sp', o_[:], self.og[b * 128:(b + 1) * 128, :], reads=[self.ogk(b)], writes=[o_.name])
                S.dma('act', h_[:], self.hres[b * 128:(b + 1) * 128, :], reads=[self.hk(b)], writes=[h_.name])
                if b == 0:
                    S.op('pool', lambda: nc.gpsimd.memset(o_[0:112, :], 0.0), reads=[o_.name], writes=[o_.name])
                for mc in range(KC):
                    S.op('pe', lambda: nc.tensor.transpose(p_[:, mc * 128:(mc + 1) * 128], o_[:, mc * 128:(mc + 1) * 128], self.ident_b[:]),
                         reads=[o_.name, 'ident_b'], writes=[p_.name])
                pv = p_[:].rearrange("p (k t) -> p k t", t=128)
                S.op('act', lambda: nc.scalar.copy(out=t_[:, 0:8, :], in_=pv[:, 0:8, :]), reads=[p_.name], writes=[t_.name + 'a'])
                S.op('dve', lambda: nc.vector.tensor_copy(out=t_[:, 8:16, :], in_=pv[:, 8:16, :]), reads=[p_.name], writes=[t_.name + 'b'])
                for cg in range(4):
                    pp = ps[cg]
                    for mc in range(KC):
                        S.op('pe', lambda: nc.tensor.matmul(pp[:], lhsT=t_[:, mc, :], rhs=wo[:, mc, cg * 512:(cg + 1) * 512],
                                                            start=(mc == 0), stop=(mc == KC - 1)),
                             reads=[t_.name + 'a', t_.name + 'b'] + wkeys, writes=[pp.name])
                    S.op('dve', lambda: nc.vector.tensor_tensor(out=n_[:, cg * 512:(cg + 1) * 512], in0=pp[:], in1=h_[:, cg * 512:(cg + 1) * 512], op=ALU.add),
                         reads=[pp.name, h_.name], writes=[n_.name + str(cg)])
                nk = [n_.name + str(cg) for cg in range(4)]
                if not last:
                    if b == 0:
                        S.dma('sp', self.hres[112:128, :], n_[112:128, :], reads=nk, pwrites=[self.hk(0)])
                    else:
                        S.dma('sp', self.hres[b * 128:(b + 1) * 128, :], n_[:], reads=nk, writes=[self.hk(b)])
                else:
                    y_ = yo[b % 2]
                    ks = [self.key('stc') for _ in range(3)]
                    S.op('act', lambda: nc.scalar.activation(out=junk[:], in_=n_[:], func=AF.Square, accum_out=st_[:, 3 * b:3 * b + 1]),
                         reads=nk, writes=[junk.name, ks[0]])
                    S.op('act', lambda: nc.scalar.activation(out=st_[:, 3 * b + 1:3 * b + 2], in_=st_[:, 3 * b:3 * b + 1], func=AF.Ln,
                                                             scale=1.0 / D, bias=EPS), reads=[ks[0]], writes=[ks[1]])
                    S.op('act', lambda: nc.scalar.activation(out=st_[:, 3 * b + 2:3 * b + 3], in_=st_[:, 3 * b + 1:3 * b + 2], func=AF.Exp,
                                                             scale=-0.5), reads=[ks[1]], writes=[ks[2]])
                    S.op('dve', lambda: nc.vector.scalar_tensor_tensor(out=y_[:], in0=n_[:], scalar=st_[:, 3 * b + 2:3 * b + 3], in1=fw[:],
                                                                        op0=ALU.mult, op1=ALU.mult), reads=nk + [ks[2], fw.name], writes=[y_.name])
                    S.dma('sp', self.out[(b - 1) * 128:b * 128, :], y_[:], reads=[y_.name], pwrites=['out'])

    def phase_b(self, l):
        self.mix_fox(l)
        self.S.fence()
        self.mix_gla(l)
        self.S.fence()
        self.mix_dn(l)
        self.S.fence()

    def silu_into(self, z, e, src_cols, nwt=None):
        nc, S, NB = self.nc, self.S, self.NB
        S.dma('act', z[:], self.projM[:, src_cols:src_cols + 128].rearrange("(n p) c -> p n c", p=128), reads=['projM'], writes=[z.name])
        S.op('act', lambda: nc.scalar.activation(out=e[:], in_=z[:], func=AF.Exp, scale=-1.0), reads=[z.name], writes=[e.name])
        S.op('pool', lambda: nc.gpsimd.tensor_scalar_add(out=e[:], in0=e[:], scalar1=1.0), reads=[e.name], writes=[e.name])
        S.op('dve', lambda: nc.vector.reciprocal(out=e[:], in_=e[:]), reads=[e.name], writes=[e.name])
        S.op('pool', lambda: nc.gpsimd.tensor_tensor(out=z[:], in0=z[:], in1=e[:], op=ALU.mult), reads=[z.name, e.name], writes=[z.name])
        if nwt is not None:
            S.op('pool', lambda: nc.gpsimd.tensor_tensor(out=z[:], in0=z[:], in1=nwt[:].unsqueeze(1).broadcast_to([128, NB, 128]), op=ALU.mult),
                 reads=[z.name, nwt.name], writes=[z.name])

    def mix_fox(self, l):
        nc, S, NB, P = self.nc, self.S, self.NB, self.P
        with contextlib.ExitStack() as st:
            T = lambda n, s, d=F32: st.enter_context(nc.sbuf_tensor(self.key(n), s, d))
            PS = lambda n, s, d=F32: st.enter_context(nc.psum_tensor(self.key(n), s, d))
            bf = T("bf", [128, 5])
            ff = T("ff", [128, NB, 5])
            sp_ = T("sp", [128, NB, 5])
            tot = T("tot", [128, NB, 5])
            offs = T("offs", [128, NB + 1, 5])
            csc = T("csc", [128, NB, 5])
            S.dma('sp', bf[:], self.b_f[l:l + 1, :].partition_broadcast(128), writes=[bf.name])
            S.dma('sp', ff[:], self.projM[:, M_FF:M_FF + 5].rearrange("(n p) c -> p n c", p=128), reads=['projM'], writes=[ff.name])
            S.op('dve', lambda: nc.vector.tensor_tensor(out=ff[:], in0=ff[:], in1=bf[:].unsqueeze(1).broadcast_to([128, NB, 5]), op=ALU.add),
                 reads=[ff.name, bf.name], writes=[ff.name])
            S.op('act', lambda: nc.scalar.activation(out=sp_[:], in_=ff[:], func=AF.Exp, scale=-1.0), reads=[ff.name], writes=[sp_.name])
            S.op('act', lambda: nc.scalar.activation(out=sp_[:], in_=sp_[:], func=AF.Ln, bias=1.0), reads=[sp_.name], writes=[sp_.name])
            pc = PS("pc", [128, 512])
            pt_ = PS("ptot", [128, 512])
            spf = sp_[:].rearrange("p n c -> p (n c)")
            S.op('pe', lambda: nc.tensor.matmul(pc[:, 0:NB * 5], lhsT=self.tri_f[:], rhs=spf, start=True, stop=True), reads=[sp_.name, 'tri_f'], writes=[pc.name])
            S.op('pe', lambda: nc.tensor.matmul(pt_[:, 0:NB * 5], lhsT=self.ones_f[:], rhs=spf, start=True, stop=True), reads=[sp_.name, 'ones_f'], writes=[pt_.name])
            S.op('act', lambda: nc.scalar.copy(out=tot[:].rearrange("p n c -> p (n c)"), in_=pt_[:, 0:NB * 5]), reads=[pt_.name], writes=[tot.name])
            S.op('dve', lambda: nc.vector.memset(offs[:, 0, :], 0.0), writes=[offs.name])
            for j in range(NB):
                S.op('dve', lambda: nc.vector.tensor_tensor(out=offs[:, j + 1, :], in0=offs[:, j, :], in1=tot[:, j, :], op=ALU.add),
                     reads=[offs.name, tot.name], writes=[offs.name])
            S.op('dve', lambda: nc.vector.tensor_tensor(out=csc[:].rearrange("p n c -> p (n c)"), in0=pc[:, 0:NB * 5],
                                                        in1=offs[:, 0:NB, :].rearrange("p n c -> p (n c)"), op=ALU.add),
                 reads=[pc.name, offs.name], writes=[csc.name])
            if getattr(self, 'fox_stage', 0) == 1:
                return
            qf = T("qf", [128, P])
            kf = T("kf", [128, P])
            qb = T("qb", [128, P], BF16)
            kb = T("kb", [128, P], BF16)
            vf = T("vf", [128, NB, 128])
            vb = T("vb", [128, NB, 132], BF16)
            btab = T("btab", [128, NB, NB])
            ogh = T("ogh", [128, NB, 128], BF16)
            gate = T("gate", [128, NB, 128])
            gate_e = T("gate_e", [128, NB, 128])
            rec = T("rec", [128, 2 * NB])
            pts = [T("pt%d" % i, [128, 4, 128], BF16) for i in range(3)]
            sb = [PS("sb%d" % i, [128, 512]) for i in range(3)]
            acc = [PS("acc%d" % i, [128, 512]) for i in range(2)]
            S.op('pool', lambda: nc.gpsimd.memset(vb[:, :, 128:129], 1.0), writes=[vb.name + 'one'])
            S.op('pool', lambda: nc.gpsimd.memset(vb[0:112, 0, 128:129], 0.0), reads=[vb.name + 'one'], writes=[vb.name + 'one'])
            cnt = 0
            for h in range(5):
                S.dma('sp', qf[:], self.projT[(CH_FQ + h) * 128:(CH_FQ + h + 1) * 128, :], reads=['projT'], writes=[qf.name])
                S.dma('act', kf[:], self.projT[(CH_FK + h) * 128:(CH_FK + h + 1) * 128, :], reads=['projT'], writes=[kf.name])
                S.dma('sp', vf[:], self.projM[:, M_FV + h * 128:M_FV + (h + 1) * 128].rearrange("(n p) c -> p n c", p=128), reads=['projM'], writes=[vf.name])
                S.op('act', lambda: nc.scalar.mul(out=qb[:], in_=qf[:], mul=128.0 ** -0.5), reads=[qf.name], writes=[qb.name])
                S.op('dve', lambda: nc.vector.tensor_copy(out=kb[:], in_=kf[:]), reads=[kf.name], writes=[kb.name])
                S.op('pool', lambda: nc.gpsimd.tensor_copy(out=vb[:, :, 0:128], in_=vf[:]), reads=[vf.name], writes=[vb.name])
                self.silu_into(gate, gate_e, M_FZ + h * 128)
                if getattr(self, 'fox_stage', 0) == 2:
                    continue
                for i in range(NB):
                    S.op('dve', lambda: nc.vector.tensor_scalar(out=btab[:, i, 0:i + 1], in0=csc[:, 0:i + 1, h], scalar1=offs[:, i + 1, h:h + 1],
                                                                scalar2=None, op0=ALU.subtract), reads=[csc.name, offs.name], pwrites=[btab.name])
                if getattr(self, 'fox_stage', 0) == 3:
                    continue
                for i in range(getattr(self, 'fox_ni', NB)):
                    a_ = acc[i % 2]
                    for jb in range(0, i + 1, 4):
                        js = list(range(jb, min(jb + 4, i + 1)))
                        s_ = sb[cnt % 3]
                        p_ = pts[cnt % 3]
                        cnt += 1
                        for j in js:
                            S.op('pe', lambda: nc.tensor.matmul(s_[:, (j - jb) * 128:(j - jb + 1) * 128], lhsT=kb[:, j * 128:(j + 1) * 128],
                                                                rhs=qb[:, i * 128:(i + 1) * 128], start=True, stop=True),
                                 reads=[kb.name, qb.name], writes=[s_.name])
                        for j in js:
                            S.op('act', lambda: nc.scalar.activation(out=p_[:, j - jb, :], in_=s_[:, (j - jb) * 128:(j - jb + 1) * 128], func=AF.Exp,
                                                                     bias=btab[:, i, j:j + 1], scale=1.0),
                                 reads=[s_.name, btab.name], writes=[p_.name])
                        if i in js and getattr(self, 'fox_var', 0) != 2:
                            S.op('pool', lambda: nc.gpsimd.tensor_tensor(out=p_[:, i - jb, :], in0=p_[:, i - jb, :], in1=self.tri_b[:], op=ALU.mult),
                                 reads=[p_.name, 'tri_b'], writes=[p_.name])
                        for j in js:
                            NV = 128 if getattr(self, 'fox_var', 0) == 1 else 129
                            S.op('pe', lambda: nc.tensor.matmul(a_[:, 0:NV], lhsT=p_[:, j - jb, :], rhs=vb[:, j, 0:NV], start=(j == 0), stop=(j == i)),
                                 reads=[p_.name, vb.name, vb.name + 'one'], writes=[a_.name])
                    S.op('dve', lambda: nc.vector.tensor_scalar_max(out=rec[:, 2 * i:2 * i + 1], in0=a_[:, 128:129], scalar1=1e-30), reads=[a_.name], writes=[rec.name + 'a'])
                    S.op('dve', lambda: nc.vector.reciprocal(out=rec[:, 2 * i + 1:2 * i + 2], in_=rec[:, 2 * i:2 * i + 1]), reads=[rec.name + 'a'], writes=[rec.name + 'b'])
                    S.op('dve', lambda: nc.vector.scalar_tensor_tensor(out=ogh[:, i, :], in0=a_[:, 0:128], scalar=rec[:, 2 * i + 1:2 * i + 2], in1=gate[:, i, :],
                                                                       op0=ALU.mult, op1=ALU.mult), reads=[a_.name, rec.name + 'b', gate.name], pwrites=[ogh.name])
                c0 = O_FOX + h * 128
                S.dma('sp', self.og[:, c0:c0 + 128].rearrange("(n p) c -> p n c", p=128), ogh[:], reads=[ogh.name], pwrites=[self.ogk(b_) for b_ in range(NB)])

    def mix_gla(self, l):
        nc, S, NB, P, NCH = self.nc, self.S, self.NB, self.P, self.NCH
        with contextlib.ExitStack() as st:
            T = lambda n, s, d=F32: st.enter_context(nc.sbuf_tensor(self.key(n), s, d))
            PS = lambda n, s, d=F32: st.enter_context(nc.psum_tensor(self.key(n), s, d))
            gl16 = T("gl16", [16, P])
            nwt = T("gnw", [128, 128])
            qf = T("gqf", [64, P])
            kf = T("gkf", [64, P])
            wg = T("wg", [16, 64])
            nb_ = T("gnb", [64, 2])
            csA = T("csA", [64, P])
            csB_full = T("csB", [128, P])
            csB = csB_full[0:64, :]
            eg_full = T("geg", [128, P])
            eg = eg_full[0:64, :]
            qd = T("gqd", [64, P], BF16)
            ki = T("gki", [64, P], BF16)
            kdT = T("gkdT", [64, P], BF16)
            cl = T("gcl", [64, NCH])
            egl = T("gegl", [64, NCH])
            kd_tm = T("gkdtm", [128, NB, 64], BF16)
            vf = eg_full[:].rearrange("p (n c) -> p n c", c=128)
            vb = T("gvb", [128, NB, 128], BF16)
            gate = T("ggate", [128, NB, 128])
            gate_e = csB_full[:].rearrange("p (n c) -> p n c", c=128)
            at_all = T("gat", [128, NB, 128], BF16)
            Sst = T("gS", [64, 128])
            Sb = [T("gSb%d" % i, [64, 128], BF16) for i in range(2)]
            ogh = T("gogh", [128, NB, 128], BF16)
            stat = T("gstat", [128, 3 * NB])
            junk = T("gjunk", [128, 128], BF16)
            pz = PS("gpz", [128, 512])
            pat = PS("gpat", [128, 512])
            pkd = PS("gpkd", [128, 1024], BF16)
            po = [PS("gpo%d" % i, [128, 512]) for i in range(2)]
            pds = [PS("gpds%d" % i, [128, 512]) for i in range(2)]
            S.dma('sp', gl16[:], self.projT[CH_SM * 128:CH_SM * 128 + 16, :], reads=['projT'], writes=[gl16.name])
            S.dma('sp', nwt[:], self.gla_norm_w[l:l + 1, :].partition_broadcast(128), writes=[nwt.name])
            tgs = [(t0, min(512, P - t0)) for t0 in range(0, P, 512)]
            csAv = csA[:].rearrange("p (n c) -> p n c", c=64)
            csBv = csB[:].rearrange("p (n c) -> p n c", c=64)
            for h in range(5):
                r0 = (CH_GQ + h // 2) * 128 + (h % 2) * 64
                S.dma('sp', qf[:], self.projT[r0:r0 + 64, :], reads=['projT'], writes=[qf.name])
                r0 = (CH_GK + h // 2) * 128 + (h % 2) * 64
                S.dma('act', kf[:], self.projT[r0:r0 + 64, :], reads=['projT'], writes=[kf.name])
                S.dma('sp', wg[:], self.w_gk2[l, :, h * 64:(h + 1) * 64], writes=[wg.name])
                S.dma('sp', nb_[:, 0:1], self.b_gk[l:l + 1, h * 64:(h + 1) * 64].rearrange("o c -> c o"), writes=[nb_.name])
                S.op('dve', lambda: nc.vector.tensor_scalar(out=nb_[:, 1:2], in0=nb_[:, 0:1], scalar1=-1.0, scalar2=None, op0=ALU.mult),
                     reads=[nb_.name], writes=[nb_.name + 'n'])
                S.dma('sp', vf[:], self.projM[:, M_GV + h * 128:M_GV + (h + 1) * 128].rearrange("(n p) c -> p n c", p=128), reads=['projM'], writes=[vf.name])
                S.op('pool', lambda: nc.gpsimd.tensor_copy(out=vb[:], in_=vf[:]), reads=[vf.name], writes=[vb.name])
                self.silu_into(gate, gate_e, M_GZ + h * 128, nwt)
                for (t0, tn) in tgs:
                    S.op('pe', lambda: nc.tensor.matmul(pz[0:64, 0:tn], lhsT=wg[:, :], rhs=gl16[:, t0:t0 + tn], start=True, stop=True),
                         reads=[wg.name, gl16.name], writes=[pz.name])
                    S.op('act', lambda: nc.scalar.activation(out=csA[:, t0:t0 + tn], in_=pz[0:64, 0:tn], func=AF.Exp, scale=-1.0, bias=nb_[:, 1:2]),
                         reads=[pz.name, nb_.name + 'n'], pwrites=[csA.name])
                S.op('act', lambda: nc.scalar.activation(out=csA[:], in_=csA[:], func=AF.Ln, bias=1.0), reads=[csA.name], writes=[csA.name])
                src, dst, srcv, dstv = csA, csB, csAv, csBv
                for sft in (1, 2, 4, 8, 16, 32):
                    S.op('dve', lambda: nc.vector.tensor_tensor(out=dstv[:, :, sft:64], in0=srcv[:, :, sft:64], in1=srcv[:, :, 0:64 - sft], op=ALU.add),
                         reads=[src.name], writes=[dst.name])
                    S.op('pool', lambda: nc.gpsimd.tensor_copy(out=dstv[:, :, 0:sft], in_=srcv[:, :, 0:sft]), reads=[src.name], pwrites=[dst.name])
                    src, dst, srcv, dstv = dst, src, dstv, srcv
                assert src is csA
                S.op('pool', lambda: nc.gpsimd.tensor_copy(out=cl[:], in_=csAv[:, :, 63]), reads=[csA.name], writes=[cl.name])
                S.op('act', lambda: nc.scalar.activation(out=eg[:], in_=csA[:], func=AF.Exp, scale=-1.0 / 16), reads=[csA.name], writes=[eg.name])
                S.op('dve', lambda: nc.vector.scalar_tensor_tensor(out=qd[:], in0=qf[:], scalar=0.125, in1=eg[:], op0=ALU.mult, op1=ALU.mult),
                     reads=[qf.name, eg.name], writes=[qd.name])
                S.op('act', lambda: nc.scalar.activation(out=eg[:], in_=csA[:], func=AF.Exp, scale=1.0 / 16), reads=[csA.name], writes=[eg.name])
                S.op('pool', lambda: nc.gpsimd.tensor_tensor(out=ki[:], in0=kf[:], in1=eg[:], op=ALU.mult), reads=[kf.name, eg.name], writes=[ki.name])
                S.op('dve', lambda: nc.vector.tensor_tensor(out=csBv, in0=csAv, in1=cl[:].unsqueeze(2).broadcast_to([64, NCH, 64]), op=ALU.subtract),
                     reads=[csA.name, cl.name], writes=[csB.name])
                S.op('act', lambda: nc.scalar.activation(out=csB[:], in_=csB[:], func=AF.Exp, scale=1.0 / 16), reads=[csB.name], writes=[csB.name])
                S.op('dve', lambda: nc.vector.tensor_tensor(out=kdT[:], in0=kf[:], in1=csB[:], op=ALU.mult), reads=[kf.name, csB.name], writes=[kdT.name])
                S.op('act', lambda: nc.scalar.activation(out=egl[:], in_=cl[:], func=AF.Exp, scale=-1.0 / 16), reads=[cl.name], writes=[egl.name])
                for b0 in range(0, NB, 16):
                    nb2 = min(16, NB - b0)
                    for b in range(b0, b0 + nb2):
                        S.op('pe', lambda: nc.tensor.transpose(pkd[:, (b - b0) * 64:(b - b0 + 1) * 64], kdT[:, b * 128:(b + 1) * 128], self.ident_b[0:64, 0:64]),
                             reads=[kdT.name, 'ident_b'], writes=[pkd.name])
                    S.op('act', lambda: nc.scalar.copy(out=kd_tm[:, b0:b0 + nb2, :], in_=pkd[:, 0:nb2 * 64].rearrange("p (n c) -> p n c", c=64)),
                         reads=[pkd.name], pwrites=[kd_tm.name])
                for b0 in range(0, NB, 4):
                    nb2 = min(4, NB - b0)
                    for b in range(b0, b0 + nb2):
                        S.op('pe', lambda: nc.tensor.matmul(pat[:, (b - b0) * 128:(b - b0 + 1) * 128], lhsT=ki[:, b * 128:(b + 1) * 128],
                                                            rhs=qd[:, b * 128:(b + 1) * 128], start=True, stop=True),
                             reads=[ki.name, qd.name], writes=[pat.name])
                    S.op('dve', lambda: nc.vector.tensor_tensor(out=at_all[:, b0:b0 + nb2, :], in0=pat[:, 0:nb2 * 128].rearrange("p (n c) -> p n c", c=128),
                                                                in1=self.bd_b[:].unsqueeze(1).broadcast_to([128, nb2, 128]), op=ALU.mult),
                         reads=[pat.name, 'bd_b'], pwrites=[at_all.name])
                S.op('pool', lambda: nc.gpsimd.memset(Sst[:], 0.0), writes=[Sst.name])
                S.op('pool', lambda: nc.gpsimd.memset(Sb[0][:], 0.0), writes=[Sb[0].name])
                for b in range(NB):
                    po_ = po[b % 2]
                    S.op('pe', lambda: nc.tensor.matmul(po_[:, 0:128], lhsT=at_all[:, b, :], rhs=vb[:, b, :], start=True, stop=False),
                         reads=[at_all.name, vb.name], writes=[po_.name])
                    for half in range(2):
                        n = 2 * b + half
                        r_ = slice(half * 64, half * 64 + 64)
                        cur, nxt = Sb[n % 2], Sb[(n + 1) % 2]
                        pd_ = pds[n % 2]
                        S.op('pe', lambda: nc.tensor.matmul(po_[r_, 0:128], lhsT=qd[:, n * 64:(n + 1) * 64], rhs=cur[:, :], start=False, stop=(half == 1)),
                             reads=[qd.name, cur.name], writes=[po_.name])
                        S.op('pe', lambda: nc.tensor.matmul(pd_[0:64, 0:128], lhsT=kd_tm[r_, b, :], rhs=vb[r_, b, :], start=True, stop=True),
                             reads=[kd_tm.name, vb.name], writes=[pd_.name])
                        S.op('dve', lambda: nc.vector.scalar_tensor_tensor(out=Sst[:], in0=Sst[:], scalar=egl[:, n:n + 1], in1=pd_[0:64, 0:128],
                                                                           op0=ALU.mult, op1=ALU.add), reads=[Sst.name, egl.name, pd_.name], writes=[Sst.name])
                        S.op('pool', lambda: nc.gpsimd.tensor_copy(out=nxt[:], in_=Sst[:]), reads=[Sst.name], writes=[nxt.name])
                    self.head_norm(po_, stat, junk, b, gate, ogh)
                c0 = O_GLA + h * 128
                S.dma('sp', self.og[:, c0:c0 + 128].rearrange("(n p) c -> p n c", p=128), ogh[:], reads=[ogh.name], pwrites=[self.ogk(b_) for b_ in range(NB)])

    def head_norm(self, po_, stat, junk, b, gate, ogh):
        nc, S = self.nc, self.S
        ks = [self.key('hn') for _ in range(3)]
        S.op('act', lambda: nc.scalar.activation(out=junk[:], in_=po_[:, 0:128], func=AF.Square, accum_out=stat[:, 3 * b:3 * b + 1]),
             reads=[po_.name], writes=[junk.name, ks[0]])
        S.op('act', lambda: nc.scalar.activation(out=stat[:, 3 * b + 1:3 * b + 2], in_=stat[:, 3 * b:3 * b + 1], func=AF.Ln, scale=1.0 / 128, bias=EPS),
             reads=[ks[0]], writes=[ks[1]])
        S.op('act', lambda: nc.scalar.activation(out=stat[:, 3 * b + 2:3 * b + 3], in_=stat[:, 3 * b + 1:3 * b + 2], func=AF.Exp, scale=-0.5),
             reads=[ks[1]], writes=[ks[2]])
        S.op('dve', lambda: nc.vector.scalar_tensor_tensor(out=ogh[:, b, :], in0=po_[:, 0:128], scalar=stat[:, 3 * b + 2:3 * b + 3], in1=gate[:, b, :],
                                                           op0=ALU.mult, op1=ALU.mult), reads=[po_.name, ks[2], gate.name], pwrites=[ogh.name])


    def mix_dn(self, l):
        nc, S, NB, P, NCH = self.nc, self.S, self.NB, self.P, self.NCH
        BIG = 30000.0
        QS = 128.0 ** -0.5
        with contextlib.ExitStack() as st:
            T = lambda n, s, d=F32: st.enter_context(nc.sbuf_tensor(self.key(n), s, d))
            PS = lambda n, s, d=F32: st.enter_context(nc.psum_tensor(self.key(n), s, d))
            g = nc.gpsimd
            mA = T("mA", [128, 128])
            mB = T("mB", [128, 128])
            bo_f = T("bo_f", [128, 128])
            ind0 = T("ind0", [128, 128])
            ind1 = T("ind1", [128, 128])
            sel = T("sel", [64, 6, 128])
            S.op('pool', lambda: g.memset(mA[:], 0.0), writes=[mA.name])
            S.op('pool', lambda: g.affine_select(out=mA[:], in_=mA[:], pattern=[[-1, 128]], compare_op=ALU.is_gt, fill=-BIG, base=0, channel_multiplier=1),
                 reads=[mA.name], writes=[mA.name])
            S.op('pool', lambda: g.memset(mA[64:128, 0:64], -BIG), reads=[mA.name], writes=[mA.name])
            S.op('pool', lambda: g.memset(mB[:], 0.0), writes=[mB.name])
            S.op('pool', lambda: g.affine_select(out=mB[:], in_=mB[:], pattern=[[1, 128]], compare_op=ALU.is_gt, fill=BIG, base=0, channel_multiplier=-1),
                 reads=[mB.name], writes=[mB.name])
            S.op('pool', lambda: g.memset(mB[0:64, 64:128], BIG), reads=[mB.name], writes=[mB.name])
            S.op('pool', lambda: g.memset(bo_f[:], 1.0), writes=[bo_f.name])
            S.op('pool', lambda: g.memset(bo_f[0:64, 64:128], 0.0), reads=[bo_f.name], writes=[bo_f.name])
            S.op('pool', lambda: g.memset(bo_f[64:128, 0:64], 0.0), reads=[bo_f.name], writes=[bo_f.name])
            S.op('pool', lambda: g.memset(ind0[:], 0.0), writes=[ind0.name])
            S.op('pool', lambda: g.memset(ind0[0:64, :], 1.0), reads=[ind0.name], writes=[ind0.name])
            S.op('pool', lambda: g.memset(ind1[:], 0.0), writes=[ind1.name])
            S.op('pool', lambda: g.memset(ind1[64:128, :], 1.0), reads=[ind1.name], writes=[ind1.name])
            S.op('pool', lambda: g.memset(sel[:], 1.0), writes=[sel.name])
            S.op('pool', lambda: g.affine_select(out=sel[0:32], in_=sel[0:32], pattern=[[1, 6], [0, 128]], compare_op=ALU.is_equal, fill=0.0, base=0,
                                                 channel_multiplier=-1), reads=[sel.name], writes=[sel.name])
            S.op('pool', lambda: g.affine_select(out=sel[32:64], in_=sel[32:64], pattern=[[1, 6], [0, 128]], compare_op=ALU.is_equal, fill=0.0, base=0,
                                                 channel_multiplier=-1), reads=[sel.name], writes=[sel.name])
            cw = T("cw", [128, 18, 4])
            for j in range(4):
                S.dma('sp', cw[:, :, j], self.conv_w[l, j:j + 1, :].rearrange("o (n c) -> c (o n)", c=128), pwrites=[cw.name],
                      allow_slow_non_contiguous=True)
            nwt = T("dnw", [128, 128])
            S.dma('sp', nwt[:], self.dn_norm_w[l:l + 1, :].partition_broadcast(128), writes=[nwt.name])
            colp = T("colp", [64, 4])
            S.dma('sp', colp[32:38, 0:1], self.a_log[l:l + 1, :].rearrange("o c -> c o"), pwrites=[colp.name])
            S.dma('sp', colp[32:38, 1:2], self.dt_bias[l:l + 1, :].rearrange("o c -> c o"), pwrites=[colp.name])
            S.op('act', lambda: nc.scalar.activation(out=colp[32:38, 2:3], in_=colp[32:38, 0:1], func=AF.Exp), reads=[colp.name], writes=[colp.name + 'A'])
            bcp = T("bcp", [128, 3, 6])
            S.dma('sp', bcp[:, 0, :], self.a_log[l:l + 1, :].partition_broadcast(128), pwrites=[bcp.name])
            S.dma('sp', bcp[:, 1, :], self.dt_bias[l:l + 1, :].partition_broadcast(128), pwrites=[bcp.name])
            S.op('act', lambda: nc.scalar.activation(out=bcp[:, 2, :], in_=bcp[:, 0, :], func=AF.Exp), reads=[bcp.name], writes=[bcp.name + 'A'])
            if getattr(self, 'dn_stage', 0) == 11:
                return
            xt = T("dx", [128, P])
            yt = T("dy", [128, P])
            rows = T("rows", [64, P])
            rb = rows[0:6, :]
            ra = rows[32:38, :]
            rt = xt[32:38, :]
            r0 = CH_SM * 128 + 32
            kb_, ka_ = rows.name + 'b', rows.name + 'a'
            S.dma('sp', rb, self.projT[r0:r0 + 6, :], reads=['projT'], writes=[kb_])
            S.dma('sp', ra, self.projT[r0 + 6:r0 + 12, :], reads=['projT'], writes=[ka_])
            S.op('act', lambda: nc.scalar.activation(out=rb, in_=rb, func=AF.Exp, scale=-1.0), reads=[kb_], writes=[kb_])
            S.op('pool', lambda: g.tensor_scalar_add(out=rb, in0=rb, scalar1=1.0), reads=[kb_], writes=[kb_])
            S.op('dve', lambda: nc.vector.reciprocal(out=rb, in_=rb), reads=[kb_], writes=[kb_])
            S.op('act', lambda: nc.scalar.activation(out=ra, in_=ra, func=AF.Exp, bias=colp[32:38, 1:2], scale=1.0), reads=[ka_, colp.name], writes=[ka_])
            S.op('act', lambda: nc.scalar.activation(out=ra, in_=ra, func=AF.Ln, bias=1.0), reads=[ka_], writes=[ka_])
            S.op('dve', lambda: nc.vector.tensor_scalar(out=ra, in0=ra, scalar1=colp[32:38, 2:3], scalar2=None, op0=ALU.mult),
                 reads=[ka_, colp.name + 'A'], writes=[ka_])
            if getattr(self, 'dn_stage', 0) == 12:
                return
            src, dst, sk, dk_ = ra, rt, ka_, xt.name
            for sft in (1, 2, 4, 8, 16, 32):
                sv_ = src.rearrange("p (n c) -> p n c", c=64)
                dv_ = dst.rearrange("p (n c) -> p n c", c=64)
                S.op('dve', lambda: nc.vector.tensor_tensor(out=dv_[:, :, sft:64], in0=sv_[:, :, sft:64], in1=sv_[:, :, 0:64 - sft], op=ALU.add),
                     reads=[sk], writes=[dk_])
                S.op('pool', lambda: g.tensor_copy(out=dv_[:, :, 0:sft], in_=sv_[:, :, 0:sft]), reads=[sk], pwrites=[dk_])
                src, dst, sk, dk_ = dst, src, dk_, sk
            assert sk == ka_
            if getattr(self, 'dn_stage', 0) == 13:
                return
            ps = [PS("dps%d" % i, [128, 512]) for i in range(7)]
            pTb = PS("dpTb", [128, 1024], BF16)
            dm = T("dm", [128, NB, 12])
            S.dma('sp', dm[:], self.projM[:, M_DB:M_DB + 12].rearrange("(n p) c -> p n c", p=128), reads=['projM'], writes=[dm.name])
            beta_tm = T("beta_tm", [128, NB, 6])
            lpos = T("lpos", [128, NB, 6])
            cs_tm = T("cs_tm", [128, NB, 6])
            negbeg = T("negbeg", [128, NB, 6])
            ekd = T("ekd", [128, NB, 6])
            egl = [T("egl%d" % i, [128, NB, 6]) for i in range(2)]
            if getattr(self, 'dn_stage', 0) == 14:
                return
            S.op('act', lambda: nc.scalar.activation(out=beta_tm[:], in_=dm[:, :, 0:6], func=AF.Exp, scale=-1.0), reads=[dm.name], writes=[beta_tm.name])
            S.op('pool', lambda: g.tensor_scalar_add(out=beta_tm[:], in0=beta_tm[:], scalar1=1.0), reads=[beta_tm.name], writes=[beta_tm.name])
            S.op('dve', lambda: nc.vector.reciprocal(out=beta_tm[:], in_=beta_tm[:]), reads=[beta_tm.name], writes=[beta_tm.name])
            S.op('dve', lambda: nc.vector.tensor_tensor(out=lpos[:], in0=dm[:, :, 6:12], in1=bcp[:, 1, :].unsqueeze(1).broadcast_to([128, NB, 6]), op=ALU.add),
                 reads=[dm.name, bcp.name], writes=[lpos.name])
            S.op('act', lambda: nc.scalar.activation(out=lpos[:], in_=lpos[:], func=AF.Exp), reads=[lpos.name], writes=[lpos.name])
            S.op('act', lambda: nc.scalar.activation(out=lpos[:], in_=lpos[:], func=AF.Ln, bias=1.0), reads=[lpos.name], writes=[lpos.name])
            S.op('dve', lambda: nc.vector.tensor_tensor(out=lpos[:], in0=lpos[:], in1=bcp[:, 2, :].unsqueeze(1).broadcast_to([128, NB, 6]), op=ALU.mult),
                 reads=[lpos.name, bcp.name + 'A'], writes=[lpos.name])
            if getattr(self, 'dn_stage', 0) == 15:
                return
            lpf = lpos[:].rearrange("p n c -> p (n c)")
            N6 = NB * 6
            for i, lt in enumerate((self.bd_f, bo_f, ind0, ind1)):
                S.op('pe', lambda: nc.tensor.matmul(ps[i][:, 0:N6], lhsT=lt[:], rhs=lpf, start=True, stop=True), reads=[lpos.name, lt.name], writes=[ps[i].name])
            if getattr(self, 'dn_stage', 0) == 16:
                return
            fl = lambda t: t[:].rearrange("p n c -> p (n c)")
            S.op('dve', lambda: nc.vector.tensor_copy(out=fl(cs_tm), in_=ps[0][:, 0:N6]), reads=[ps[0].name], writes=[cs_tm.name])
            S.op('act', lambda: nc.scalar.activation(out=fl(negbeg), in_=fl(cs_tm), func=AF.Exp, scale=-1.0), reads=[cs_tm.name], writes=[negbeg.name])
            S.op('dve', lambda: nc.vector.scalar_tensor_tensor(out=fl(negbeg), in0=fl(negbeg), scalar=-1.0, in1=fl(beta_tm), op0=ALU.mult, op1=ALU.mult),
                 reads=[negbeg.name, beta_tm.name], writes=[negbeg.name])
            if getattr(self, 'dn_stage', 0) == 17:
                return
            S.op('dve', lambda: nc.vector.tensor_tensor(out=fl(ekd), in0=fl(cs_tm), in1=ps[1][:, 0:N6], op=ALU.subtract), reads=[cs_tm.name, ps[1].name], writes=[ekd.name])
            S.op('act', lambda: nc.scalar.activation(out=fl(ekd), in_=fl(ekd), func=AF.Exp), reads=[ekd.name], writes=[ekd.name])
            if getattr(self, 'dn_stage', 0) == 18:
                return
            for i in range(2):
                S.op('act', lambda: nc.scalar.activation(out=fl(egl[i]), in_=ps[2 + i][:, 0:N6], func=AF.Exp, scale=-1.0), reads=[ps[2 + i].name], writes=[egl[i].name])
            if getattr(self, 'dn_stage', 0) == 19:
                return
            cs_bc = T("cs_bc", [128, P])
            bb = T("bb", [128, P])
            rn = T("rn", [128, 512])
            kT = T("kT", [128, P], BF16)
            qT = T("qT", [128, P], BF16)
            qg = T("qg", [128, P], BF16)
            Tt = T("Tt", [128, NB, 128], BF16)
            qkT = T("qkT", [128, NB, 128], BF16)
            kd_tm = T("dkd", [128, NB, 128], BF16)
            bv_tm = T("dbv", [128, NB, 128], BF16)
            gate = T("dgate", [128, NB, 128], BF16)
            ogh = T("dogh", [128, NB, 128], BF16)
            Ap = [T("Ap%d" % i, [128, 128]) for i in range(2)]
            Bp = [T("Bp%d" % i, [128, 128]) for i in range(2)]
            X = T("X", [128, 128])
            G1 = T("G1", [128, 128])
            G2 = T("G2", [128, 128])
            G3 = T("G3", [128, 128])
            Sst = T("dS", [128, 128])
            Sb = [T("dSb%d" % i, [128, 128], BF16) for i in range(2)]
            Rp = [T("Rp%d" % i, [128, 128], BF16) for i in range(2)]
            vn = [T("vn%d" % i, [128, 128], BF16) for i in range(2)]
            stat = T("dstat", [128, 3 * NB])
            junk = T("djunk", [128, 128], BF16)
            tgs = [(t0, min(512, P - t0)) for t0 in range(0, P, 512)]

            def conv_silu(ci):
                S.dma('sp', xt[:], self.projT[ci * 128:(ci + 1) * 128, :], reads=['projT'], writes=[xt.name])
                S.op('dve', lambda: nc.vector.tensor_scalar(out=yt[:], in0=xt[:], scalar1=cw[:, ci, 3:4], scalar2=None, op0=ALU.mult),
                     reads=[xt.name, cw.name], writes=[yt.name])
                for sft in (1, 2, 3):
                    S.op('dve', lambda: nc.vector.scalar_tensor_tensor(out=yt[:, sft:P], in0=xt[:, 0:P - sft], scalar=cw[:, ci, 3 - sft:4 - sft], in1=yt[:, sft:P],
                                                                       op0=ALU.mult, op1=ALU.add), reads=[xt.name, cw.name, yt.name], writes=[yt.name])
                S.op('act', lambda: nc.scalar.activation(out=xt[:], in_=yt[:], func=AF.Exp, scale=-1.0), reads=[yt.name], writes=[xt.name])
                S.op('pool', lambda: g.tensor_scalar_add(out=xt[:], in0=xt[:], scalar1=1.0), reads=[xt.name], writes=[xt.name])
                S.op('dve', lambda: nc.vector.reciprocal(out=xt[:], in_=xt[:]), reads=[xt.name], writes=[xt.name])
                S.op('pool', lambda: g.tensor_tensor(out=yt[:], in0=yt[:], in1=xt[:], op=ALU.mult), reads=[yt.name, xt.name], writes=[yt.name])

            def l2norm():
                S.op('act', lambda: nc.scalar.activation(out=xt[:], in_=yt[:], func=AF.Square), reads=[yt.name], writes=[xt.name])
                for gi, (t0, tn) in enumerate(tgs):
                    p_ = ps[gi % 2]
                    S.op('pe', lambda: nc.tensor.matmul(p_[:, 0:tn], lhsT=self.ones_f[:], rhs=xt[:, t0:t0 + tn], start=True, stop=True),
                         reads=[xt.name, 'ones_f'], writes=[p_.name])
                    S.op('act', lambda: nc.scalar.activation(out=rn[:, 0:tn], in_=p_[:, 0:tn], func=AF.Ln, bias=EPS), reads=[p_.name], writes=[rn.name])
                    S.op('act', lambda: nc.scalar.activation(out=rn[:, 0:tn], in_=rn[:, 0:tn], func=AF.Exp, scale=-0.5), reads=[rn.name], writes=[rn.name])
                    S.op('dve', lambda: nc.vector.tensor_tensor(out=yt[:, t0:t0 + tn], in0=yt[:, t0:t0 + tn], in1=rn[:, 0:tn], op=ALU.mult),
                         reads=[yt.name, rn.name], writes=[yt.name])

            def bcast_row(src_rows, skey, part0, dst):
                for gi, (t0, tn) in enumerate(tgs):
                    p_ = ps[gi % 2]
                    S.op('pe', lambda: nc.tensor.matmul(p_[:, 0:tn], lhsT=sel[part0:part0 + 6, h, :], rhs=src_rows[:, t0:t0 + tn], start=True, stop=True),
                         reads=[sel.name, skey], writes=[p_.name])
                    if gi % 2 == 0:
                        S.op('act', lambda: nc.scalar.copy(out=dst[:, t0:t0 + tn], in_=p_[:, 0:tn]), reads=[p_.name], pwrites=[dst.name])
                    else:
                        S.op('dve', lambda: nc.vector.tensor_copy(out=dst[:, t0:t0 + tn], in_=p_[:, 0:tn]), reads=[p_.name], pwrites=[dst.name])

            dn_stage = getattr(self, 'dn_stage', 0)
            if dn_stage == 1:
                return
            for h in range(getattr(self, 'dn_heads', 6)):
                bcast_row(ra, ka_, 32, cs_bc)
                S.op('act', lambda: nc.scalar.activation(out=bb[:], in_=cs_bc[:], func=AF.Exp, scale=-1.0), reads=[cs_bc.name], writes=[bb.name])
                conv_silu(h)
                l2norm()
                S.op('pool', lambda: g.tensor_scalar(out=qT[:], in0=yt[:], scalar1=QS, scalar2=None, op0=ALU.mult), reads=[yt.name], writes=[qT.name])
                S.op('dve', lambda: nc.vector.scalar_tensor_tensor(out=qg[:], in0=yt[:], scalar=QS, in1=bb[:], op0=ALU.mult, op1=ALU.mult),
                     reads=[yt.name, bb.name], writes=[qg.name])
                S.op('pool', lambda: g.memset(bb[:, 0:2], 0.0), reads=[bb.name], writes=[bb.name])
                bcast_row(rb, kb_, 0, bb)
                conv_silu(6 + h)
                l2norm()
                S.op('pool', lambda: g.tensor_copy(out=kT[:], in_=yt[:]), reads=[yt.name], writes=[kT.name])
                for b0 in range(0, NB, 8):
                    nb2 = min(8, NB - b0)
                    for b in range(b0, b0 + nb2):
                        S.op('pe', lambda: nc.tensor.transpose(pTb[:, (b - b0) * 128:(b - b0 + 1) * 128], kT[:, b * 128:(b + 1) * 128], self.ident_b[:]),
                             reads=[kT.name, 'ident_b'], writes=[pTb.name])
                    S.op('dve', lambda: nc.vector.tensor_tensor(out=kd_tm[:, b0:b0 + nb2, :], in0=pTb[:, 0:nb2 * 128].rearrange("p (n c) -> p n c", c=128),
                                                                in1=ekd[:, b0:b0 + nb2, h:h + 1].broadcast_to([128, nb2, 128]), op=ALU.mult),
                         reads=[pTb.name, ekd.name], pwrites=[kd_tm.name])
                conv_silu(12 + h)
                for b0 in range(0, NB, 4):
                    nb2 = min(4, NB - b0)
                    p_ = ps[4 + (b0 // 4) % 2]
                    for b in range(b0, b0 + nb2):
                        S.op('pe', lambda: nc.tensor.transpose(p_[:, (b - b0) * 128:(b - b0 + 1) * 128], yt[:, b * 128:(b + 1) * 128], self.ident_f[:]),
                             reads=[yt.name, 'ident_f'], writes=[p_.name])
                    S.op('dve', lambda: nc.vector.tensor_tensor(out=bv_tm[:, b0:b0 + nb2, :], in0=p_[:, 0:nb2 * 128].rearrange("p (n c) -> p n c", c=128),
                                                                in1=beta_tm[:, b0:b0 + nb2, h:h + 1].broadcast_to([128, nb2, 128]), op=ALU.mult),
                         reads=[p_.name, beta_tm.name], pwrites=[bv_tm.name])
                xv = xt[:].rearrange("p (n c) -> p n c", c=128)
                yv = yt[:].rearrange("p (n c) -> p n c", c=128)
                c0 = M_DZ + h * 128
                S.dma('act', xv, self.projM[:, c0:c0 + 128].rearrange("(n p) c -> p n c", p=128), reads=['projM'], writes=[xt.name])
                S.op('act', lambda: nc.scalar.activation(out=yt[:], in_=xt[:], func=AF.Exp, scale=-1.0), reads=[xt.name], writes=[yt.name])
                S.op('pool', lambda: g.tensor_scalar_add(out=yt[:], in0=yt[:], scalar1=1.0), reads=[yt.name], writes=[yt.name])
                S.op('dve', lambda: nc.vector.reciprocal(out=yt[:], in_=yt[:]), reads=[yt.name], writes=[yt.name])
                S.op('pool', lambda: g.tensor_tensor(out=yt[:], in0=xt[:], in1=yt[:], op=ALU.mult), reads=[xt.name, yt.name], writes=[yt.name])
                S.op('pool', lambda: g.tensor_tensor(out=gate[:], in0=yv, in1=nwt[:].unsqueeze(1).broadcast_to([128, NB, 128]), op=ALU.mult),
                     reads=[yt.name, nwt.name], writes=[gate.name])
                if dn_stage == 2:
                    continue
                for b in range(NB):
                    cols = slice(b * 128, (b + 1) * 128)
                    csc = cs_tm[:, b, h:h + 1]
                    S.op('dve', lambda: nc.vector.scalar_tensor_tensor(out=G1[:], in0=cs_bc[:, cols], scalar=csc, in1=mA[:], op0=ALU.subtract, op1=ALU.add),
                         reads=[cs_bc.name, cs_tm.name, mA.name], writes=[G1.name])
                    S.op('act', lambda: nc.scalar.activation(out=G1[:], in_=G1[:], func=AF.Exp), reads=[G1.name], writes=[G1.name])
                    S.op('dve', lambda: nc.vector.scalar_tensor_tensor(out=G2[:], in0=cs_bc[:, cols], scalar=csc, in1=mB[:], op0=ALU.subtract, op1=ALU.add),
                         reads=[cs_bc.name, cs_tm.name, mB.name], writes=[G2.name])
                    S.op('act', lambda: nc.scalar.activation(out=G2[:], in_=G2[:], func=AF.Exp, scale=-1.0), reads=[G2.name], writes=[G2.name])
                    S.op('pe', lambda: nc.tensor.matmul(ps[0][:, 0:128], lhsT=kT[:, cols], rhs=kT[:, cols], start=True, stop=True), reads=[kT.name], writes=[ps[0].name])
                    S.op('pe', lambda: nc.tensor.matmul(ps[2][:, 0:128], lhsT=kT[:, cols], rhs=qT[:, cols], start=True, stop=True), reads=[qT.name, kT.name], writes=[ps[2].name])
                    a_, b_ = Ap[0], Bp[0]
                    S.op('dve', lambda: nc.vector.scalar_tensor_tensor(out=a_[:], in0=ps[0][:, 0:128], scalar=beta_tm[:, b, h:h + 1], in1=G1[:], op0=ALU.mult, op1=ALU.mult),
                         reads=[ps[0].name, G1.name, beta_tm.name], writes=[a_.name])
                    S.op('pool', lambda: g.tensor_tensor(out=G3[:], in0=G2[:], in1=bb[:, cols], op=ALU.mult), reads=[G2.name, bb.name], writes=[G3.name])
                    S.op('dve', lambda: nc.vector.tensor_tensor(out=b_[:], in0=ps[0][:, 0:128], in1=G3[:], op=ALU.mult), reads=[ps[0].name, G3.name], writes=[b_.name])
                    S.op('pool', lambda: g.tensor_tensor(out=G2[:], in0=G2[:], in1=self.ident_f[:], op=ALU.add), reads=[G2.name, 'ident_f'], writes=[G2.name])
                    S.op('dve', lambda: nc.vector.tensor_tensor(out=qkT[:, b, :], in0=ps[2][:, 0:128], in1=G2[:], op=ALU.mult), reads=[ps[2].name, G2.name], pwrites=[qkT.name])
                    S.op('pool', lambda: g.tensor_tensor(out=X[:], in0=self.ident_f[:], in1=b_[:], op=ALU.subtract), reads=[b_.name, 'ident_f'], writes=[X.name])
                    for lv in range(5):
                        a2, b2 = Ap[(lv + 1) % 2], Bp[(lv + 1) % 2]
                        S.op('pe', lambda: nc.tensor.matmul(ps[3][:, 0:128], lhsT=b_[:], rhs=a_[:], start=True, stop=True), reads=[a_.name, b_.name], writes=[ps[3].name])
                        if lv < 4:
                            S.op('pe', lambda: nc.tensor.matmul(ps[4][:, 0:128], lhsT=a_[:], rhs=b_[:], start=True, stop=True), reads=[a_.name, b_.name], writes=[ps[4].name])
                        S.op('act', lambda: nc.scalar.copy(out=a2[:], in_=ps[3][:, 0:128]), reads=[ps[3].name], writes=[a2.name])
                        if lv < 4:
                            S.op('dve', lambda: nc.vector.tensor_copy(out=b2[:], in_=ps[4][:, 0:128]), reads=[ps[4].name], writes=[b2.name])
                        S.op('pe', lambda: nc.tensor.matmul(ps[5][:, 0:128], lhsT=a2[:], rhs=X[:], start=True, stop=True), reads=[a2.name, X.name], writes=[ps[5].name])
                        S.op('dve', lambda: nc.vector.tensor_tensor(out=X[:], in0=X[:], in1=ps[5][:, 0:128], op=ALU.add), reads=[X.name, ps[5].name], writes=[X.name])
                        a_, b_ = a2, b2
                    S.op('pool', lambda: g.tensor_copy(out=Tt[:, b, :], in_=X[:]), reads=[X.name], pwrites=[Tt.name])
                if dn_stage == 3:
                    continue
                S.op('pool', lambda: g.memset(Sst[:], 0.0), writes=[Sst.name])
                S.op('pool', lambda: g.memset(Sb[0][:], 0.0), writes=[Sb[0].name])
                for b in range(NB):
                    po_ = ps[3 + b % 2]
                    for half in range(2):
                        n = 2 * b + half
                        r_ = slice(half * 64, half * 64 + 64)
                        cur, nxt = Sb[n % 2], Sb[(n + 1) % 2]
                        R_, v_ = Rp[n % 2], vn[n % 2]
                        pk, pv, pd = ps[0], ps[1], ps[2]
                        S.op('pe', lambda: nc.tensor.matmul(pk[r_, 0:128], lhsT=kT[:, n * 64:(n + 1) * 64], rhs=cur[:], start=True, stop=True),
                             reads=[kT.name, cur.name], writes=[pk.name])
                        S.op('dve', lambda: nc.vector.scalar_tensor_tensor(out=R_[r_, :], in0=pk[r_, 0:128], scalar=negbeg[r_, b, h:h + 1], in1=bv_tm[r_, b, :],
                                                                           op0=ALU.mult, op1=ALU.add), reads=[pk.name, negbeg.name, bv_tm.name], writes=[R_.name])
                        S.op('pe', lambda: nc.tensor.matmul(pv[r_, 0:128], lhsT=Tt[r_, b, half * 64:half * 64 + 64], rhs=R_[r_, :], start=True, stop=True),
                             reads=[Tt.name, R_.name], writes=[pv.name])
                        S.op('act', lambda: nc.scalar.copy(out=v_[r_, :], in_=pv[r_, 0:128]), reads=[pv.name], writes=[v_.name])
                        S.op('pe', lambda: nc.tensor.matmul(po_[r_, 0:128], lhsT=qg[:, n * 64:(n + 1) * 64], rhs=cur[:], start=True, stop=False),
                             reads=[qg.name, cur.name], writes=[po_.name])
                        S.op('pe', lambda: nc.tensor.matmul(po_[r_, 0:128], lhsT=qkT[r_, b, half * 64:half * 64 + 64], rhs=v_[r_, :], start=False, stop=True),
                             reads=[qkT.name, v_.name], writes=[po_.name])
                        S.op('pe', lambda: nc.tensor.matmul(pd[:, 0:128], lhsT=kd_tm[r_, b, :], rhs=v_[r_, :], start=True, stop=True),
                             reads=[kd_tm.name, v_.name], writes=[pd.name])
                        S.op('dve', lambda: nc.vector.scalar_tensor_tensor(out=Sst[:], in0=Sst[:], scalar=egl[half][:, b, h:h + 1], in1=pd[:, 0:128],
                                                                           op0=ALU.mult, op1=ALU.add), reads=[Sst.name, egl[half].name, pd.name], writes=[Sst.name])
                        S.op('pool', lambda: g.tensor_copy(out=nxt[:], in_=Sst[:]), reads=[Sst.name], writes=[nxt.name])
                    self.head_norm(po_, stat, junk, b, gate, ogh)
                c0 = O_DN + h * 128
                S.dma('sp', self.og[:, c0:c0 + 128].rearrange("(n p) c -> p n c", p=128), ogh[:], reads=[ogh.name], pwrites=[self.ogk(b_) for b_ in range(NB)])


def kernel(x, meta_tokens, norm_w, w_in, conv_w, a_log, dt_bias, dn_norm_w, w_gk2, b_gk,
           gla_norm_w, b_f, w_out, final_norm_w):
    f = lambda a: np.ascontiguousarray(np.asarray(a, dtype=np.float32))
    x = f(x)
    B, S_LEN, _ = x.shape
    L = int(np.asarray(w_in).shape[0])
    prog = Prog(S_LEN, L)
    nc = prog.build()
    common = dict(meta_tokens=f(meta_tokens), norm_w=f(norm_w), w_in=f(w_in), conv_w=f(conv_w), a_log=f(a_log),
                  dt_bias=f(dt_bias), dn_norm_w=f(dn_norm_w), w_gk2=f(w_gk2), b_gk=f(b_gk), gla_norm_w=f(gla_norm_w),
                  b_f=f(b_f), w_out=f(w_out), final_norm_w=f(final_norm_w).reshape(1, -1))
    in_maps = [dict(common, x=np.ascontiguousarray(x[b])) for b in range(B)]
    res = run_bass_kernel_spmd(nc, in_maps, core_ids=list(range(B)))
    return np.stack([np.asarray(r["out"], dtype=np.float32) for r in res.results], 0)
```

```python
import contextlib
import numpy as np
import concourse.bass as bass
import concourse.mybir as mybir
from concourse.bass_utils import run_bass_kernel_spmd

F32 = mybir.dt.float32
BF16 = mybir.dt.bfloat16
AF = mybir.ActivationFunctionType
ALU = mybir.AluOpType

D = 2048
KC = 16
IN_COLS = 7585
EPS = 1e-6
C_DQ, C_DK, C_DV, C_DZ, C_DB, C_DA = 0, 768, 1536, 2304, 3072, 3078
C_GQ, C_GK, C_GV, C_GZ, C_GL = 3084, 3404, 3724, 4364, 5004
C_FQ, C_FK, C_FV, C_FZ, C_FF = 5020, 5660, 6300, 6940, 7580

FM_CHUNKS = []
for i in range(18):
    FM_CHUNKS.append((128, [(0, 128 * i, 128)]))
FM_CHUNKS.append((128, [(0, C_GQ, 128)]))
FM_CHUNKS.append((128, [(0, C_GQ + 128, 128)]))
FM_CHUNKS.append((64, [(0, C_GQ + 256, 64)]))
FM_CHUNKS.append((128, [(0, C_GK, 128)]))
FM_CHUNKS.append((128, [(0, C_GK + 128, 128)]))
FM_CHUNKS.append((64, [(0, C_GK + 256, 64)]))
FM_CHUNKS.append((64, [(0, C_GL, 16), (32, C_DB, 12)]))
for i in range(5):
    FM_CHUNKS.append((128, [(0, C_FQ + 128 * i, 128)]))
for i in range(5):
    FM_CHUNKS.append((128, [(0, C_FK + 128 * i, 128)]))
NFM = len(FM_CHUNKS)
FM_GROUPS = [[0, 1, 2, 3], [4, 5, 6, 7], [8, 9, 10, 11], [12, 13, 14, 15], [16, 17],
             [18, 19, 20, 21], [22, 23, 24], [25, 26, 27, 28], [29, 30, 31, 32], [33, 34]]
CH_GQ, CH_GK, CH_SM, CH_FQ, CH_FK = 18, 21, 24, 25, 30

TM_SEGS = [(0, C_DZ, 768), (768, C_GV, 1280), (2048, C_FV, 1280), (3328, C_DB, 12), (3340, C_FF, 5)]
TMW = 3345
M_DZ, M_GV, M_GZ, M_FV, M_FZ, M_DB, M_DA, M_FF = 0, 768, 1408, 2048, 2688, 3328, 3334, 3340
O_DN, O_GLA, O_FOX = 0, 768, 1408


class Sched:
    COMPUTE = ('pe', 'act', 'dve', 'pool')

    def __init__(self, nc, stack, n_dma_sems=20):
        self.nc = nc
        self.E = {'pe': nc.tensor, 'act': nc.scalar, 'dve': nc.vector, 'pool': nc.gpsimd, 'sp': nc.sync}
        self.psem = {e: stack.enter_context(nc.semaphore('prog_' + e)) for e in self.COMPUTE}
        self.cnt = {e: 0 for e in self.COMPUTE}
        self.queues = ('sp', 'act', 'pool')
        self.dsem = {q: [stack.enter_context(nc.semaphore('d%s%d' % (q, i))) for i in range(n_dma_sems)]
                     for q in self.queues}
        self.dval = {q: [0] * n_dma_sems for q in self.queues}
        self.dnext = {q: 0 for q in self.queues}
        self.waited = {}
        self.xw = {}
        self.pw = {}
        self.rd = {}
        self.n_wait = 0
        self.n_ins = 0

    def _wait(self, eng, src, val, same_ok):
        if src[0] == 'c':
            e = src[1]
            if e == eng and (same_ok or e == 'pe'):
                return
            sem = self.psem[e]
        else:
            sem = self.dsem[src[1]][src[2]]
        key = (eng, src)
        if self.waited.get(key, 0) >= val:
            return
        self.waited[key] = val
        self.E[eng].wait_ge(sem, val)
        self.n_wait += 1

    def _deps(self, eng, reads, writes, pwrites):
        for k in reads:
            for m in (self.xw.get(k), self.pw.get(k)):
                if m:
                    for src, v in m.items():
                        self._wait(eng, src, v, False)
        for k in writes:
            for m in (self.xw.get(k), self.pw.get(k)):
                if m:
                    for src, v in m.items():
                        self._wait(eng, src, v, False)
            m = self.rd.get(k)
            if m:
                for src, v in m.items():
                    self._wait(eng, src, v, True)
        for k in pwrites:
            m = self.xw.get(k)
            if m:
                for src, v in m.items():
                    self._wait(eng, src, v, False)
            m = self.rd.get(k)
            if m:
                for src, v in m.items():
                    self._wait(eng, src, v, True)

    def _record(self, src, val, reads, writes, pwrites):
        for k in reads:
            if k not in writes:
                self.rd.setdefault(k, {})[src] = val
        for k in writes:
            self.xw[k] = {src: val}
            self.pw[k] = {}
            self.rd[k] = {}
        for k in pwrites:
            self.pw.setdefault(k, {})[src] = val

    def op(self, eng, fn, reads=(), writes=(), pwrites=()):
        self._deps(eng, reads, writes, pwrites)
        ins = fn()
        self.cnt[eng] += 1
        ins.then_inc(self.psem[eng], 1)
        self._record(('c', eng), self.cnt[eng], reads, writes, pwrites)
        self.n_ins += 1
        return ins

    def dma(self, q, out, in_, reads=(), writes=(), pwrites=(), **kw):
        eng = q
        self._deps(eng, reads, writes, pwrites)
        i = self.dnext[q]
        self.dnext[q] = (i + 1) % len(self.dsem[q])
        if self.dval[q][i] > 0:
            self._wait(eng, ('d', q, i), self.dval[q][i], False)
        ins = self.E[eng].dma_start(out=out, in_=in_, **kw)
        self.dval[q][i] += 16
        ins.then_inc(self.dsem[q][i], 16)
        self._record(('d', q, i), self.dval[q][i], reads, writes, pwrites)
        self.n_ins += 1
        return ins

    def fence(self):
        for eng in ('pe', 'act', 'dve', 'pool', 'sp'):
            self.finish(eng)

    def finish(self, eng='sp'):
        for e in self.COMPUTE:
            if self.cnt[e] > 0:
                self._wait(eng, ('c', e), self.cnt[e], False)
        for q in self.queues:
            for i in range(len(self.dsem[q])):
                if self.dval[q][i] > 0:
                    self._wait(eng, ('d', q, i), self.dval[q][i], False)


class Prog:
    def __init__(self, S_LEN, L, debug=False, stop_after=None):
        self.S_LEN, self.L, self.debug = S_LEN, L, debug
        self.NB = S_LEN // 128 + 1
        self.P = self.NB * 128
        self.NCH = self.NB * 2
        self.stop_after = stop_after
        nc = self.nc = bass.Bass("TRN2", target_bir_lowering=False)
        P = self.P
        I = lambda n, s: nc.dram_tensor(n, s, F32, kind="ExternalInput").ap()
        self.x = I("x", [S_LEN, D])
        self.meta = I("meta_tokens", [16, D])
        self.norm_w = I("norm_w", [L, D])
        self.w_in = I("w_in", [L, D, IN_COLS])
        self.conv_w = I("conv_w", [L, 4, 2304])
        self.a_log = I("a_log", [L, 6])
        self.dt_bias = I("dt_bias", [L, 6])
        self.dn_norm_w = I("dn_norm_w", [L, 128])
        self.w_gk2 = I("w_gk2", [L, 16, 320])
        self.b_gk = I("b_gk", [L, 320])
        self.gla_norm_w = I("gla_norm_w", [L, 128])
        self.b_f = I("b_f", [L, 5])
        self.w_out = I("w_out", [L, D, D])
        self.final_norm_w = I("final_norm_w", [1, D])
        self.out = nc.dram_tensor("out", [S_LEN, D], F32, kind="ExternalOutput").ap()
        kind = "ExternalOutput" if debug else "Internal"
        self.hres = nc.dram_tensor("hres", [P, D], F32, kind=kind).ap()
        self.projT = nc.dram_tensor("projT", [NFM * 128, P], F32, kind=kind).ap()
        self.projM = nc.dram_tensor("projM", [P, TMW], F32, kind=kind).ap()
        self.og = nc.dram_tensor("og", [P, D], BF16, kind=kind).ap()
        if debug:
            self.hT_dbg = nc.dram_tensor("hT_dbg", [128, KC * P], BF16, kind="ExternalOutput").ap()
        self.uid = 0

    def key(self, s):
        self.uid += 1
        return '%s#%d' % (s, self.uid)

    def build(self):
        nc = self.nc
        with contextlib.ExitStack() as st:
            self.S = Sched(nc, st)
            self.consts(st)
            self.init_hres()
            for l in range(self.L):
                self.phase_a(l)
                self.S.fence()
                if self.stop_after == ('a', l):
                    break
                self.phase_b(l)
                if self.stop_after == ('b', l):
                    break
                self.phase_c(l)
                self.S.fence()
            self.S.finish()
        return nc

    def consts(self, st):
        nc, S = self.nc, self.S
        T = lambda n, s, d=F32: st.enter_context(nc.sbuf_tensor(n, s, d))
        self.ident_b = T("ident_b", [128, 128], BF16)
        self.ident_f = T("ident_f", [128, 128], F32)
        self.ones_f = T("ones_f", [128, 128], F32)
        self.tri_f = T("tri_f", [128, 128], F32)
        self.tri_b = T("tri_b", [128, 128], BF16)
        self.bd_b = T("bd_b", [128, 128], BF16)
        self.bd_f = T("bd_f", [128, 128], F32)
        self.zeros_f = T("zeros_f", [128, 2048], F32)
        g = nc.gpsimd
        for t in (self.ident_b, self.ident_f):
            S.op('pool', lambda: g.memset(t[:], 1.0), writes=[t.name])
            S.op('pool', lambda: g.affine_select(out=t[:], in_=t[:], pattern=[[-1, 128]], compare_op=ALU.is_equal,
                                                 fill=0.0, base=0, channel_multiplier=1), reads=[t.name], writes=[t.name])
        S.op('pool', lambda: g.memset(self.ones_f[:], 1.0), writes=['ones_f'])
        S.op('pool', lambda: g.memset(self.zeros_f[:], 0.0), writes=['zeros_f'])
        for t in (self.tri_f, self.tri_b, self.bd_f, self.bd_b):
            S.op('pool', lambda: g.memset(t[:], 1.0), writes=[t.name])
            S.op('pool', lambda: g.affine_select(out=t[:], in_=t[:], pattern=[[1, 128]], compare_op=ALU.is_ge,
                                                 fill=0.0, base=0, channel_multiplier=-1), reads=[t.name], writes=[t.name])
        for t in (self.bd_f, self.bd_b):
            S.op('pool', lambda: g.memset(t[0:64, 64:128], 0.0), reads=[t.name], writes=[t.name])

    def hk(self, b):
        return 'hres_%d' % b

    def ogk(self, b):
        return 'og_%d' % b

    def init_hres(self):
        S = self.S
        S.dma('sp', self.hres[0:112, :], self.zeros_f[0:112, :], reads=['zeros_f'], pwrites=[self.hk(0)])
        S.dma('sp', self.hres[112:128, :], self.meta[:, :], pwrites=[self.hk(0)])
        for b in range(1, self.NB):
            S.dma('sp' if b % 2 else 'act', self.hres[b * 128:(b + 1) * 128, :], self.x[(b - 1) * 128:b * 128, :], writes=[self.hk(b)])

    def phase_a(self, l):
        nc, S, NB, P = self.nc, self.S, self.NB, self.P
        with contextlib.ExitStack() as st:
            T = lambda n, s, d=F32: st.enter_context(nc.sbuf_tensor(self.key(n), s, d))
            hT = T("hT", [128, KC, P], BF16)
            kh = self.key('hT')
            hkeys = [kh + '_%d' % b for b in range(NB)]
            with contextlib.ExitStack() as st1:
                T1 = lambda n, s, d=F32: st1.enter_context(nc.sbuf_tensor(self.key(n), s, d))
                nw = T1("nw", [128, D])
                xb = [T1("xb%d" % i, [128, D]) for i in range(2)]
                hn = [T1("hn%d" % i, [128, D], BF16) for i in range(2)]
                junk = T1("junk", [128, D], BF16)
                st_ = T1("stat", [128, 3 * NB])
                pT = [st1.enter_context(nc.psum_tensor(self.key("pTa%d" % i), [128, D], BF16)) for i in range(2)]
                S.dma('sp', nw[:], self.norm_w[l:l + 1, :].partition_broadcast(128), writes=[nw.name])
                for b in range(NB):
                    x_ = xb[b % 2]
                    h_ = hn[b % 2]
                    p_ = pT[b % 2]
                    S.dma('sp' if b % 2 == 0 else 'act', x_[:], self.hres[b * 128:(b + 1) * 128, :], reads=[self.hk(b)], writes=[x_.name])
                    ks = [self.key('st') for _ in range(3)]
                    S.op('act', lambda: nc.scalar.activation(out=junk[:], in_=x_[:], func=AF.Square,
                                                             accum_out=st_[:, 3 * b:3 * b + 1]), reads=[x_.name], writes=[junk.name, ks[0]])
                    S.op('act', lambda: nc.scalar.activation(out=st_[:, 3 * b + 1:3 * b + 2], in_=st_[:, 3 * b:3 * b + 1], func=AF.Ln,
                                                             scale=1.0 / D, bias=EPS), reads=[ks[0]], writes=[ks[1]])
                    S.op('act', lambda: nc.scalar.activation(out=st_[:, 3 * b + 2:3 * b + 3], in_=st_[:, 3 * b + 1:3 * b + 2], func=AF.Exp,
                                                             scale=-0.5), reads=[ks[1]], writes=[ks[2]])
                    S.op('dve', lambda: nc.vector.scalar_tensor_tensor(out=h_[:], in0=x_[:], scalar=st_[:, 3 * b + 2:3 * b + 3], in1=nw[:],
                                                                       op0=ALU.mult, op1=ALU.mult), reads=[x_.name, ks[2], nw.name], writes=[h_.name])
                    for kc in range(KC):
                        S.op('pe', lambda: nc.tensor.transpose(p_[:, kc * 128:(kc + 1) * 128], h_[:, kc * 128:(kc + 1) * 128], self.ident_b[:]),
                             reads=[h_.name, 'ident_b'], writes=[p_.name])
                    pv = p_[:].rearrange("p (k t) -> p k t", t=128)
                    S.op('act', lambda: nc.scalar.copy(out=hT[:, 0:8, b * 128:(b + 1) * 128], in_=pv[:, 0:8, :]), reads=[p_.name], writes=[hkeys[b] + 'a'])
                    S.op('dve', lambda: nc.vector.tensor_copy(out=hT[:, 8:16, b * 128:(b + 1) * 128], in_=pv[:, 8:16, :]), reads=[p_.name], writes=[hkeys[b] + 'b'])
            S.fence()
            if self.debug:
                S.dma('sp', self.hT_dbg[:, :], hT[:].rearrange("p k t -> p (k t)"), reads=[k + s_ for k in hkeys for s_ in 'ab'], writes=['hT_dbg'])
            wt = [T("wt%d" % i, [128, KC, 512], BF16) for i in range(2)]
            stg = [T("stg%d" % i, [128, 512]) for i in range(4)]
            ps = [st.enter_context(nc.psum_tensor(self.key("psA%d" % i), [128, 512], F32)) for i in range(4)]
            wsrc = self.w_in
            gi = 0
            cnt = 0
            tgs = [(t0, min(512, P - t0)) for t0 in range(0, P, 512)]

            def load_w(w, segs, need_zero):
                S.op('pool', lambda: nc.gpsimd.memset(w[:, 0, 0:2] if not need_zero else w[:], 0.0), writes=[w.name])
                for (d0, s0, n) in segs:
                    S.dma('pool', w[:, :, d0:d0 + n], wsrc[l, :, s0:s0 + n].rearrange("(k p) c -> p k c", p=128), pwrites=[w.name])

            for grp in FM_GROUPS:
                w = wt[gi % 2]
                gi += 1
                segs = []
                need_zero = False
                for j, ci in enumerate(grp):
                    M, sg = FM_CHUNKS[ci]
                    tot = sum(n for _, _, n in sg)
                    if tot != M:
                        need_zero = True
                    for (d0, s0, n) in sg:
                        if segs and segs[-1][0] + segs[-1][2] == j * 128 + d0 and segs[-1][1] + segs[-1][2] == s0:
                            segs[-1] = (segs[-1][0], segs[-1][1], segs[-1][2] + n)
                        else:
                            segs.append((j * 128 + d0, s0, n))
                load_w(w, segs, need_zero)
                for (t0, tn) in tgs:
                    b0, b1 = t0 // 128, (t0 + tn) // 128
                    hk = [hkeys[b] + s for b in range(b0, b1) for s in 'ab']
                    for j, ci in enumerate(grp):
                        M = FM_CHUNKS[ci][0]
                        p_ = ps[cnt % 4]
                        s_ = stg[cnt % 4]
                        for kc in range(KC):
                            S.op('pe', lambda: nc.tensor.matmul(p_[0:M, 0:tn], lhsT=w[:, kc, j * 128:j * 128 + M], rhs=hT[:, kc, t0:t0 + tn],
                                                                start=(kc == 0), stop=(kc == KC - 1)),
                                 reads=[w.name] + hk, writes=[p_.name])
                        if cnt % 2 == 0:
                            S.op('act', lambda: nc.scalar.copy(out=s_[0:M, 0:tn], in_=p_[0:M, 0:tn]), reads=[p_.name], writes=[s_.name])
                        else:
                            S.op('dve', lambda: nc.vector.tensor_copy(out=s_[0:M, 0:tn], in_=p_[0:M, 0:tn]), reads=[p_.name], writes=[s_.name])
                        S.dma('sp', self.projT[ci * 128:ci * 128 + M, t0:t0 + tn], s_[0:M, 0:tn], reads=[s_.name], pwrites=['projT'])
                        cnt += 1
            for c0 in range(0, TMW, 512):
                cn = min(512, TMW - c0)
                w = wt[gi % 2]
                gi += 1
                segs = []
                for (d0, s0, n) in TM_SEGS:
                    lo, hi = max(c0, d0), min(c0 + cn, d0 + n)
                    if lo < hi:
                        segs.append((lo - c0, s0 + (lo - d0), hi - lo))
                load_w(w, segs, False)
                for b in range(NB):
                    p_ = ps[cnt % 4]
                    s_ = stg[cnt % 4]
                    hk = [hkeys[b] + 'a', hkeys[b] + 'b']
                    for kc in range(KC):
                        S.op('pe', lambda: nc.tensor.matmul(p_[:, 0:cn], lhsT=hT[:, kc, b * 128:(b + 1) * 128], rhs=w[:, kc, 0:cn],
                                                            start=(kc == 0), stop=(kc == KC - 1)),
                             reads=[w.name] + hk, writes=[p_.name])
                    if cnt % 2 == 0:
                        S.op('act', lambda: nc.scalar.copy(out=s_[:, 0:cn], in_=p_[:, 0:cn]), reads=[p_.name], writes=[s_.name])
                    else:
                        S.op('dve', lambda: nc.vector.tensor_copy(out=s_[:, 0:cn], in_=p_[:, 0:cn]), reads=[p_.name], writes=[s_.name])
                    S.dma('sp', self.projM[b * 128:(b + 1) * 128, c0:c0 + cn], s_[:, 0:cn], reads=[s_.name], pwrites=['projM'])
                    cnt += 1

    def phase_c(self, l):
        nc, S, NB, P = self.nc, self.S, self.NB, self.P
        last = (l == self.L - 1)
        with contextlib.ExitStack() as st:
            T = lambda n, s, d=F32: st.enter_context(nc.sbuf_tensor(self.key(n), s, d))
            wo = T("wo", [128, KC, D], BF16)
            for q in range(4):
                S.dma('pool', wo[:, 4 * q:4 * q + 4, :], self.w_out[l, 512 * q:512 * (q + 1), :].rearrange("(k p) c -> p k c", p=128),
                      writes=[wo.name + str(q)])
            wkeys = [wo.name + str(q) for q in range(4)]
            ob = [T("ob%d" % i, [128, D], BF16) for i in range(2)]
            hb = [T("hb%d" % i, [128, D]) for i in range(2)]
            oT = [T("oT%d" % i, [128, KC, 128], BF16) for i in range(2)]
            hnew = [T("hnew%d" % i, [128, D]) for i in range(2)]
            junk = T("junkc", [128, D], BF16)
            st_ = T("statc", [128, 3 * NB])
            pT = [st.enter_context(nc.psum_tensor(self.key("pTc%d" % i), [128, D], BF16)) for i in range(2)]
            ps = [st.enter_context(nc.psum_tensor(self.key("psC%d" % i), [128, 512], F32)) for i in range(4)]
            if last:
                fw = T("fw", [128, D])
                S.dma('sp', fw[:], self.final_norm_w[0:1, :].partition_broadcast(128), writes=[fw.name])
                yo = [T("yo%d" % i, [128, D]) for i in range(2)]
            for b in range(NB):
                if last and b == 0:
                    continue
                o_, h_, t_, n_, p_ = ob[b % 2], hb[b % 2], oT[b % 2], hnew[b % 2], pT[b % 2]
                S.dma('sp', o_[:], self.og[b * 128:(b + 1) * 128, :], reads=[self.ogk(b)], writes=[o_.name])
                S.dma('act', h_[:], self.hres[b * 128:(b + 1) * 128, :], reads=[self.hk(b)], writes=[h_.name])
                if b == 0:
                    S.op('pool', lambda: nc.gpsimd.memset(o_[0:112, :], 0.0), reads=[o_.name], writes=[o_.name])
                for mc in range(KC):
                    S.op('pe', lambda: nc.tensor.transpose(p_[:, mc * 128:(mc + 1) * 128], o_[:, mc * 128:(mc + 1) * 128], self.ident_b[:]),
                         reads=[o_.name, 'ident_b'], writes=[p_.name])
                pv = p_[:].rearrange("p (k t) -> p k t", t=128)
                S.op('act', lambda: nc.scalar.copy(out=t_[:, 0:8, :], in_=pv[:, 0:8, :]), reads=[p_.name], writes=[t_.name + 'a'])
                S.op('dve', lambda: nc.vector.tensor_copy(out=t_[:, 8:16, :], in_=pv[:, 8:16, :]), reads=[p_.name], writes=[t_.name + 'b'])
                for cg in range(4):
                    pp = ps[cg]
                    for mc in range(KC):
                        S.op('pe', lambda: nc.tensor.matmul(pp[:], lhsT=t_[:, mc, :], rhs=wo[:, mc, cg * 512:(cg + 1) * 512],
                                                            start=(mc == 0), stop=(mc == KC - 1)),
                             reads=[t_.name + 'a', t_.name + 'b'] + wkeys, writes=[pp.name])
                    S.op('dve', lambda: nc.vector.tensor_tensor(out=n_[:, cg * 512:(cg + 1) * 512], in0=pp[:], in1=h_[:, cg * 512:(cg + 1) * 512], op=ALU.add),
                         reads=[pp.name, h_.name], writes=[n_.name + str(cg)])
                nk = [n_.name + str(cg) for cg in range(4)]
                if not last:
                    if b == 0:
                        S.dma('sp', self.hres[112:128, :], n_[112:128, :], reads=nk, pwrites=[self.hk(0)])
                    else:
                        S.dma('sp', self.hres[b * 128:(b + 1) * 128, :], n_[:], reads=nk, writes=[self.hk(b)])
                else:
                    y_ = yo[b % 2]
                    ks = [self.key('stc') for _ in range(3)]
                    S.op('act', lambda: nc.scalar.activation(out=junk[:], in_=n_[:], func=AF.Square, accum_out=st_[:, 3 * b:3 * b + 1]),
                         reads=nk, writes=[junk.name, ks[0]])
                    S.op('act', lambda: nc.scalar.activation(out=st_[:, 3 * b + 1:3 * b + 2], in_=st_[:, 3 * b:3 * b + 1], func=AF.Ln,
                                                             scale=1.0 / D, bias=EPS), reads=[ks[0]], writes=[ks[1]])
                    S.op('act', lambda: nc.scalar.activation(out=st_[:, 3 * b + 2:3 * b + 3], in_=st_[:, 3 * b + 1:3 * b + 2], func=AF.Exp,
                                                             scale=-0.5), reads=[ks[1]], writes=[ks[2]])
                    S.op('dve', lambda: nc.vector.scalar_tensor_tensor(out=y_[:], in0=n_[:], scalar=st_[:, 3 * b + 2:3 * b + 3], in1=fw[:],
                                                                        op0=ALU.mult, op1=ALU.mult), reads=nk + [ks[2], fw.name], writes=[y_.name])
                    S.dma('sp', self.out[(b - 1) * 128:b * 128, :], y_[:], reads=[y_.name], pwrites=['out'])

    def phase_b(self, l):
        self.mix_fox(l)
        self.S.fence()
        self.mix_gla(l)
        self.S.fence()
        self.mix_dn(l)
        self.S.fence()

    def silu_into(self, z, e, src_cols, nwt=None):
        nc, S, NB = self.nc, self.S, self.NB
        S.dma('act', z[:], self.projM[:, src_cols:src_cols + 128].rearrange("(n p) c -> p n c", p=128), reads=['projM'], writes=[z.name])
        S.op('act', lambda: nc.scalar.activation(out=z[:], in_=z[:], func=AF.Silu), reads=[z.name], writes=[z.name])
        if nwt is not None:
            S.op('pool', lambda: nc.gpsimd.tensor_tensor(out=z[:], in0=z[:], in1=nwt[:].unsqueeze(1).broadcast_to([128, NB, 128]), op=ALU.mult),
                 reads=[z.name, nwt.name], writes=[z.name])

    def mix_fox(self, l):
        nc, S, NB, P = self.nc, self.S, self.NB, self.P
        with contextlib.ExitStack() as st:
            T = lambda n, s, d=F32: st.enter_context(nc.sbuf_tensor(self.key(n), s, d))
            PS = lambda n, s, d=F32: st.enter_context(nc.psum_tensor(self.key(n), s, d))
            bf = T("bf", [128, 5])
            ff = T("ff", [128, NB, 5])
            sp_ = T("sp", [128, NB, 5])
            tot = T("tot", [128, NB, 5])
            offs = T("offs", [128, NB + 1, 5])
            csc = T("csc", [128, NB, 5])
            S.dma('sp', bf[:], self.b_f[l:l + 1, :].partition_broadcast(128), writes=[bf.name])
            S.dma('sp', ff[:], self.projM[:, M_FF:M_FF + 5].rearrange("(n p) c -> p n c", p=128), reads=['projM'], writes=[ff.name])
            S.op('dve', lambda: nc.vector.tensor_tensor(out=ff[:], in0=ff[:], in1=bf[:].unsqueeze(1).broadcast_to([128, NB, 5]), op=ALU.add),
                 reads=[ff.name, bf.name], writes=[ff.name])
            S.op('act', lambda: nc.scalar.activation(out=sp_[:], in_=ff[:], func=AF.Exp, scale=-1.0), reads=[ff.name], writes=[sp_.name])
            S.op('act', lambda: nc.scalar.activation(out=sp_[:], in_=sp_[:], func=AF.Ln, bias=1.0), reads=[sp_.name], writes=[sp_.name])
            pc = PS("pc", [128, 512])
            pt_ = PS("ptot", [128, 512])
            spf = sp_[:].rearrange("p n c -> p (n c)")
            S.op('pe', lambda: nc.tensor.matmul(pc[:, 0:NB * 5], lhsT=self.tri_f[:], rhs=spf, start=True, stop=True), reads=[sp_.name, 'tri_f'], writes=[pc.name])
            S.op('pe', lambda: nc.tensor.matmul(pt_[:, 0:NB * 5], lhsT=self.ones_f[:], rhs=spf, start=True, stop=True), reads=[sp_.name, 'ones_f'], writes=[pt_.name])
            S.op('act', lambda: nc.scalar.copy(out=tot[:].rearrange("p n c -> p (n c)"), in_=pt_[:, 0:NB * 5]), reads=[pt_.name], writes=[tot.name])
            S.op('dve', lambda: nc.vector.memset(offs[:, 0, :], 0.0), writes=[offs.name])
            for j in range(NB):
                S.op('dve', lambda: nc.vector.tensor_tensor(out=offs[:, j + 1, :], in0=offs[:, j, :], in1=tot[:, j, :], op=ALU.add),
                     reads=[offs.name, tot.name], writes=[offs.name])
            S.op('dve', lambda: nc.vector.tensor_tensor(out=csc[:].rearrange("p n c -> p (n c)"), in0=pc[:, 0:NB * 5],
                                                        in1=offs[:, 0:NB, :].rearrange("p n c -> p (n c)"), op=ALU.add),
                 reads=[pc.name, offs.name], writes=[csc.name])
            if getattr(self, 'fox_stage', 0) == 1:
                return
            qf = T("qf", [128, P])
            kf = T("kf", [128, P])
            qb = T("qb", [128, P], BF16)
            kb = T("kb", [128, P], BF16)
            vf = T("vf", [128, NB, 128])
            vb = T("vb", [128, NB, 132], BF16)
            btab = T("btab", [128, NB, NB])
            ogh = T("ogh", [128, NB, 128], BF16)
            gate = T("gate", [128, NB, 128])
            gate_e = T("gate_e", [128, NB, 128])
            rec = T("rec", [128, 2 * NB])
            pts = [T("pt%d" % i, [128, 4, 128], BF16) for i in range(3)]
            sb = [PS("sb%d" % i, [128, 512]) for i in range(3)]
            acc = [PS("acc%d" % i, [128, 512]) for i in range(2)]
            S.op('pool', lambda: nc.gpsimd.memset(vb[:, :, 128:129], 1.0), writes=[vb.name + 'one'])
            S.op('pool', lambda: nc.gpsimd.memset(vb[0:112, 0, 128:129], 0.0), reads=[vb.name + 'one'], writes=[vb.name + 'one'])
            cnt = 0
            for h in range(5):
                S.dma('sp', qf[:], self.projT[(CH_FQ + h) * 128:(CH_FQ + h + 1) * 128, :], reads=['projT'], writes=[qf.name])
                S.dma('act', kf[:], self.projT[(CH_FK + h) * 128:(CH_FK + h + 1) * 128, :], reads=['projT'], writes=[kf.name])
                S.dma('sp', vf[:], self.projM[:, M_FV + h * 128:M_FV + (h + 1) * 128].rearrange("(n p) c -> p n c", p=128), reads=['projM'], writes=[vf.name])
                S.op('act', lambda: nc.scalar.mul(out=qb[:], in_=qf[:], mul=128.0 ** -0.5), reads=[qf.name], writes=[qb.name])
                S.op('dve', lambda: nc.vector.tensor_copy(out=kb[:], in_=kf[:]), reads=[kf.name], writes=[kb.name])
                S.op('pool', lambda: nc.gpsimd.tensor_copy(out=vb[:, :, 0:128], in_=vf[:]), reads=[vf.name], writes=[vb.name])
                self.silu_into(gate, gate_e, M_FZ + h * 128)
                if getattr(self, 'fox_stage', 0) == 2:
                    continue
                for i in range(NB):
                    S.op('dve', lambda: nc.vector.tensor_scalar(out=btab[:, i, 0:i + 1], in0=csc[:, 0:i + 1, h], scalar1=offs[:, i + 1, h:h + 1],
                                                                scalar2=None, op0=ALU.subtract), reads=[csc.name, offs.name], pwrites=[btab.name])
                if getattr(self, 'fox_stage', 0) == 3:
                    continue
                for i in range(getattr(self, 'fox_ni', NB)):
                    a_ = acc[i % 2]
                    for jb in range(0, i + 1, 4):
                        js = list(range(jb, min(jb + 4, i + 1)))
                        s_ = sb[cnt % 3]
                        p_ = pts[cnt % 3]
                        cnt += 1
                        for j in js:
                            S.op('pe', lambda: nc.tensor.matmul(s_[:, (j - jb) * 128:(j - jb + 1) * 128], lhsT=kb[:, j * 128:(j + 1) * 128],
                                                                rhs=qb[:, i * 128:(i + 1) * 128], start=True, stop=True),
                                 reads=[kb.name, qb.name], writes=[s_.name])
                        for j in js:
                            S.op('act', lambda: nc.scalar.activation(out=p_[:, j - jb, :], in_=s_[:, (j - jb) * 128:(j - jb + 1) * 128], func=AF.Exp,
                                                                     bias=btab[:, i, j:j + 1], scale=1.0),
                                 reads=[s_.name, btab.name], writes=[p_.name])
                        if i in js and getattr(self, 'fox_var', 0) != 2:
                            S.op('pool', lambda: nc.gpsimd.tensor_tensor(out=p_[:, i - jb, :], in0=p_[:, i - jb, :], in1=self.tri_b[:], op=ALU.mult),
                                 reads=[p_.name, 'tri_b'], writes=[p_.name])
                        for j in js:
                            NV = 128 if getattr(self, 'fox_var', 0) == 1 else 129
                            S.op('pe', lambda: nc.tensor.matmul(a_[:, 0:NV], lhsT=p_[:, j - jb, :], rhs=vb[:, j, 0:NV], start=(j == 0), stop=(j == i)),
                                 reads=[p_.name, vb.name, vb.name + 'one'], writes=[a_.name])
                    S.op('dve', lambda: nc.vector.tensor_scalar_max(out=rec[:, 2 * i:2 * i + 1], in0=a_[:, 128:129], scalar1=1e-30), reads=[a_.name], writes=[rec.name + 'a'])
                    S.op('dve', lambda: nc.vector.reciprocal(out=rec[:, 2 * i + 1:2 * i + 2], in_=rec[:, 2 * i:2 * i + 1]), reads=[rec.name + 'a'], writes=[rec.name + 'b'])
                    S.op('dve', lambda: nc.vector.scalar_tensor_tensor(out=ogh[:, i, :], in0=a_[:, 0:128], scalar=rec[:, 2 * i + 1:2 * i + 2], in1=gate[:, i, :],
                                                                       op0=ALU.mult, op1=ALU.mult), reads=[a_.name, rec.name + 'b', gate.name], pwrites=[ogh.name])
                c0 = O_FOX + h * 128
                S.dma('sp', self.og[:, c0:c0 + 128].rearrange("(n p) c -> p n c", p=128), ogh[:], reads=[ogh.name], pwrites=[self.ogk(b_) for b_ in range(NB)])

    def mix_gla(self, l):
        nc, S, NB, P, NCH = self.nc, self.S, self.NB, self.P, self.NCH
        with contextlib.ExitStack() as st:
            T = lambda n, s, d=F32: st.enter_context(nc.sbuf_tensor(self.key(n), s, d))
            PS = lambda n, s, d=F32: st.enter_context(nc.psum_tensor(self.key(n), s, d))
            gl16 = T("gl16", [16, P])
            nwt = T("gnw", [128, 128])
            qf = T("gqf", [64, P])
            kf = T("gkf", [64, P])
            wg = T("wg", [16, 64])
            nb_ = T("gnb", [64, 2])
            csA = T("csA", [64, P])
            csB_full = T("csB", [128, P])
            csB = csB_full[0:64, :]
            eg_full = T("geg", [128, P])
            eg = eg_full[0:64, :]
            qd = T("gqd", [64, P], BF16)
            ki = T("gki", [64, P], BF16)
            kdT = T("gkdT", [64, P], BF16)
            cl = T("gcl", [64, NCH])
            egl = T("gegl", [64, NCH])
            kd_tm = T("gkdtm", [128, NB, 64], BF16)
            vf = eg_full[:].rearrange("p (n c) -> p n c", c=128)
            vb = T("gvb", [128, NB, 128], BF16)
            gate = T("ggate", [128, NB, 128])
            gate_e = csB_full[:].rearrange("p (n c) -> p n c", c=128)
            at_all = T("gat", [128, NB, 128], BF16)
            Sst = T("gS", [64, 128])
            Sb = [T("gSb%d" % i, [64, 128], BF16) for i in range(2)]
            ogh = T("gogh", [128, NB, 128], BF16)
            stat = T("gstat", [128, 3 * NB])
            junk = T("gjunk", [128, 128], BF16)
            pz = PS("gpz", [128, 512])
            pat = PS("gpat", [128, 512])
            pkd = PS("gpkd", [128, 1024], BF16)
            po = [PS("gpo%d" % i, [128, 512]) for i in range(2)]
            pds = [PS("gpds%d" % i, [128, 512]) for i in range(2)]
            S.dma('sp', gl16[:], self.projT[CH_SM * 128:CH_SM * 128 + 16, :], reads=['projT'], writes=[gl16.name])
            S.dma('sp', nwt[:], self.gla_norm_w[l:l + 1, :].partition_broadcast(128), writes=[nwt.name])
            tgs = [(t0, min(512, P - t0)) for t0 in range(0, P, 512)]
            csAv = csA[:].rearrange("p (n c) -> p n c", c=64)
            csBv = csB[:].rearrange("p (n c) -> p n c", c=64)
            for h in range(5):
                r0 = (CH_GQ + h // 2) * 128 + (h % 2) * 64
                S.dma('sp', qf[:], self.projT[r0:r0 + 64, :], reads=['projT'], writes=[qf.name])
                r0 = (CH_GK + h // 2) * 128 + (h % 2) * 64
                S.dma('act', kf[:], self.projT[r0:r0 + 64, :], reads=['projT'], writes=[kf.name])
                S.dma('sp', wg[:], self.w_gk2[l, :, h * 64:(h + 1) * 64], writes=[wg.name])
                S.dma('sp', nb_[:, 0:1], self.b_gk[l:l + 1, h * 64:(h + 1) * 64].rearrange("o c -> c o"), writes=[nb_.name])
                S.op('dve', lambda: nc.vector.tensor_scalar(out=nb_[:, 1:2], in0=nb_[:, 0:1], scalar1=-1.0, scalar2=None, op0=ALU.mult),
                     reads=[nb_.name], writes=[nb_.name + 'n'])
                S.dma('sp', vf[:], self.projM[:, M_GV + h * 128:M_GV + (h + 1) * 128].rearrange("(n p) c -> p n c", p=128), reads=['projM'], writes=[vf.name])
                S.op('pool', lambda: nc.gpsimd.tensor_copy(out=vb[:], in_=vf[:]), reads=[vf.name], writes=[vb.name])
                self.silu_into(gate, gate_e, M_GZ + h * 128, nwt)
                for (t0, tn) in tgs:
                    S.op('pe', lambda: nc.tensor.matmul(pz[0:64, 0:tn], lhsT=wg[:, :], rhs=gl16[:, t0:t0 + tn], start=True, stop=True),
                         reads=[wg.name, gl16.name], writes=[pz.name])
                    S.op('act', lambda: nc.scalar.activation(out=csA[:, t0:t0 + tn], in_=pz[0:64, 0:tn], func=AF.Exp, scale=-1.0, bias=nb_[:, 1:2]),
                         reads=[pz.name, nb_.name + 'n'], pwrites=[csA.name])
                S.op('act', lambda: nc.scalar.activation(out=csA[:], in_=csA[:], func=AF.Ln, bias=1.0), reads=[csA.name], writes=[csA.name])
                src, dst, srcv, dstv = csA, csB, csAv, csBv
                for sft in (1, 2, 4, 8, 16, 32):
                    S.op('dve', lambda: nc.vector.tensor_tensor(out=dstv[:, :, sft:64], in0=srcv[:, :, sft:64], in1=srcv[:, :, 0:64 - sft], op=ALU.add),
                         reads=[src.name], writes=[dst.name])
                    S.op('pool', lambda: nc.gpsimd.tensor_copy(out=dstv[:, :, 0:sft], in_=srcv[:, :, 0:sft]), reads=[src.name], pwrites=[dst.name])
                    src, dst, srcv, dstv = dst, src, dstv, srcv
                assert src is csA
                S.op('pool', lambda: nc.gpsimd.tensor_copy(out=cl[:], in_=csAv[:, :, 63]), reads=[csA.name], writes=[cl.name])
                S.op('act', lambda: nc.scalar.activation(out=eg[:], in_=csA[:], func=AF.Exp, scale=-1.0 / 16), reads=[csA.name], writes=[eg.name])
                S.op('dve', lambda: nc.vector.scalar_tensor_tensor(out=qd[:], in0=qf[:], scalar=0.125, in1=eg[:], op0=ALU.mult, op1=ALU.mult),
                     reads=[qf.name, eg.name], writes=[qd.name])
                S.op('act', lambda: nc.scalar.activation(out=eg[:], in_=csA[:], func=AF.Exp, scale=1.0 / 16), reads=[csA.name], writes=[eg.name])
                S.op('pool', lambda: nc.gpsimd.tensor_tensor(out=ki[:], in0=kf[:], in1=eg[:], op=ALU.mult), reads=[kf.name, eg.name], writes=[ki.name])
                S.op('dve', lambda: nc.vector.tensor_tensor(out=csBv, in0=csAv, in1=cl[:].unsqueeze(2).broadcast_to([64, NCH, 64]), op=ALU.subtract),
                     reads=[csA.name, cl.name], writes=[csB.name])
                S.op('act', lambda: nc.scalar.activation(out=csB[:], in_=csB[:], func=AF.Exp, scale=1.0 / 16), reads=[csB.name], writes=[csB.name])
                S.op('dve', lambda: nc.vector.tensor_tensor(out=kdT[:], in0=kf[:], in1=csB[:], op=ALU.mult), reads=[kf.name, csB.name], writes=[kdT.name])
                S.op('act', lambda: nc.scalar.activation(out=egl[:], in_=cl[:], func=AF.Exp, scale=-1.0 / 16), reads=[cl.name], writes=[egl.name])
                for b0 in range(0, NB, 16):
                    nb2 = min(16, NB - b0)
                    for b in range(b0, b0 + nb2):
                        S.op('pe', lambda: nc.tensor.transpose(pkd[:, (b - b0) * 64:(b - b0 + 1) * 64], kdT[:, b * 128:(b + 1) * 128], self.ident_b[0:64, 0:64]),
                             reads=[kdT.name, 'ident_b'], writes=[pkd.name])
                    S.op('act', lambda: nc.scalar.copy(out=kd_tm[:, b0:b0 + nb2, :], in_=pkd[:, 0:nb2 * 64].rearrange("p (n c) -> p n c", c=64)),
                         reads=[pkd.name], pwrites=[kd_tm.name])
                for b0 in range(0, NB, 4):
                    nb2 = min(4, NB - b0)
                    for b in range(b0, b0 + nb2):
                        S.op('pe', lambda: nc.tensor.matmul(pat[:, (b - b0) * 128:(b - b0 + 1) * 128], lhsT=ki[:, b * 128:(b + 1) * 128],
                                                            rhs=qd[:, b * 128:(b + 1) * 128], start=True, stop=True),
                             reads=[ki.name, qd.name], writes=[pat.name])
                    S.op('dve', lambda: nc.vector.tensor_tensor(out=at_all[:, b0:b0 + nb2, :], in0=pat[:, 0:nb2 * 128].rearrange("p (n c) -> p n c", c=128),
                                                                in1=self.bd_b[:].unsqueeze(1).broadcast_to([128, nb2, 128]), op=ALU.mult),
                         reads=[pat.name, 'bd_b'], pwrites=[at_all.name])
                S.op('pool', lambda: nc.gpsimd.memset(Sst[:], 0.0), writes=[Sst.name])
                S.op('pool', lambda: nc.gpsimd.memset(Sb[0][:], 0.0), writes=[Sb[0].name])
                for b in range(NB):
                    po_ = po[b % 2]
                    S.op('pe', lambda: nc.tensor.matmul(po_[:, 0:128], lhsT=at_all[:, b, :], rhs=vb[:, b, :], start=True, stop=False),
                         reads=[at_all.name, vb.name], writes=[po_.name])
                    for half in range(2):
                        n = 2 * b + half
                        r_ = slice(half * 64, half * 64 + 64)
                        cur, nxt = Sb[n % 2], Sb[(n + 1) % 2]
                        pd_ = pds[n % 2]
                        S.op('pe', lambda: nc.tensor.matmul(po_[r_, 0:128], lhsT=qd[:, n * 64:(n + 1) * 64], rhs=cur[:, :], start=False, stop=(half == 1)),
                             reads=[qd.name, cur.name], writes=[po_.name])
                        S.op('pe', lambda: nc.tensor.matmul(pd_[0:64, 0:128], lhsT=kd_tm[r_, b, :], rhs=vb[r_, b, :], start=True, stop=True),
                             reads=[kd_tm.name, vb.name], writes=[pd_.name])
                        S.op('dve', lambda: nc.vector.scalar_tensor_tensor(out=Sst[:], in0=Sst[:], scalar=egl[:, n:n + 1], in1=pd_[0:64, 0:128],
                                                                           op0=ALU.mult, op1=ALU.add), reads=[Sst.name, egl.name, pd_.name], writes=[Sst.name])
                        S.op('pool', lambda: nc.gpsimd.tensor_copy(out=nxt[:], in_=Sst[:]), reads=[Sst.name], writes=[nxt.name])
                    self.head_norm(po_, stat, junk, b, gate, ogh)
                c0 = O_GLA + h * 128
                S.dma('sp', self.og[:, c0:c0 + 128].rearrange("(n p) c -> p n c", p=128), ogh[:], reads=[ogh.name], pwrites=[self.ogk(b_) for b_ in range(NB)])

    def head_norm(self, po_, stat, junk, b, gate, ogh):
        nc, S = self.nc, self.S
        ks = [self.key('hn') for _ in range(3)]
        S.op('act', lambda: nc.scalar.activation(out=junk[:], in_=po_[:, 0:128], func=AF.Square, accum_out=stat[:, 3 * b:3 * b + 1]),
             reads=[po_.name], writes=[junk.name, ks[0]])
        S.op('act', lambda: nc.scalar.activation(out=stat[:, 3 * b + 1:3 * b + 2], in_=stat[:, 3 * b:3 * b + 1], func=AF.Ln, scale=1.0 / 128, bias=EPS),
             reads=[ks[0]], writes=[ks[1]])
        S.op('act', lambda: nc.scalar.activation(out=stat[:, 3 * b + 2:3 * b + 3], in_=stat[:, 3 * b + 1:3 * b + 2], func=AF.Exp, scale=-0.5),
             reads=[ks[1]], writes=[ks[2]])
        S.op('dve', lambda: nc.vector.scalar_tensor_tensor(out=ogh[:, b, :], in0=po_[:, 0:128], scalar=stat[:, 3 * b + 2:3 * b + 3], in1=gate[:, b, :],
                                                           op0=ALU.mult, op1=ALU.mult), reads=[po_.name, ks[2], gate.name], pwrites=[ogh.name])


    def mix_dn(self, l):
        nc, S, NB, P, NCH = self.nc, self.S, self.NB, self.P, self.NCH
        BIG = 30000.0
        QS = 128.0 ** -0.5
        with contextlib.ExitStack() as st:
            T = lambda n, s, d=F32: st.enter_context(nc.sbuf_tensor(self.key(n), s, d))
            PS = lambda n, s, d=F32: st.enter_context(nc.psum_tensor(self.key(n), s, d))
            g = nc.gpsimd
            mA = T("mA", [128, 128])
            mB = T("mB", [128, 128])
            bo_f = T("bo_f", [128, 128])
            ind0 = T("ind0", [128, 128])
            ind1 = T("ind1", [128, 128])
            sel = T("sel", [64, 6, 128])
            S.op('pool', lambda: g.memset(mA[:], 0.0), writes=[mA.name])
            S.op('pool', lambda: g.affine_select(out=mA[:], in_=mA[:], pattern=[[-1, 128]], compare_op=ALU.is_gt, fill=-BIG, base=0, channel_multiplier=1),
                 reads=[mA.name], writes=[mA.name])
            S.op('pool', lambda: g.memset(mA[64:128, 0:64], -BIG), reads=[mA.name], writes=[mA.name])
            S.op('pool', lambda: g.memset(mB[:], 0.0), writes=[mB.name])
            S.op('pool', lambda: g.affine_select(out=mB[:], in_=mB[:], pattern=[[1, 128]], compare_op=ALU.is_gt, fill=BIG, base=0, channel_multiplier=-1),
                 reads=[mB.name], writes=[mB.name])
            S.op('pool', lambda: g.memset(mB[0:64, 64:128], BIG), reads=[mB.name], writes=[mB.name])
            S.op('pool', lambda: g.memset(bo_f[:], 1.0), writes=[bo_f.name])
            S.op('pool', lambda: g.memset(bo_f[0:64, 64:128], 0.0), reads=[bo_f.name], writes=[bo_f.name])
            S.op('pool', lambda: g.memset(bo_f[64:128, 0:64], 0.0), reads=[bo_f.name], writes=[bo_f.name])
            S.op('pool', lambda: g.memset(ind0[:], 0.0), writes=[ind0.name])
            S.op('pool', lambda: g.memset(ind0[0:64, :], 1.0), reads=[ind0.name], writes=[ind0.name])
            S.op('pool', lambda: g.memset(ind1[:], 0.0), writes=[ind1.name])
            S.op('pool', lambda: g.memset(ind1[64:128, :], 1.0), reads=[ind1.name], writes=[ind1.name])
            S.op('pool', lambda: g.memset(sel[:], 1.0), writes=[sel.name])
            S.op('pool', lambda: g.affine_select(out=sel[0:32], in_=sel[0:32], pattern=[[1, 6], [0, 128]], compare_op=ALU.is_equal, fill=0.0, base=0,
                                                 channel_multiplier=-1), reads=[sel.name], writes=[sel.name])
            S.op('pool', lambda: g.affine_select(out=sel[32:64], in_=sel[32:64], pattern=[[1, 6], [0, 128]], compare_op=ALU.is_equal, fill=0.0, base=0,
                                                 channel_multiplier=-1), reads=[sel.name], writes=[sel.name])
            cw = T("cw", [128, 18, 4])
            for j in range(4):
                S.dma('sp', cw[:, :, j], self.conv_w[l, j:j + 1, :].rearrange("o (n c) -> c (o n)", c=128), pwrites=[cw.name],
                      allow_slow_non_contiguous=True)
            nwt = T("dnw", [128, 128])
            S.dma('sp', nwt[:], self.dn_norm_w[l:l + 1, :].partition_broadcast(128), writes=[nwt.name])
            colp = T("colp", [64, 4])
            S.dma('sp', colp[32:38, 0:1], self.a_log[l:l + 1, :].rearrange("o c -> c o"), pwrites=[colp.name])
            S.dma('sp', colp[32:38, 1:2], self.dt_bias[l:l + 1, :].rearrange("o c -> c o"), pwrites=[colp.name])
            S.op('act', lambda: nc.scalar.activation(out=colp[32:38, 2:3], in_=colp[32:38, 0:1], func=AF.Exp), reads=[colp.name], writes=[colp.name + 'A'])
            bcp = T("bcp", [128, 3, 6])
            S.dma('sp', bcp[:, 0, :], self.a_log[l:l + 1, :].partition_broadcast(128), pwrites=[bcp.name])
            S.dma('sp', bcp[:, 1, :], self.dt_bias[l:l + 1, :].partition_broadcast(128), pwrites=[bcp.name])
            S.op('act', lambda: nc.scalar.activation(out=bcp[:, 2, :], in_=bcp[:, 0, :], func=AF.Exp), reads=[bcp.name], writes=[bcp.name + 'A'])
            if getattr(self, 'dn_stage', 0) == 11:
                return
            xt = T("dx", [128, P])
            yt = T("dy", [128, P])
            rows = T("rows", [64, P])
            rb = rows[0:6, :]
            ra = rows[32:38, :]
            rt = xt[32:38, :]
            r0 = CH_SM * 128 + 32
            kb_, ka_ = rows.name + 'b', rows.name + 'a'
            S.dma('sp', rb, self.projT[r0:r0 + 6, :], reads=['projT'], writes=[kb_])
            S.dma('sp', ra, self.projT[r0 + 6:r0 + 12, :], reads=['projT'], writes=[ka_])
            S.op('act', lambda: nc.scalar.activation(out=rb, in_=rb, func=AF.Exp, scale=-1.0), reads=[kb_], writes=[kb_])
            S.op('pool', lambda: g.tensor_scalar_add(out=rb, in0=rb, scalar1=1.0), reads=[kb_], writes=[kb_])
            S.op('dve', lambda: nc.vector.reciprocal(out=rb, in_=rb), reads=[kb_], writes=[kb_])
            S.op('act', lambda: nc.scalar.activation(out=ra, in_=ra, func=AF.Exp, bias=colp[32:38, 1:2], scale=1.0), reads=[ka_, colp.name], writes=[ka_])
            S.op('act', lambda: nc.scalar.activation(out=ra, in_=ra, func=AF.Ln, bias=1.0), reads=[ka_], writes=[ka_])
            S.op('dve', lambda: nc.vector.tensor_scalar(out=ra, in0=ra, scalar1=colp[32:38, 2:3], scalar2=None, op0=ALU.mult),
                 reads=[ka_, colp.name + 'A'], writes=[ka_])
            if getattr(self, 'dn_stage', 0) == 12:
                return
            src, dst, sk, dk_ = ra, rt, ka_, xt.name
            for sft in (1, 2, 4, 8, 16, 32):
                sv_ = src.rearrange("p (n c) -> p n c", c=64)
                dv_ = dst.rearrange("p (n c) -> p n c", c=64)
                S.op('dve', lambda: nc.vector.tensor_tensor(out=dv_[:, :, sft:64], in0=sv_[:, :, sft:64], in1=sv_[:, :, 0:64 - sft], op=ALU.add),
                     reads=[sk], writes=[dk_])
                S.op('pool', lambda: g.tensor_copy(out=dv_[:, :, 0:sft], in_=sv_[:, :, 0:sft]), reads=[sk], pwrites=[dk_])
                src, dst, sk, dk_ = dst, src, dk_, sk
            assert sk == ka_
            if getattr(self, 'dn_stage', 0) == 13:
                return
            ps = [PS("dps%d" % i, [128, 512]) for i in range(7)]
            pTb = PS("dpTb", [128, 1024], BF16)
            dm = T("dm", [128, NB, 12])
            S.dma('sp', dm[:], self.projM[:, M_DB:M_DB + 12].rearrange("(n p) c -> p n c", p=128), reads=['projM'], writes=[dm.name])
            beta_tm = T("beta_tm", [128, NB, 6])
            lpos = T("lpos", [128, NB, 6])
            cs_tm = T("cs_tm", [128, NB, 6])
            negbeg = T("negbeg", [128, NB, 6])
            ekd = T("ekd", [128, NB, 6])
            egl = [T("egl%d" % i, [128, NB, 6]) for i in range(2)]
            if getattr(self, 'dn_stage', 0) == 14:
                return
            S.op('act', lambda: nc.scalar.activation(out=beta_tm[:], in_=dm[:, :, 0:6], func=AF.Exp, scale=-1.0), reads=[dm.name], writes=[beta_tm.name])
            S.op('pool', lambda: g.tensor_scalar_add(out=beta_tm[:], in0=beta_tm[:], scalar1=1.0), reads=[beta_tm.name], writes=[beta_tm.name])
            S.op('dve', lambda: nc.vector.reciprocal(out=beta_tm[:], in_=beta_tm[:]), reads=[beta_tm.name], writes=[beta_tm.name])
            S.op('dve', lambda: nc.vector.tensor_tensor(out=lpos[:], in0=dm[:, :, 6:12], in1=bcp[:, 1, :].unsqueeze(1).broadcast_to([128, NB, 6]), op=ALU.add),
                 reads=[dm.name, bcp.name], writes=[lpos.name])
            S.op('act', lambda: nc.scalar.activation(out=lpos[:], in_=lpos[:], func=AF.Exp), reads=[lpos.name], writes=[lpos.name])
            S.op('act', lambda: nc.scalar.activation(out=lpos[:], in_=lpos[:], func=AF.Ln, bias=1.0), reads=[lpos.name], writes=[lpos.name])
            S.op('dve', lambda: nc.vector.tensor_tensor(out=lpos[:], in0=lpos[:], in1=bcp[:, 2, :].unsqueeze(1).broadcast_to([128, NB, 6]), op=ALU.mult),
                 reads=[lpos.name, bcp.name + 'A'], writes=[lpos.name])
            if getattr(self, 'dn_stage', 0) == 15:
                return
            lpf = lpos[:].rearrange("p n c -> p (n c)")
            N6 = NB * 6
            for i, lt in enumerate((self.bd_f, bo_f, ind0, ind1)):
                S.op('pe', lambda: nc.tensor.matmul(ps[i][:, 0:N6], lhsT=lt[:], rhs=lpf, start=True, stop=True), reads=[lpos.name, lt.name], writes=[ps[i].name])
            if getattr(self, 'dn_stage', 0) == 16:
                return
            fl = lambda t: t[:].rearrange("p n c -> p (n c)")
            S.op('dve', lambda: nc.vector.tensor_copy(out=fl(cs_tm), in_=ps[0][:, 0:N6]), reads=[ps[0].name], writes=[cs_tm.name])
            S.op('act', lambda: nc.scalar.activation(out=fl(negbeg), in_=fl(cs_tm), func=AF.Exp, scale=-1.0), reads=[cs_tm.name], writes=[negbeg.name])
            S.op('dve', lambda: nc.vector.scalar_tensor_tensor(out=fl(negbeg), in0=fl(negbeg), scalar=-1.0, in1=fl(beta_tm), op0=ALU.mult, op1=ALU.mult),
                 reads=[negbeg.name, beta_tm.name], writes=[negbeg.name])
            if getattr(self, 'dn_stage', 0) == 17:
                return
            S.op('dve', lambda: nc.vector.tensor_tensor(out=fl(ekd), in0=fl(cs_tm), in1=ps[1][:, 0:N6], op=ALU.subtract), reads=[cs_tm.name, ps[1].name], writes=[ekd.name])
            S.op('act', lambda: nc.scalar.activation(out=fl(ekd), in_=fl(ekd), func=AF.Exp), reads=[ekd.name], writes=[ekd.name])
            if getattr(self, 'dn_stage', 0) == 18:
                return
            for i in range(2):
                S.op('act', lambda: nc.scalar.activation(out=fl(egl[i]), in_=ps[2 + i][:, 0:N6], func=AF.Exp, scale=-1.0), reads=[ps[2 + i].name], writes=[egl[i].name])
            if getattr(self, 'dn_stage', 0) == 19:
                return
            cs_bc = T("cs_bc", [128, P])
            bb = T("bb", [128, P])
            rn = T("rn", [128, 512])
            kT = T("kT", [128, P], BF16)
            qT = T("qT", [128, P], BF16)
            qg = T("qg", [128, P], BF16)
            Tt = T("Tt", [128, NB, 128], BF16)
            qkT = T("qkT", [128, NB, 128], BF16)
            kd_tm = T("dkd", [128, NB, 128], BF16)
            bv_tm = T("dbv", [128, NB, 128], BF16)
            gate = T("dgate", [128, NB, 128], BF16)
            ogh = T("dogh", [128, NB, 128], BF16)
            GB = getattr(self, 'dn_gb', 4)
            Ap = [T("Ap%d" % i, [128, GB * 128]) for i in range(2)]
            Bp = [T("Bp%d" % i, [128, GB * 128]) for i in range(2)]
            X = T("X", [128, GB * 128])
            G1 = T("G1", [128, GB * 128])
            G2 = T("G2", [128, GB * 128])
            G3 = T("G3", [128, GB * 128])
            Sst = T("dS", [128, 128])
            Sb = [T("dSb%d" % i, [128, 128], BF16) for i in range(2)]
            Rp = [T("Rp%d" % i, [128, 128], BF16) for i in range(2)]
            vn = [T("vn%d" % i, [128, 128], BF16) for i in range(2)]
            stat = T("dstat", [128, 3 * NB])
            junk = T("djunk", [128, 128], BF16)
            tgs = [(t0, min(512, P - t0)) for t0 in range(0, P, 512)]

            def conv_silu(ci):
                S.dma('sp', xt[:], self.projT[ci * 128:(ci + 1) * 128, :], reads=['projT'], writes=[xt.name])
                S.op('dve', lambda: nc.vector.tensor_scalar(out=yt[:], in0=xt[:], scalar1=cw[:, ci, 3:4], scalar2=None, op0=ALU.mult),
                     reads=[xt.name, cw.name], writes=[yt.name])
                for sft in (1, 2, 3):
                    S.op('dve', lambda: nc.vector.scalar_tensor_tensor(out=yt[:, sft:P], in0=xt[:, 0:P - sft], scalar=cw[:, ci, 3 - sft:4 - sft], in1=yt[:, sft:P],
                                                                       op0=ALU.mult, op1=ALU.add), reads=[xt.name, cw.name, yt.name], writes=[yt.name])
                S.op('act', lambda: nc.scalar.activation(out=yt[:], in_=yt[:], func=AF.Silu), reads=[yt.name], writes=[yt.name])

            def l2norm():
                S.op('act', lambda: nc.scalar.activation(out=xt[:], in_=yt[:], func=AF.Square), reads=[yt.name], writes=[xt.name])
                for gi, (t0, tn) in enumerate(tgs):
                    p_ = ps[gi % 2]
                    S.op('pe', lambda: nc.tensor.matmul(p_[:, 0:tn], lhsT=self.ones_f[:], rhs=xt[:, t0:t0 + tn], start=True, stop=True),
                         reads=[xt.name, 'ones_f'], writes=[p_.name])
                    S.op('act', lambda: nc.scalar.activation(out=rn[:, 0:tn], in_=p_[:, 0:tn], func=AF.Ln, bias=EPS), reads=[p_.name], writes=[rn.name])
                    S.op('act', lambda: nc.scalar.activation(out=rn[:, 0:tn], in_=rn[:, 0:tn], func=AF.Exp, scale=-0.5), reads=[rn.name], writes=[rn.name])
                    S.op('dve', lambda: nc.vector.tensor_tensor(out=yt[:, t0:t0 + tn], in0=yt[:, t0:t0 + tn], in1=rn[:, 0:tn], op=ALU.mult),
                         reads=[yt.name, rn.name], writes=[yt.name])

            def bcast_row(src_rows, skey, part0, dst):
                for gi, (t0, tn) in enumerate(tgs):
                    p_ = ps[gi % 2]
                    S.op('pe', lambda: nc.tensor.matmul(p_[:, 0:tn], lhsT=sel[part0:part0 + 6, h, :], rhs=src_rows[:, t0:t0 + tn], start=True, stop=True),
                         reads=[sel.name, skey], writes=[p_.name])
                    if gi % 2 == 0:
                        S.op('act', lambda: nc.scalar.copy(out=dst[:, t0:t0 + tn], in_=p_[:, 0:tn]), reads=[p_.name], pwrites=[dst.name])
                    else:
                        S.op('dve', lambda: nc.vector.tensor_copy(out=dst[:, t0:t0 + tn], in_=p_[:, 0:tn]), reads=[p_.name], pwrites=[dst.name])

            dn_stage = getattr(self, 'dn_stage', 0)
            if dn_stage == 1:
                return
            for h in range(getattr(self, 'dn_heads', 6)):
                bcast_row(ra, ka_, 32, cs_bc)
                S.op('act', lambda: nc.scalar.activation(out=bb[:], in_=cs_bc[:], func=AF.Exp, scale=-1.0), reads=[cs_bc.name], writes=[bb.name])
                conv_silu(h)
                l2norm()
                S.op('pool', lambda: g.tensor_scalar(out=qT[:], in0=yt[:], scalar1=QS, scalar2=None, op0=ALU.mult), reads=[yt.name], writes=[qT.name])
                S.op('dve', lambda: nc.vector.scalar_tensor_tensor(out=qg[:], in0=yt[:], scalar=QS, in1=bb[:], op0=ALU.mult, op1=ALU.mult),
                     reads=[yt.name, bb.name], writes=[qg.name])
                S.op('pool', lambda: g.memset(bb[:, 0:2], 0.0), reads=[bb.name], writes=[bb.name])
                bcast_row(rb, kb_, 0, bb)
                conv_silu(6 + h)
                l2norm()
                S.op('pool', lambda: g.tensor_copy(out=kT[:], in_=yt[:]), reads=[yt.name], writes=[kT.name])
                for b0 in range(0, NB, 8):
                    nb2 = min(8, NB - b0)
                    for b in range(b0, b0 + nb2):
                        S.op('pe', lambda: nc.tensor.transpose(pTb[:, (b - b0) * 128:(b - b0 + 1) * 128], kT[:, b * 128:(b + 1) * 128], self.ident_b[:]),
                             reads=[kT.name, 'ident_b'], writes=[pTb.name])
                    S.op('dve', lambda: nc.vector.tensor_tensor(out=kd_tm[:, b0:b0 + nb2, :], in0=pTb[:, 0:nb2 * 128].rearrange("p (n c) -> p n c", c=128),
                                                                in1=ekd[:, b0:b0 + nb2, h:h + 1].broadcast_to([128, nb2, 128]), op=ALU.mult),
                         reads=[pTb.name, ekd.name], pwrites=[kd_tm.name])
                conv_silu(12 + h)
                for b0 in range(0, NB, 4):
                    nb2 = min(4, NB - b0)
                    p_ = ps[4 + (b0 // 4) % 2]
                    for b in range(b0, b0 + nb2):
                        S.op('pe', lambda: nc.tensor.transpose(p_[:, (b - b0) * 128:(b - b0 + 1) * 128], yt[:, b * 128:(b + 1) * 128], self.ident_f[:]),
                             reads=[yt.name, 'ident_f'], writes=[p_.name])
                    S.op('dve', lambda: nc.vector.tensor_tensor(out=bv_tm[:, b0:b0 + nb2, :], in0=p_[:, 0:nb2 * 128].rearrange("p (n c) -> p n c", c=128),
                                                                in1=beta_tm[:, b0:b0 + nb2, h:h + 1].broadcast_to([128, nb2, 128]), op=ALU.mult),
                         reads=[p_.name, beta_tm.name], pwrites=[bv_tm.name])
                xv = xt[:].rearrange("p (n c) -> p n c", c=128)
                yv = yt[:].rearrange("p (n c) -> p n c", c=128)
                c0 = M_DZ + h * 128
                S.dma('act', xv, self.projM[:, c0:c0 + 128].rearrange("(n p) c -> p n c", p=128), reads=['projM'], writes=[xt.name])
                S.op('act', lambda: nc.scalar.activation(out=yt[:], in_=xt[:], func=AF.Silu), reads=[xt.name], writes=[yt.name])
                S.op('pool', lambda: g.tensor_tensor(out=gate[:], in0=yv, in1=nwt[:].unsqueeze(1).broadcast_to([128, NB, 128]), op=ALU.mult),
                     reads=[yt.name, nwt.name], writes=[gate.name])
                if dn_stage == 2:
                    continue
                for b0 in range(0, NB, GB):
                    nb = min(GB, NB - b0)
                    W = nb * 128
                    cols = slice(b0 * 128, b0 * 128 + W)
                    v3 = lambda t: t[:, 0:W].rearrange("p (n c) -> p n c", c=128)
                    bc3 = lambda t: t[:].unsqueeze(1).broadcast_to([128, nb, 128])
                    csb = cs_tm[:, b0:b0 + nb, h:h + 1].broadcast_to([128, nb, 128])
                    btb = beta_tm[:, b0:b0 + nb, h:h + 1].broadcast_to([128, nb, 128])
                    cs3 = cs_bc[:, cols].rearrange("p (n c) -> p n c", c=128)
                    S.op('dve', lambda: nc.vector.tensor_tensor(out=v3(G1), in0=cs3, in1=csb, op=ALU.subtract), reads=[cs_bc.name, cs_tm.name], writes=[G1.name])
                    S.op('pool', lambda: g.tensor_tensor(out=v3(G2), in0=v3(G1), in1=bc3(mB), op=ALU.add), reads=[G1.name, mB.name], writes=[G2.name])
                    S.op('dve', lambda: nc.vector.tensor_tensor(out=v3(G1), in0=v3(G1), in1=bc3(mA), op=ALU.add), reads=[G1.name, mA.name], writes=[G1.name])
                    S.op('act', lambda: nc.scalar.activation(out=G1[:, 0:W], in_=G1[:, 0:W], func=AF.Exp), reads=[G1.name], writes=[G1.name])
                    S.op('act', lambda: nc.scalar.activation(out=G2[:, 0:W], in_=G2[:, 0:W], func=AF.Exp, scale=-1.0), reads=[G2.name], writes=[G2.name])
                    for j in range(nb):
                        bc = slice((b0 + j) * 128, (b0 + j + 1) * 128)
                        S.op('pe', lambda: nc.tensor.matmul(ps[0][:, j * 128:(j + 1) * 128], lhsT=kT[:, bc], rhs=kT[:, bc], start=True, stop=True), reads=[kT.name], writes=[ps[0].name])
                        S.op('pe', lambda: nc.tensor.matmul(ps[2][:, j * 128:(j + 1) * 128], lhsT=kT[:, bc], rhs=qT[:, bc], start=True, stop=True), reads=[qT.name, kT.name], writes=[ps[2].name])
                    a_, b_ = Ap[0], Bp[0]
                    S.op('pool', lambda: g.tensor_tensor(out=v3(G1), in0=v3(G1), in1=btb, op=ALU.mult), reads=[G1.name, beta_tm.name], writes=[G1.name])
                    S.op('dve', lambda: nc.vector.tensor_tensor(out=a_[:, 0:W], in0=ps[0][:, 0:W], in1=G1[:, 0:W], op=ALU.mult), reads=[ps[0].name, G1.name], writes=[a_.name])
                    S.op('pool', lambda: g.tensor_tensor(out=G3[:, 0:W], in0=G2[:, 0:W], in1=bb[:, cols], op=ALU.mult), reads=[G2.name, bb.name], writes=[G3.name])
                    S.op('dve', lambda: nc.vector.tensor_tensor(out=b_[:, 0:W], in0=ps[0][:, 0:W], in1=G3[:, 0:W], op=ALU.mult), reads=[ps[0].name, G3.name], writes=[b_.name])
                    S.op('pool', lambda: g.tensor_tensor(out=v3(G2), in0=v3(G2), in1=bc3(self.ident_f), op=ALU.add), reads=[G2.name, 'ident_f'], writes=[G2.name])
                    S.op('dve', lambda: nc.vector.tensor_tensor(out=qkT[:, b0:b0 + nb, :], in0=v3(ps[2]), in1=v3(G2), op=ALU.mult), reads=[ps[2].name, G2.name], pwrites=[qkT.name])
                    S.op('pool', lambda: g.tensor_tensor(out=v3(X), in0=bc3(self.ident_f), in1=v3(b_), op=ALU.subtract), reads=[b_.name, 'ident_f'], writes=[X.name])
                    for lv in range(5):
                        a2, b2 = Ap[(lv + 1) % 2], Bp[(lv + 1) % 2]
                        for j in range(nb):
                            jc = slice(j * 128, (j + 1) * 128)
                            S.op('pe', lambda: nc.tensor.matmul(ps[3][:, jc], lhsT=b_[:, jc], rhs=a_[:, jc], start=True, stop=True), reads=[a_.name, b_.name], writes=[ps[3].name])
                            if lv < 4:
                                S.op('pe', lambda: nc.tensor.matmul(ps[4][:, jc], lhsT=a_[:, jc], rhs=b_[:, jc], start=True, stop=True), reads=[a_.name, b_.name], writes=[ps[4].name])
                        S.op('act', lambda: nc.scalar.copy(out=a2[:, 0:W], in_=ps[3][:, 0:W]), reads=[ps[3].name], writes=[a2.name])
                        if lv < 4:
                            S.op('dve', lambda: nc.vector.tensor_copy(out=b2[:, 0:W], in_=ps[4][:, 0:W]), reads=[ps[4].name], writes=[b2.name])
                        for j in range(nb):
                            jc = slice(j * 128, (j + 1) * 128)
                            S.op('pe', lambda: nc.tensor.matmul(ps[5][:, jc], lhsT=a2[:, jc], rhs=X[:, jc], start=True, stop=True), reads=[a2.name, X.name], writes=[ps[5].name])
                        S.op('dve', lambda: nc.vector.tensor_tensor(out=X[:, 0:W], in0=X[:, 0:W], in1=ps[5][:, 0:W], op=ALU.add), reads=[X.name, ps[5].name], writes=[X.name])
                        a_, b_ = a2, b2
                    S.op('pool', lambda: g.tensor_copy(out=Tt[:, b0:b0 + nb, :], in_=v3(X)), reads=[X.name], pwrites=[Tt.name])
                if dn_stage == 3:
                    continue
                S.op('pool', lambda: g.memset(Sst[:], 0.0), writes=[Sst.name])
                S.op('pool', lambda: g.memset(Sb[0][:], 0.0), writes=[Sb[0].name])
                for b in range(NB):
                    po_ = ps[3 + b % 2]
                    for half in range(2):
                        n = 2 * b + half
                        r_ = slice(half * 64, half * 64 + 64)
                        cur, nxt = Sb[n % 2], Sb[(n + 1) % 2]
                        R_, v_ = Rp[n % 2], vn[n % 2]
                        pk, pv, pd = ps[0], ps[1], ps[2]
                        S.op('pe', lambda: nc.tensor.matmul(pk[r_, 0:128], lhsT=kT[:, n * 64:(n + 1) * 64], rhs=cur[:], start=True, stop=True),
                             reads=[kT.name, cur.name], writes=[pk.name])
                        S.op('dve', lambda: nc.vector.scalar_tensor_tensor(out=R_[r_, :], in0=pk[r_, 0:128], scalar=negbeg[r_, b, h:h + 1], in1=bv_tm[r_, b, :],
                                                                           op0=ALU.mult, op1=ALU.add), reads=[pk.name, negbeg.name, bv_tm.name], writes=[R_.name])
                        S.op('pe', lambda: nc.tensor.matmul(pv[r_, 0:128], lhsT=Tt[r_, b, half * 64:half * 64 + 64], rhs=R_[r_, :], start=True, stop=True),
                             reads=[Tt.name, R_.name], writes=[pv.name])
                        S.op('act', lambda: nc.scalar.copy(out=v_[r_, :], in_=pv[r_, 0:128]), reads=[pv.name], writes=[v_.name])
                        S.op('pe', lambda: nc.tensor.matmul(po_[r_, 0:128], lhsT=qg[:, n * 64:(n + 1) * 64], rhs=cur[:], start=True, stop=False),
                             reads=[qg.name, cur.name], writes=[po_.name])
                        S.op('pe', lambda: nc.tensor.matmul(po_[r_, 0:128], lhsT=qkT[r_, b, half * 64:half * 64 + 64], rhs=v_[r_, :], start=False, stop=True),
                             reads=[qkT.name, v_.name], writes=[po_.name])
                        S.op('pe', lambda: nc.tensor.matmul(pd[:, 0:128], lhsT=kd_tm[r_, b, :], rhs=v_[r_, :], start=True, stop=True),
                             reads=[kd_tm.name, v_.name], writes=[pd.name])
                        S.op('dve', lambda: nc.vector.scalar_tensor_tensor(out=Sst[:], in0=Sst[:], scalar=egl[half][:, b, h:h + 1], in1=pd[:, 0:128],
                                                                           op0=ALU.mult, op1=ALU.add), reads=[Sst.name, egl[half].name, pd.name], writes=[Sst.name])
                        S.op('pool', lambda: g.tensor_copy(out=nxt[:], in_=Sst[:]), reads=[Sst.name], writes=[nxt.name])
                    self.head_norm(po_, stat, junk, b, gate, ogh)
                c0 = O_DN + h * 128
                S.dma('sp', self.og[:, c0:c0 + 128].rearrange("(n p) c -> p n c", p=128), ogh[:], reads=[ogh.name], pwrites=[self.ogk(b_) for b_ in range(NB)])


def kernel(x, meta_tokens, norm_w, w_in, conv_w, a_log, dt_bias, dn_norm_w, w_gk2, b_gk,
           gla_norm_w, b_f, w_out, final_norm_w):
    f = lambda a: np.ascontiguousarray(np.asarray(a, dtype=np.float32))
    x = f(x)
    B, S_LEN, _ = x.shape
    L = int(np.asarray(w_in).shape[0])
    prog = Prog(S_LEN, L)
    nc = prog.build()
    common = dict(meta_tokens=f(meta_tokens), norm_w=f(norm_w), w_in=f(w_in), conv_w=f(conv_w), a_log=f(a_log),
                  dt_bias=f(dt_bias), dn_norm_w=f(dn_norm_w), w_gk2=f(w_gk2), b_gk=f(b_gk), gla_norm_w=f(gla_norm_w),
                  b_f=f(b_f), w_out=f(w_out), final_norm_w=f(final_norm_w).reshape(1, -1))
    in_maps = [dict(common, x=np.ascontiguousarray(x[b])) for b in range(B)]
    res = run_bass_kernel_spmd(nc, in_maps, core_ids=list(range(B)))
    return np.stack([np.asarray(r["out"], dtype=np.float32) for r in res.results], 0)
```

```python
import contextlib
import numpy as np
import concourse.bass as bass
import concourse.mybir as mybir
from concourse.bass_utils import run_bass_kernel_spmd

F32 = mybir.dt.float32
BF16 = mybir.dt.bfloat16
AF = mybir.ActivationFunctionType
ALU = mybir.AluOpType

D = 2048
KC = 16
IN_COLS = 7585
EPS = 1e-6
C_DQ, C_DK, C_DV, C_DZ, C_DB, C_DA = 0, 768, 1536, 2304, 3072, 3078
C_GQ, C_GK, C_GV, C_GZ, C_GL = 3084, 3404, 3724, 4364, 5004
C_FQ, C_FK, C_FV, C_FZ, C_FF = 5020, 5660, 6300, 6940, 7580

FM_CHUNKS = []
for i in range(18):
    FM_CHUNKS.append((128, [(0, 128 * i, 128)]))
FM_CHUNKS.append((128, [(0, C_GQ, 128)]))
FM_CHUNKS.append((128, [(0, C_GQ + 128, 128)]))
FM_CHUNKS.append((64, [(0, C_GQ + 256, 64)]))
FM_CHUNKS.append((128, [(0, C_GK, 128)]))
FM_CHUNKS.append((128, [(0, C_GK + 128, 128)]))
FM_CHUNKS.append((64, [(0, C_GK + 256, 64)]))
FM_CHUNKS.append((64, [(0, C_GL, 16), (32, C_DB, 12)]))
for i in range(5):
    FM_CHUNKS.append((128, [(0, C_FQ + 128 * i, 128)]))
for i in range(5):
    FM_CHUNKS.append((128, [(0, C_FK + 128 * i, 128)]))
NFM = len(FM_CHUNKS)
FM_GROUPS = [[0, 1, 2, 3], [4, 5, 6, 7], [8, 9, 10, 11], [12, 13, 14, 15], [16, 17],
             [18, 19, 20, 21], [22, 23, 24], [25, 26, 27, 28], [29, 30, 31, 32], [33, 34]]
CH_GQ, CH_GK, CH_SM, CH_FQ, CH_FK = 18, 21, 24, 25, 30

TM_SEGS = [(0, C_DZ, 768), (768, C_GV, 1280), (2048, C_FV, 1280), (3328, C_DB, 12), (3340, C_FF, 5)]
TMW = 3345
M_DZ, M_GV, M_GZ, M_FV, M_FZ, M_DB, M_DA, M_FF = 0, 768, 1408, 2048, 2688, 3328, 3334, 3340
O_DN, O_GLA, O_FOX = 0, 768, 1408


class Sched:
    COMPUTE = ('pe', 'act', 'dve', 'pool')

    def __init__(self, nc, stack, n_dma_sems=20):
        self.nc = nc
        self.E = {'pe': nc.tensor, 'act': nc.scalar, 'dve': nc.vector, 'pool': nc.gpsimd, 'sp': nc.sync}
        self.psem = {e: stack.enter_context(nc.semaphore('prog_' + e)) for e in self.COMPUTE}
        self.cnt = {e: 0 for e in self.COMPUTE}
        self.queues = ('sp', 'act', 'pool')
        self.dsem = {q: [stack.enter_context(nc.semaphore('d%s%d' % (q, i))) for i in range(n_dma_sems)]
                     for q in self.queues}
        self.dval = {q: [0] * n_dma_sems for q in self.queues}
        self.dnext = {q: 0 for q in self.queues}
        self.waited = {}
        self.xw = {}
        self.pw = {}
        self.rd = {}
        self.n_wait = 0
        self.n_ins = 0

    def _wait(self, eng, src, val, same_ok):
        if src[0] == 'c':
            e = src[1]
            if e == eng and (same_ok or e == 'pe'):
                return
            sem = self.psem[e]
        else:
            sem = self.dsem[src[1]][src[2]]
        key = (eng, src)
        if self.waited.get(key, 0) >= val:
            return
        self.waited[key] = val
        self.E[eng].wait_ge(sem, val)
        self.n_wait += 1

    def _deps(self, eng, reads, writes, pwrites):
        for k in reads:
            for m in (self.xw.get(k), self.pw.get(k)):
                if m:
                    for src, v in m.items():
                        self._wait(eng, src, v, False)
        for k in writes:
            for m in (self.xw.get(k), self.pw.get(k)):
                if m:
                    for src, v in m.items():
                        self._wait(eng, src, v, False)
            m = self.rd.get(k)
            if m:
                for src, v in m.items():
                    self._wait(eng, src, v, True)
        for k in pwrites:
            m = self.xw.get(k)
            if m:
                for src, v in m.items():
                    self._wait(eng, src, v, False)
            m = self.rd.get(k)
            if m:
                for src, v in m.items():
                    self._wait(eng, src, v, True)

    def _record(self, src, val, reads, writes, pwrites):
        for k in reads:
            if k not in writes:
                self.rd.setdefault(k, {})[src] = val
        for k in writes:
            self.xw[k] = {src: val}
            self.pw[k] = {}
            self.rd[k] = {}
        for k in pwrites:
            self.pw.setdefault(k, {})[src] = val

    def op(self, eng, fn, reads=(), writes=(), pwrites=()):
        self._deps(eng, reads, writes, pwrites)
        ins = fn()
        self.cnt[eng] += 1
        ins.then_inc(self.psem[eng], 1)
        self._record(('c', eng), self.cnt[eng], reads, writes, pwrites)
        self.n_ins += 1
        return ins

    def dma(self, q, out, in_, reads=(), writes=(), pwrites=(), **kw):
        if q == 'act':
            q = 'sp'
        eng = q
        self._deps(eng, reads, writes, pwrites)
        i = self.dnext[q]
        self.dnext[q] = (i + 1) % len(self.dsem[q])
        if self.dval[q][i] > 0:
            self._wait(eng, ('d', q, i), self.dval[q][i], False)
        ins = self.E[eng].dma_start(out=out, in_=in_, **kw)
        self.dval[q][i] += 16
        ins.then_inc(self.dsem[q][i], 16)
        self._record(('d', q, i), self.dval[q][i], reads, writes, pwrites)
        self.n_ins += 1
        return ins

    def fence(self):
        for eng in ('pe', 'act', 'dve', 'pool', 'sp'):
            self.finish(eng)

    def finish(self, eng='sp'):
        for e in self.COMPUTE:
            if self.cnt[e] > 0:
                self._wait(eng, ('c', e), self.cnt[e], False)
        for q in self.queues:
            for i in range(len(self.dsem[q])):
                if self.dval[q][i] > 0:
                    self._wait(eng, ('d', q, i), self.dval[q][i], False)


class Prog:
    def __init__(self, S_LEN, L, debug=False, stop_after=None):
        self.S_LEN, self.L, self.debug = S_LEN, L, debug
        self.NB = S_LEN // 128 + 1
        self.P = self.NB * 128
        self.NCH = self.NB * 2
        self.stop_after = stop_after
        nc = self.nc = bass.Bass("TRN2", target_bir_lowering=False)
        P = self.P
        I = lambda n, s: nc.dram_tensor(n, s, F32, kind="ExternalInput").ap()
        self.x = I("x", [S_LEN, D])
        self.meta = I("meta_tokens", [16, D])
        self.norm_w = I("norm_w", [L, D])
        self.w_in = I("w_in", [L, D, IN_COLS])
        self.conv_w = I("conv_w", [L, 4, 2304])
        self.a_log = I("a_log", [L, 6])
        self.dt_bias = I("dt_bias", [L, 6])
        self.dn_norm_w = I("dn_norm_w", [L, 128])
        self.w_gk2 = I("w_gk2", [L, 16, 320])
        self.b_gk = I("b_gk", [L, 320])
        self.gla_norm_w = I("gla_norm_w", [L, 128])
        self.b_f = I("b_f", [L, 5])
        self.w_out = I("w_out", [L, D, D])
        self.final_norm_w = I("final_norm_w", [1, D])
        self.out = nc.dram_tensor("out", [S_LEN, D], F32, kind="ExternalOutput").ap()
        kind = "ExternalOutput" if debug else "Internal"
        self.hres = nc.dram_tensor("hres", [P, D], F32, kind=kind).ap()
        self.projT = nc.dram_tensor("projT", [NFM * 128, P], F32, kind=kind).ap()
        self.projM = nc.dram_tensor("projM", [P, TMW], F32, kind=kind).ap()
        self.og = nc.dram_tensor("og", [P, D], BF16, kind=kind).ap()
        if debug:
            self.hT_dbg = nc.dram_tensor("hT_dbg", [128, KC * P], BF16, kind="ExternalOutput").ap()
        self.uid = 0

    def key(self, s):
        self.uid += 1
        return '%s#%d' % (s, self.uid)

    def build(self):
        nc = self.nc
        with contextlib.ExitStack() as st:
            self.S = Sched(nc, st)
            self.consts(st)
            self.init_hres()
            for l in range(self.L):
                self.phase_a(l)
                self.S.fence()
                if self.stop_after == ('a', l):
                    break
                self.phase_b(l)
                if self.stop_after == ('b', l):
                    break
                self.phase_c(l)
                self.S.fence()
            self.S.finish()
        return nc

    def consts(self, st):
        nc, S = self.nc, self.S
        T = lambda n, s, d=F32: st.enter_context(nc.sbuf_tensor(n, s, d))
        self.ident_b = T("ident_b", [128, 128], BF16)
        self.ident_f = T("ident_f", [128, 128], F32)
        self.ones_f = T("ones_f", [128, 128], F32)
        self.tri_f = T("tri_f", [128, 128], F32)
        self.tri_b = T("tri_b", [128, 128], BF16)
        self.bd_b = T("bd_b", [128, 128], BF16)
        self.bd_f = T("bd_f", [128, 128], F32)
        self.zeros_f = T("zeros_f", [128, 2048], F32)
        g = nc.gpsimd
        for t in (self.ident_b, self.ident_f):
            S.op('pool', lambda: g.memset(t[:], 1.0), writes=[t.name])
            S.op('pool', lambda: g.affine_select(out=t[:], in_=t[:], pattern=[[-1, 128]], compare_op=ALU.is_equal,
                                                 fill=0.0, base=0, channel_multiplier=1), reads=[t.name], writes=[t.name])
        S.op('pool', lambda: g.memset(self.ones_f[:], 1.0), writes=['ones_f'])
        S.op('pool', lambda: g.memset(self.zeros_f[:], 0.0), writes=['zeros_f'])
        for t in (self.tri_f, self.tri_b, self.bd_f, self.bd_b):
            S.op('pool', lambda: g.memset(t[:], 1.0), writes=[t.name])
            S.op('pool', lambda: g.affine_select(out=t[:], in_=t[:], pattern=[[1, 128]], compare_op=ALU.is_ge,
                                                 fill=0.0, base=0, channel_multiplier=-1), reads=[t.name], writes=[t.name])
        for t in (self.bd_f, self.bd_b):
            S.op('pool', lambda: g.memset(t[0:64, 64:128], 0.0), reads=[t.name], writes=[t.name])

    def hk(self, b):
        return 'hres_%d' % b

    def ogk(self, b):
        return 'og_%d' % b

    def init_hres(self):
        S = self.S
        S.dma('sp', self.hres[0:112, :], self.zeros_f[0:112, :], reads=['zeros_f'], pwrites=[self.hk(0)])
        S.dma('sp', self.hres[112:128, :], self.meta[:, :], pwrites=[self.hk(0)])
        for b in range(1, self.NB):
            S.dma('sp' if b % 2 else 'act', self.hres[b * 128:(b + 1) * 128, :], self.x[(b - 1) * 128:b * 128, :], writes=[self.hk(b)])

    def phase_a(self, l):
        nc, S, NB, P = self.nc, self.S, self.NB, self.P
        with contextlib.ExitStack() as st:
            T = lambda n, s, d=F32: st.enter_context(nc.sbuf_tensor(self.key(n), s, d))
            hT = T("hT", [128, KC, P], BF16)
            kh = self.key('hT')
            hkeys = [kh + '_%d' % b for b in range(NB)]
            with contextlib.ExitStack() as st1:
                T1 = lambda n, s, d=F32: st1.enter_context(nc.sbuf_tensor(self.key(n), s, d))
                nw = T1("nw", [128, D])
                xb = [T1("xb%d" % i, [128, D]) for i in range(2)]
                hn = [T1("hn%d" % i, [128, D], BF16) for i in range(2)]
                junk = T1("junk", [128, D], BF16)
                st_ = T1("stat", [128, 3 * NB])
                pT = [st1.enter_context(nc.psum_tensor(self.key("pTa%d" % i), [128, D], BF16)) for i in range(2)]
                S.dma('sp', nw[:], self.norm_w[l:l + 1, :].partition_broadcast(128), writes=[nw.name])
                for b in range(NB):
                    x_ = xb[b % 2]
                    h_ = hn[b % 2]
                    p_ = pT[b % 2]
                    S.dma('sp' if b % 2 == 0 else 'act', x_[:], self.hres[b * 128:(b + 1) * 128, :], reads=[self.hk(b)], writes=[x_.name])
                    ks = [self.key('st') for _ in range(3)]
                    S.op('act', lambda: nc.scalar.activation(out=junk[:], in_=x_[:], func=AF.Square,
                                                             accum_out=st_[:, 3 * b:3 * b + 1]), reads=[x_.name], writes=[junk.name, ks[0]])
                    S.op('act', lambda: nc.scalar.activation(out=st_[:, 3 * b + 1:3 * b + 2], in_=st_[:, 3 * b:3 * b + 1], func=AF.Ln,
                                                             scale=1.0 / D, bias=EPS), reads=[ks[0]], writes=[ks[1]])
                    S.op('act', lambda: nc.scalar.activation(out=st_[:, 3 * b + 2:3 * b + 3], in_=st_[:, 3 * b + 1:3 * b + 2], func=AF.Exp,
                                                             scale=-0.5), reads=[ks[1]], writes=[ks[2]])
                    S.op('dve', lambda: nc.vector.scalar_tensor_tensor(out=h_[:], in0=x_[:], scalar=st_[:, 3 * b + 2:3 * b + 3], in1=nw[:],
                                                                       op0=ALU.mult, op1=ALU.mult), reads=[x_.name, ks[2], nw.name], writes=[h_.name])
                    for kc in range(KC):
                        S.op('pe', lambda: nc.tensor.transpose(p_[:, kc * 128:(kc + 1) * 128], h_[:, kc * 128:(kc + 1) * 128], self.ident_b[:]),
                             reads=[h_.name, 'ident_b'], writes=[p_.name])
                    pv = p_[:].rearrange("p (k t) -> p k t", t=128)
                    S.op('act', lambda: nc.scalar.copy(out=hT[:, 0:8, b * 128:(b + 1) * 128], in_=pv[:, 0:8, :]), reads=[p_.name], writes=[hkeys[b] + 'a'])
                    S.op('dve', lambda: nc.vector.tensor_copy(out=hT[:, 8:16, b * 128:(b + 1) * 128], in_=pv[:, 8:16, :]), reads=[p_.name], writes=[hkeys[b] + 'b'])
            S.fence()
            if self.debug:
                S.dma('sp', self.hT_dbg[:, :], hT[:].rearrange("p k t -> p (k t)"), reads=[k + s_ for k in hkeys for s_ in 'ab'], writes=['hT_dbg'])
            wt = [T("wt%d" % i, [128, KC, 512], BF16) for i in range(2)]
            stg = [T("stg%d" % i, [128, 512]) for i in range(4)]
            ps = [st.enter_context(nc.psum_tensor(self.key("psA%d" % i), [128, 512], F32)) for i in range(4)]
            wsrc = self.w_in
            gi = 0
            cnt = 0
            tgs = [(t0, min(512, P - t0)) for t0 in range(0, P, 512)]

            def load_w(w, segs, need_zero):
                S.op('pool', lambda: nc.gpsimd.memset(w[:, 0, 0:2] if not need_zero else w[:], 0.0), writes=[w.name])
                for (d0, s0, n) in segs:
                    S.dma('pool', w[:, :, d0:d0 + n], wsrc[l, :, s0:s0 + n].rearrange("(k p) c -> p k c", p=128), pwrites=[w.name])

            for grp in FM_GROUPS:
                w = wt[gi % 2]
                gi += 1
                segs = []
                need_zero = False
                for j, ci in enumerate(grp):
                    M, sg = FM_CHUNKS[ci]
                    tot = sum(n for _, _, n in sg)
                    if tot != M:
                        need_zero = True
                    for (d0, s0, n) in sg:
                        if segs and segs[-1][0] + segs[-1][2] == j * 128 + d0 and segs[-1][1] + segs[-1][2] == s0:
                            segs[-1] = (segs[-1][0], segs[-1][1], segs[-1][2] + n)
                        else:
                            segs.append((j * 128 + d0, s0, n))
                load_w(w, segs, need_zero)
                for (t0, tn) in tgs:
                    b0, b1 = t0 // 128, (t0 + tn) // 128
                    hk = [hkeys[b] + s for b in range(b0, b1) for s in 'ab']
                    for j, ci in enumerate(grp):
                        M = FM_CHUNKS[ci][0]
                        p_ = ps[cnt % 4]
                        s_ = stg[cnt % 4]
                        for kc in range(KC):
                            S.op('pe', lambda: nc.tensor.matmul(p_[0:M, 0:tn], lhsT=w[:, kc, j * 128:j * 128 + M], rhs=hT[:, kc, t0:t0 + tn],
                                                                start=(kc == 0), stop=(kc == KC - 1)),
                                 reads=[w.name] + hk, writes=[p_.name])
                        if cnt % 2 == 0:
                            S.op('act', lambda: nc.scalar.copy(out=s_[0:M, 0:tn], in_=p_[0:M, 0:tn]), reads=[p_.name], writes=[s_.name])
                        else:
                            S.op('dve', lambda: nc.vector.tensor_copy(out=s_[0:M, 0:tn], in_=p_[0:M, 0:tn]), reads=[p_.name], writes=[s_.name])
                        S.dma('sp', self.projT[ci * 128:ci * 128 + M, t0:t0 + tn], s_[0:M, 0:tn], reads=[s_.name], pwrites=['projT'])
                        cnt += 1
            for c0 in range(0, TMW, 512):
                cn = min(512, TMW - c0)
                w = wt[gi % 2]
                gi += 1
                segs = []
                for (d0, s0, n) in TM_SEGS:
                    lo, hi = max(c0, d0), min(c0 + cn, d0 + n)
                    if lo < hi:
                        segs.append((lo - c0, s0 + (lo - d0), hi - lo))
                load_w(w, segs, False)
                for b in range(NB):
                    p_ = ps[cnt % 4]
                    s_ = stg[cnt % 4]
                    hk = [hkeys[b] + 'a', hkeys[b] + 'b']
                    for kc in range(KC):
                        S.op('pe', lambda: nc.tensor.matmul(p_[:, 0:cn], lhsT=hT[:, kc, b * 128:(b + 1) * 128], rhs=w[:, kc, 0:cn],
                                                            start=(kc == 0), stop=(kc == KC - 1)),
                             reads=[w.name] + hk, writes=[p_.name])
                    if cnt % 2 == 0:
                        S.op('act', lambda: nc.scalar.copy(out=s_[:, 0:cn], in_=p_[:, 0:cn]), reads=[p_.name], writes=[s_.name])
                    else:
                        S.op('dve', lambda: nc.vector.tensor_copy(out=s_[:, 0:cn], in_=p_[:, 0:cn]), reads=[p_.name], writes=[s_.name])
                    S.dma('sp', self.projM[b * 128:(b + 1) * 128, c0:c0 + cn], s_[:, 0:cn], reads=[s_.name], pwrites=['projM'])
                    cnt += 1

    def phase_c(self, l):
        nc, S, NB, P = self.nc, self.S, self.NB, self.P
        last = (l == self.L - 1)
        with contextlib.ExitStack() as st:
            T = lambda n, s, d=F32: st.enter_context(nc.sbuf_tensor(self.key(n), s, d))
            wo = T("wo", [128, KC, D], BF16)
            for q in range(4):
                S.dma('pool', wo[:, 4 * q:4 * q + 4, :], self.w_out[l, 512 * q:512 * (q + 1), :].rearrange("(k p) c -> p k c", p=128),
                      writes=[wo.name + str(q)])
            wkeys = [wo.name + str(q) for q in range(4)]
            ob = [T("ob%d" % i, [128, D], BF16) for i in range(2)]
            hb = [T("hb%d" % i, [128, D]) for i in range(2)]
            oT = [T("oT%d" % i, [128, KC, 128], BF16) for i in range(2)]
            hnew = [T("hnew%d" % i, [128, D]) for i in range(2)]
            junk = T("junkc", [128, D], BF16)
            st_ = T("statc", [128, 3 * NB])
            pT = [st.enter_context(nc.psum_tensor(self.key("pTc%d" % i), [128, D], BF16)) for i in range(2)]
            ps = [st.enter_context(nc.psum_tensor(self.key("psC%d" % i), [128, 512], F32)) for i in range(4)]
            if last:
                fw = T("fw", [128, D])
                S.dma('sp', fw[:], self.final_norm_w[0:1, :].partition_broadcast(128), writes=[fw.name])
                yo = [T("yo%d" % i, [128, D]) for i in range(2)]
            for b in range(NB):
                if last and b == 0:
                    continue
                o_, h_, t_, n_, p_ = ob[b % 2], hb[b % 2], oT[b % 2], hnew[b % 2], pT[b % 2]
                S.dma('sp', o_[:], self.og[b * 128:(b + 1) * 128, :], reads=[self.ogk(b)], writes=[o_.name])
                S.dma('act', h_[:], self.hres[b * 128:(b + 1) * 128, :], reads=[self.hk(b)], writes=[h_.name])
                if b == 0:
                    S.op('pool', lambda: nc.gpsimd.memset(o_[0:112, :], 0.0), reads=[o_.name], writes=[o_.name])
                for mc in range(KC):
                    S.op('pe', lambda: nc.tensor.transpose(p_[:, mc * 128:(mc + 1) * 128], o_[:, mc * 128:(mc + 1) * 128], self.ident_b[:]),
                         reads=[o_.name, 'ident_b'], writes=[p_.name])
                pv = p_[:].rearrange("p (k t) -> p k t", t=128)
                S.op('act', lambda: nc.scalar.copy(out=t_[:, 0:8, :], in_=pv[:, 0:8, :]), reads=[p_.name], writes=[t_.name + 'a'])
                S.op('dve', lambda: nc.vector.tensor_copy(out=t_[:, 8:16, :], in_=pv[:, 8:16, :]), reads=[p_.name], writes=[t_.name + 'b'])
                for cg in range(4):
                    pp = ps[cg]
                    for mc in range(KC):
                        S.op('pe', lambda: nc.tensor.matmul(pp[:], lhsT=t_[:, mc, :], rhs=wo[:, mc, cg * 512:(cg + 1) * 512],
                                                            start=(mc == 0), stop=(mc == KC - 1)),
                             reads=[t_.name + 'a', t_.name + 'b'] + wkeys, writes=[pp.name])
                    S.op('dve', lambda: nc.vector.tensor_tensor(out=n_[:, cg * 512:(cg + 1) * 512], in0=pp[:], in1=h_[:, cg * 512:(cg + 1) * 512], op=ALU.add),
                         reads=[pp.name, h_.name], writes=[n_.name + str(cg)])
                nk = [n_.name + str(cg) for cg in range(4)]
                if not last:
                    if b == 0:
                        S.dma('sp', self.hres[112:128, :], n_[112:128, :], reads=nk, pwrites=[self.hk(0)])
                    else:
                        S.dma('sp', self.hres[b * 128:(b + 1) * 128, :], n_[:], reads=nk, writes=[self.hk(b)])
                else:
                    y_ = yo[b % 2]
                    ks = [self.key('stc') for _ in range(3)]
                    S.op('act', lambda: nc.scalar.activation(out=junk[:], in_=n_[:], func=AF.Square, accum_out=st_[:, 3 * b:3 * b + 1]),
                         reads=nk, writes=[junk.name, ks[0]])
                    S.op('act', lambda: nc.scalar.activation(out=st_[:, 3 * b + 1:3 * b + 2], in_=st_[:, 3 * b:3 * b + 1], func=AF.Ln,
                                                             scale=1.0 / D, bias=EPS), reads=[ks[0]], writes=[ks[1]])
                    S.op('act', lambda: nc.scalar.activation(out=st_[:, 3 * b + 2:3 * b + 3], in_=st_[:, 3 * b + 1:3 * b + 2], func=AF.Exp,
                                                             scale=-0.5), reads=[ks[1]], writes=[ks[2]])
                    S.op('dve', lambda: nc.vector.scalar_tensor_tensor(out=y_[:], in0=n_[:], scalar=st_[:, 3 * b + 2:3 * b + 3], in1=fw[:],
                                                                        op0=ALU.mult, op1=ALU.mult), reads=nk + [ks[2], fw.name], writes=[y_.name])
                    S.dma('sp', self.out[(b - 1) * 128:b * 128, :], y_[:], reads=[y_.name], pwrites=['out'])

    def phase_b(self, l):
        with contextlib.ExitStack() as st:
            gf, gg = self.mix_fox(l, st), self.mix_gla(l, st)
            alive = [gf, gg]
            while alive:
                for g_, n in ((gf, 3), (gg, 1)):
                    if g_ in alive:
                        for _ in range(n):
                            try:
                                next(g_)
                            except StopIteration:
                                alive.remove(g_)
                                break
        self.S.fence()
        self.mix_dn(l)
        self.S.fence()

    def silu_into(self, z, stage, src_cols, nwt=None):
        nc, S, NB = self.nc, self.S, self.NB
        src = self.projM[:, src_cols:src_cols + 128].rearrange("(n p) c -> p n c", p=128)
        if stage is not None:
            S.dma('act', stage, src, reads=['projM'], writes=[stage.name])
            S.op('act', lambda: nc.scalar.activation(out=z[:], in_=stage, func=AF.Silu), reads=[stage.name], writes=[z.name])
        else:
            S.dma('pool', z[:], src, reads=['projM'], writes=[z.name])
            S.op('act', lambda: nc.scalar.activation(out=z[:], in_=z[:], func=AF.Silu), reads=[z.name], writes=[z.name])
        if nwt is not None:
            S.op('pool', lambda: nc.gpsimd.tensor_tensor(out=z[:], in0=z[:], in1=nwt[:].unsqueeze(1).broadcast_to([128, NB, 128]), op=ALU.mult),
                 reads=[z.name, nwt.name], writes=[z.name])

    def mix_fox(self, l, ost):
        nc, S, NB, P = self.nc, self.S, self.NB, self.P
        QS = 128.0 ** -0.5
        with contextlib.ExitStack() as st:
            T = lambda n, s, d=F32: ost.enter_context(nc.sbuf_tensor(self.key(n), s, d))
            PS = lambda n, s, d=F32: ost.enter_context(nc.psum_tensor(self.key(n), s, d))
            bf = T("bf", [128, 5])
            ff = T("ff", [128, NB, 5])
            sp_ = T("sp", [128, NB, 5])
            tot = T("tot", [128, NB, 5])
            offs = T("offs", [128, NB + 1, 5])
            csc = T("csc", [128, NB, 5])
            S.dma('sp', bf[:], self.b_f[l:l + 1, :].partition_broadcast(128), writes=[bf.name])
            S.dma('sp', ff[:], self.projM[:, M_FF:M_FF + 5].rearrange("(n p) c -> p n c", p=128), reads=['projM'], writes=[ff.name])
            S.op('dve', lambda: nc.vector.tensor_tensor(out=ff[:], in0=ff[:], in1=bf[:].unsqueeze(1).broadcast_to([128, NB, 5]), op=ALU.add),
                 reads=[ff.name, bf.name], writes=[ff.name])
            S.op('act', lambda: nc.scalar.activation(out=sp_[:], in_=ff[:], func=AF.Exp, scale=-1.0), reads=[ff.name], writes=[sp_.name])
            S.op('act', lambda: nc.scalar.activation(out=sp_[:], in_=sp_[:], func=AF.Ln, bias=1.0), reads=[sp_.name], writes=[sp_.name])
            sb = [PS("sb%d" % i, [128, 512]) for i in range(2)]
            acc = [PS("acc%d" % i, [128, 512]) for i in range(2)]
            pc, pt_ = sb[0], sb[1]
            spf = sp_[:].rearrange("p n c -> p (n c)")
            S.op('pe', lambda: nc.tensor.matmul(pc[:, 0:NB * 5], lhsT=self.tri_f[:], rhs=spf, start=True, stop=True), reads=[sp_.name, 'tri_f'], writes=[pc.name])
            S.op('pe', lambda: nc.tensor.matmul(pt_[:, 0:NB * 5], lhsT=self.ones_f[:], rhs=spf, start=True, stop=True), reads=[sp_.name, 'ones_f'], writes=[pt_.name])
            S.op('act', lambda: nc.scalar.copy(out=tot[:].rearrange("p n c -> p (n c)"), in_=pt_[:, 0:NB * 5]), reads=[pt_.name], writes=[tot.name])
            S.op('dve', lambda: nc.vector.memset(offs[:, 0, :], 0.0), writes=[offs.name])
            for j in range(NB):
                S.op('dve', lambda: nc.vector.tensor_tensor(out=offs[:, j + 1, :], in0=offs[:, j, :], in1=tot[:, j, :], op=ALU.add),
                     reads=[offs.name, tot.name], writes=[offs.name])
            S.op('dve', lambda: nc.vector.tensor_tensor(out=csc[:].rearrange("p n c -> p (n c)"), in0=pc[:, 0:NB * 5],
                                                        in1=offs[:, 0:NB, :].rearrange("p n c -> p (n c)"), op=ALU.add),
                 reads=[pc.name, offs.name], writes=[csc.name])
            if getattr(self, 'fox_stage', 0) == 1:
                return
            yield
            qb = T("qb", [128, P], BF16)
            kb = T("kb", [128, P], BF16)
            vb = T("vb", [128, NB, 132], BF16)
            btab = T("btab", [128, NB, NB])
            ogh = T("ogh", [128, NB, 128], BF16)
            gate = T("gate", [128, NB, 128], BF16)
            gate_e = None
            rec = T("rec", [128, 2 * NB])
            pts = [T("pt%d" % i, [128, 4, 128], BF16) for i in range(2)]
            S.op('pool', lambda: nc.gpsimd.memset(vb[:, :, 128:129], 1.0), writes=[vb.name + 'one'])
            S.op('pool', lambda: nc.gpsimd.memset(vb[0:112, 0, 128:129], 0.0), reads=[vb.name + 'one'], writes=[vb.name + 'one'])
            cnt = 0
            for h in range(5):
                S.dma('pool', qb[:], self.projT[(CH_FQ + h) * 128:(CH_FQ + h + 1) * 128, :], reads=['projT'], writes=[qb.name])
                S.dma('pool', kb[:], self.projT[(CH_FK + h) * 128:(CH_FK + h + 1) * 128, :], reads=['projT'], writes=[kb.name])
                S.dma('pool', vb[:, :, 0:128], self.projM[:, M_FV + h * 128:M_FV + (h + 1) * 128].rearrange("(n p) c -> p n c", p=128), reads=['projM'], writes=[vb.name])
                self.silu_into(gate, gate_e, M_FZ + h * 128)
                if getattr(self, 'fox_stage', 0) == 2:
                    continue
                for i in range(NB):
                    S.op('dve', lambda: nc.vector.tensor_scalar(out=btab[:, i, 0:i + 1], in0=csc[:, 0:i + 1, h], scalar1=offs[:, i + 1, h:h + 1],
                                                                scalar2=None, op0=ALU.subtract), reads=[csc.name, offs.name], pwrites=[btab.name])
                if getattr(self, 'fox_stage', 0) == 3:
                    continue
                for i in range(getattr(self, 'fox_ni', NB)):
                    a_ = acc[i % 2]
                    for jb in range(0, i + 1, 4):
                        js = list(range(jb, min(jb + 4, i + 1)))
                        yield
                        s_ = sb[cnt % 2]
                        p_ = pts[cnt % 2]
                        cnt += 1
                        for j in js:
                            S.op('pe', lambda: nc.tensor.matmul(s_[:, (j - jb) * 128:(j - jb + 1) * 128], lhsT=kb[:, j * 128:(j + 1) * 128],
                                                                rhs=qb[:, i * 128:(i + 1) * 128], start=True, stop=True),
                                 reads=[kb.name, qb.name], writes=[s_.name])
                        for j in js:
                            S.op('act', lambda: nc.scalar.activation(out=p_[:, j - jb, :], in_=s_[:, (j - jb) * 128:(j - jb + 1) * 128], func=AF.Exp,
                                                                     bias=btab[:, i, j:j + 1], scale=QS),
                                 reads=[s_.name, btab.name], writes=[p_.name])
                        if i in js and getattr(self, 'fox_var', 0) != 2:
                            S.op('pool', lambda: nc.gpsimd.tensor_tensor(out=p_[:, i - jb, :], in0=p_[:, i - jb, :], in1=self.tri_b[:], op=ALU.mult),
                                 reads=[p_.name, 'tri_b'], writes=[p_.name])
                        for j in js:
                            NV = 128 if getattr(self, 'fox_var', 0) == 1 else 129
                            S.op('pe', lambda: nc.tensor.matmul(a_[:, 0:NV], lhsT=p_[:, j - jb, :], rhs=vb[:, j, 0:NV], start=(j == 0), stop=(j == i)),
                                 reads=[p_.name, vb.name, vb.name + 'one'], writes=[a_.name])
                    S.op('dve', lambda: nc.vector.tensor_scalar_max(out=rec[:, 2 * i:2 * i + 1], in0=a_[:, 128:129], scalar1=1e-30), reads=[a_.name], writes=[rec.name + 'a'])
                    S.op('dve', lambda: nc.vector.reciprocal(out=rec[:, 2 * i + 1:2 * i + 2], in_=rec[:, 2 * i:2 * i + 1]), reads=[rec.name + 'a'], writes=[rec.name + 'b'])
                    S.op('dve', lambda: nc.vector.scalar_tensor_tensor(out=ogh[:, i, :], in0=a_[:, 0:128], scalar=rec[:, 2 * i + 1:2 * i + 2], in1=gate[:, i, :],
                                                                       op0=ALU.mult, op1=ALU.mult), reads=[a_.name, rec.name + 'b', gate.name], pwrites=[ogh.name])
                c0 = O_FOX + h * 128
                S.dma('sp', self.og[:, c0:c0 + 128].rearrange("(n p) c -> p n c", p=128), ogh[:], reads=[ogh.name], pwrites=[self.ogk(b_) for b_ in range(NB)])

    def mix_gla(self, l, ost):
        nc, S, NB, P, NCH = self.nc, self.S, self.NB, self.P, self.NCH
        with contextlib.ExitStack() as st:
            T = lambda n, s, d=F32: ost.enter_context(nc.sbuf_tensor(self.key(n), s, d))
            PS = lambda n, s, d=F32: ost.enter_context(nc.psum_tensor(self.key(n), s, d))
            gl16 = T("gl16", [16, P])
            nwt = T("gnw", [128, 128])
            wg = T("wg", [16, 64])
            nb_ = T("gnb", [64, 2])
            csA = T("csA", [64, P])
            csB_full = T("csB", [128, P])
            csB = csB_full[0:64, :]
            eg_full = T("geg", [128, P])
            eg = eg_full[0:64, :]
            qd = T("gqd", [64, P], BF16)
            ki = T("gki", [64, P], BF16)
            kdT = T("gkdT", [64, P], BF16)
            cl = T("gcl", [64, NCH])
            egl = T("gegl", [64, NCH])
            kd_tm = T("gkdtm", [128, NB, 64], BF16)
            vf = eg_full[:].rearrange("p (n c) -> p n c", c=128)
            vb = T("gvb", [128, NB, 128], BF16)
            gate = T("ggate", [128, NB, 128], BF16)
            gate_e = vf
            at_all = T("gat", [128, NB, 128], BF16)
            Sst = T("gS", [64, 128])
            Sb = [T("gSb%d" % i, [64, 128], BF16) for i in range(2)]
            ogh = T("gogh", [128, NB, 128], BF16)
            stat = T("gstat", [128, 3 * NB])
            junk = T("gjunk", [128, 128], BF16)
            pz = PS("gpz", [128, 512])
            pat = pz
            pkd = PS("gpkd", [128, 1024], BF16)
            po = [PS("gpo", [128, 512])] * 2
            pds = [PS("gpds", [128, 512])] * 2
            S.dma('sp', gl16[:], self.projT[CH_SM * 128:CH_SM * 128 + 16, :], reads=['projT'], writes=[gl16.name])
            S.dma('sp', nwt[:], self.gla_norm_w[l:l + 1, :].partition_broadcast(128), writes=[nwt.name])
            tgs = [(t0, min(512, P - t0)) for t0 in range(0, P, 512)]
            csAv = csA[:].rearrange("p (n c) -> p n c", c=64)
            csBv = csB[:].rearrange("p (n c) -> p n c", c=64)
            for h in range(5):
                r0 = (CH_GQ + h // 2) * 128 + (h % 2) * 64
                S.dma('pool', qd[:], self.projT[r0:r0 + 64, :], reads=['projT'], writes=[qd.name])
                r0 = (CH_GK + h // 2) * 128 + (h % 2) * 64
                S.dma('pool', ki[:], self.projT[r0:r0 + 64, :], reads=['projT'], writes=[ki.name])
                S.dma('sp', wg[:], self.w_gk2[l, :, h * 64:(h + 1) * 64], writes=[wg.name])
                S.dma('sp', nb_[:, 0:1], self.b_gk[l:l + 1, h * 64:(h + 1) * 64].rearrange("o c -> c o"), writes=[nb_.name])
                S.op('dve', lambda: nc.vector.tensor_scalar(out=nb_[:, 1:2], in0=nb_[:, 0:1], scalar1=-1.0, scalar2=None, op0=ALU.mult),
                     reads=[nb_.name], writes=[nb_.name + 'n'])
                S.dma('sp', vf[:], self.projM[:, M_GV + h * 128:M_GV + (h + 1) * 128].rearrange("(n p) c -> p n c", p=128), reads=['projM'], writes=[vf.name])
                S.op('pool', lambda: nc.gpsimd.tensor_copy(out=vb[:], in_=vf[:]), reads=[vf.name], writes=[vb.name])
                self.silu_into(gate, gate_e, M_GZ + h * 128, nwt)
                yield
                for (t0, tn) in tgs:
                    yield
                    S.op('pe', lambda: nc.tensor.matmul(pz[0:64, 0:tn], lhsT=wg[:, :], rhs=gl16[:, t0:t0 + tn], start=True, stop=True),
                         reads=[wg.name, gl16.name], writes=[pz.name])
                    S.op('act', lambda: nc.scalar.activation(out=csA[:, t0:t0 + tn], in_=pz[0:64, 0:tn], func=AF.Exp, scale=-1.0, bias=nb_[:, 1:2]),
                         reads=[pz.name, nb_.name + 'n'], pwrites=[csA.name])
                S.op('act', lambda: nc.scalar.activation(out=csA[:], in_=csA[:], func=AF.Ln, bias=1.0), reads=[csA.name], writes=[csA.name])
                src, dst, srcv, dstv = csA, csB, csAv, csBv
                for sft in (1, 2, 4, 8, 16, 32):
                    yield
                    S.op('dve', lambda: nc.vector.tensor_tensor(out=dstv[:, :, sft:64], in0=srcv[:, :, sft:64], in1=srcv[:, :, 0:64 - sft], op=ALU.add),
                         reads=[src.name], writes=[dst.name])
                    S.op('pool', lambda: nc.gpsimd.tensor_copy(out=dstv[:, :, 0:sft], in_=srcv[:, :, 0:sft]), reads=[src.name], pwrites=[dst.name])
                    src, dst, srcv, dstv = dst, src, dstv, srcv
                assert src is csA
                yield
                S.op('pool', lambda: nc.gpsimd.tensor_copy(out=cl[:], in_=csAv[:, :, 63]), reads=[csA.name], writes=[cl.name])
                S.op('act', lambda: nc.scalar.activation(out=eg[:], in_=csA[:], func=AF.Exp, scale=-1.0 / 16), reads=[csA.name], writes=[eg.name])
                S.op('dve', lambda: nc.vector.scalar_tensor_tensor(out=qd[:], in0=qd[:], scalar=0.125, in1=eg[:], op0=ALU.mult, op1=ALU.mult),
                     reads=[qd.name, eg.name], writes=[qd.name])
                yield
                S.op('dve', lambda: nc.vector.tensor_tensor(out=csBv, in0=csAv, in1=cl[:].unsqueeze(2).broadcast_to([64, NCH, 64]), op=ALU.subtract),
                     reads=[csA.name, cl.name], writes=[csB.name])
                S.op('act', lambda: nc.scalar.activation(out=csB[:], in_=csB[:], func=AF.Exp, scale=1.0 / 16), reads=[csB.name], writes=[csB.name])
                S.op('dve', lambda: nc.vector.tensor_tensor(out=kdT[:], in0=ki[:], in1=csB[:], op=ALU.mult), reads=[ki.name, csB.name], writes=[kdT.name])
                yield
                S.op('act', lambda: nc.scalar.activation(out=eg[:], in_=csA[:], func=AF.Exp, scale=1.0 / 16), reads=[csA.name], writes=[eg.name])
                S.op('pool', lambda: nc.gpsimd.tensor_tensor(out=ki[:], in0=ki[:], in1=eg[:], op=ALU.mult), reads=[ki.name, eg.name], writes=[ki.name])
                S.op('act', lambda: nc.scalar.activation(out=egl[:], in_=cl[:], func=AF.Exp, scale=-1.0 / 16), reads=[cl.name], writes=[egl.name])
                for b0 in range(0, NB, 16):
                    yield
                    nb2 = min(16, NB - b0)
                    for b in range(b0, b0 + nb2):
                        S.op('pe', lambda: nc.tensor.transpose(pkd[:, (b - b0) * 64:(b - b0 + 1) * 64], kdT[:, b * 128:(b + 1) * 128], self.ident_b[0:64, 0:64]),
                             reads=[kdT.name, 'ident_b'], writes=[pkd.name])
                    S.op('act', lambda: nc.scalar.copy(out=kd_tm[:, b0:b0 + nb2, :], in_=pkd[:, 0:nb2 * 64].rearrange("p (n c) -> p n c", c=64)),
                         reads=[pkd.name], pwrites=[kd_tm.name])
                for b0 in range(0, NB, 4):
                    yield
                    nb2 = min(4, NB - b0)
                    for b in range(b0, b0 + nb2):
                        S.op('pe', lambda: nc.tensor.matmul(pat[:, (b - b0) * 128:(b - b0 + 1) * 128], lhsT=ki[:, b * 128:(b + 1) * 128],
                                                            rhs=qd[:, b * 128:(b + 1) * 128], start=True, stop=True),
                             reads=[ki.name, qd.name], writes=[pat.name])
                    S.op('dve', lambda: nc.vector.tensor_tensor(out=at_all[:, b0:b0 + nb2, :], in0=pat[:, 0:nb2 * 128].rearrange("p (n c) -> p n c", c=128),
                                                                in1=self.bd_b[:].unsqueeze(1).broadcast_to([128, nb2, 128]), op=ALU.mult),
                         reads=[pat.name, 'bd_b'], pwrites=[at_all.name])
                S.op('pool', lambda: nc.gpsimd.memset(Sst[:], 0.0), writes=[Sst.name])
                S.op('pool', lambda: nc.gpsimd.memset(Sb[0][:], 0.0), writes=[Sb[0].name])
                for b in range(NB):
                    yield
                    po_ = po[b % 2]
                    S.op('pe', lambda: nc.tensor.matmul(po_[:, 0:128], lhsT=at_all[:, b, :], rhs=vb[:, b, :], start=True, stop=False),
                         reads=[at_all.name, vb.name], writes=[po_.name])
                    for half in range(2):
                        n = 2 * b + half
                        r_ = slice(half * 64, half * 64 + 64)
                        cur, nxt = Sb[n % 2], Sb[(n + 1) % 2]
                        pd_ = pds[n % 2]
                        S.op('pe', lambda: nc.tensor.matmul(po_[r_, 0:128], lhsT=qd[:, n * 64:(n + 1) * 64], rhs=cur[:, :], start=False, stop=(half == 1)),
                             reads=[qd.name, cur.name], writes=[po_.name])
                        S.op('pe', lambda: nc.tensor.matmul(pd_[0:64, 0:128], lhsT=kd_tm[r_, b, :], rhs=vb[r_, b, :], start=True, stop=True),
                             reads=[kd_tm.name, vb.name], writes=[pd_.name])
                        S.op('dve', lambda: nc.vector.scalar_tensor_tensor(out=Sst[:], in0=Sst[:], scalar=egl[:, n:n + 1], in1=pd_[0:64, 0:128],
                                                                           op0=ALU.mult, op1=ALU.add), reads=[Sst.name, egl.name, pd_.name], writes=[Sst.name])
                        S.op('pool', lambda: nc.gpsimd.tensor_copy(out=nxt[:], in_=Sst[:]), reads=[Sst.name], writes=[nxt.name])
                    self.head_norm(po_, stat, junk, b, gate, ogh)
                c0 = O_GLA + h * 128
                S.dma('sp', self.og[:, c0:c0 + 128].rearrange("(n p) c -> p n c", p=128), ogh[:], reads=[ogh.name], pwrites=[self.ogk(b_) for b_ in range(NB)])

    def head_norm(self, po_, stat, junk, b, gate, ogh):
        nc, S = self.nc, self.S
        ks = [self.key('hn') for _ in range(3)]
        S.op('act', lambda: nc.scalar.activation(out=junk[:], in_=po_[:, 0:128], func=AF.Square, accum_out=stat[:, 3 * b:3 * b + 1]),
             reads=[po_.name], writes=[junk.name, ks[0]])
        S.op('act', lambda: nc.scalar.activation(out=stat[:, 3 * b + 1:3 * b + 2], in_=stat[:, 3 * b:3 * b + 1], func=AF.Ln, scale=1.0 / 128, bias=EPS),
             reads=[ks[0]], writes=[ks[1]])
        S.op('act', lambda: nc.scalar.activation(out=stat[:, 3 * b + 2:3 * b + 3], in_=stat[:, 3 * b + 1:3 * b + 2], func=AF.Exp, scale=-0.5),
             reads=[ks[1]], writes=[ks[2]])
        S.op('dve', lambda: nc.vector.scalar_tensor_tensor(out=ogh[:, b, :], in0=po_[:, 0:128], scalar=stat[:, 3 * b + 2:3 * b + 3], in1=gate[:, b, :],
                                                           op0=ALU.mult, op1=ALU.mult), reads=[po_.name, ks[2], gate.name], pwrites=[ogh.name])


    def mix_dn(self, l):
        nc, S, NB, P, NCH = self.nc, self.S, self.NB, self.P, self.NCH
        BIG = 30000.0
        QS = 128.0 ** -0.5
        with contextlib.ExitStack() as st:
            T = lambda n, s, d=F32: st.enter_context(nc.sbuf_tensor(self.key(n), s, d))
            PS = lambda n, s, d=F32: st.enter_context(nc.psum_tensor(self.key(n), s, d))
            g = nc.gpsimd
            mA = T("mA", [128, 128])
            mB = T("mB", [128, 128])
            bo_f = T("bo_f", [128, 128])
            ind0 = T("ind0", [128, 128])
            ind1 = T("ind1", [128, 128])
            sel = T("sel", [64, 6, 128])
            S.op('pool', lambda: g.memset(mA[:], 0.0), writes=[mA.name])
            S.op('pool', lambda: g.affine_select(out=mA[:], in_=mA[:], pattern=[[-1, 128]], compare_op=ALU.is_gt, fill=-BIG, base=0, channel_multiplier=1),
                 reads=[mA.name], writes=[mA.name])
            S.op('pool', lambda: g.memset(mA[64:128, 0:64], -BIG), reads=[mA.name], writes=[mA.name])
            S.op('pool', lambda: g.memset(mB[:], 0.0), writes=[mB.name])
            S.op('pool', lambda: g.affine_select(out=mB[:], in_=mB[:], pattern=[[1, 128]], compare_op=ALU.is_gt, fill=BIG, base=0, channel_multiplier=-1),
                 reads=[mB.name], writes=[mB.name])
            S.op('pool', lambda: g.memset(mB[0:64, 64:128], BIG), reads=[mB.name], writes=[mB.name])
            S.op('pool', lambda: g.memset(bo_f[:], 1.0), writes=[bo_f.name])
            S.op('pool', lambda: g.memset(bo_f[0:64, 64:128], 0.0), reads=[bo_f.name], writes=[bo_f.name])
            S.op('pool', lambda: g.memset(bo_f[64:128, 0:64], 0.0), reads=[bo_f.name], writes=[bo_f.name])
            S.op('pool', lambda: g.memset(ind0[:], 0.0), writes=[ind0.name])
            S.op('pool', lambda: g.memset(ind0[0:64, :], 1.0), reads=[ind0.name], writes=[ind0.name])
            S.op('pool', lambda: g.memset(ind1[:], 0.0), writes=[ind1.name])
            S.op('pool', lambda: g.memset(ind1[64:128, :], 1.0), reads=[ind1.name], writes=[ind1.name])
            S.op('pool', lambda: g.memset(sel[:], 1.0), writes=[sel.name])
            S.op('pool', lambda: g.affine_select(out=sel[0:32], in_=sel[0:32], pattern=[[1, 6], [0, 128]], compare_op=ALU.is_equal, fill=0.0, base=0,
                                                 channel_multiplier=-1), reads=[sel.name], writes=[sel.name])
            S.op('pool', lambda: g.affine_select(out=sel[32:64], in_=sel[32:64], pattern=[[1, 6], [0, 128]], compare_op=ALU.is_equal, fill=0.0, base=0,
                                                 channel_multiplier=-1), reads=[sel.name], writes=[sel.name])
            cw = T("cw", [128, 18, 4])
            for j in range(4):
                S.dma('sp', cw[:, :, j], self.conv_w[l, j:j + 1, :].rearrange("o (n c) -> c (o n)", c=128), pwrites=[cw.name],
                      allow_slow_non_contiguous=True)
            nwt = T("dnw", [128, 128])
            S.dma('sp', nwt[:], self.dn_norm_w[l:l + 1, :].partition_broadcast(128), writes=[nwt.name])
            colp = T("colp", [64, 4])
            S.dma('sp', colp[32:38, 0:1], self.a_log[l:l + 1, :].rearrange("o c -> c o"), pwrites=[colp.name])
            S.dma('sp', colp[32:38, 1:2], self.dt_bias[l:l + 1, :].rearrange("o c -> c o"), pwrites=[colp.name])
            S.op('act', lambda: nc.scalar.activation(out=colp[32:38, 2:3], in_=colp[32:38, 0:1], func=AF.Exp), reads=[colp.name], writes=[colp.name + 'A'])
            bcp = T("bcp", [128, 3, 6])
            S.dma('sp', bcp[:, 0, :], self.a_log[l:l + 1, :].partition_broadcast(128), pwrites=[bcp.name])
            S.dma('sp', bcp[:, 1, :], self.dt_bias[l:l + 1, :].partition_broadcast(128), pwrites=[bcp.name])
            S.op('act', lambda: nc.scalar.activation(out=bcp[:, 2, :], in_=bcp[:, 0, :], func=AF.Exp), reads=[bcp.name], writes=[bcp.name + 'A'])
            if getattr(self, 'dn_stage', 0) == 11:
                return
            xt = T("dx", [128, P])
            yt = T("dy", [128, P])
            rows = T("rows", [64, P])
            rb = rows[0:6, :]
            ra = rows[32:38, :]
            rt = xt[32:38, :]
            r0 = CH_SM * 128 + 32
            kb_, ka_ = rows.name + 'b', rows.name + 'a'
            S.dma('sp', rb, self.projT[r0:r0 + 6, :], reads=['projT'], writes=[kb_])
            S.dma('sp', ra, self.projT[r0 + 6:r0 + 12, :], reads=['projT'], writes=[ka_])
            S.op('act', lambda: nc.scalar.activation(out=rb, in_=rb, func=AF.Exp, scale=-1.0), reads=[kb_], writes=[kb_])
            S.op('pool', lambda: g.tensor_scalar_add(out=rb, in0=rb, scalar1=1.0), reads=[kb_], writes=[kb_])
            S.op('dve', lambda: nc.vector.reciprocal(out=rb, in_=rb), reads=[kb_], writes=[kb_])
            S.op('act', lambda: nc.scalar.activation(out=ra, in_=ra, func=AF.Exp, bias=colp[32:38, 1:2], scale=1.0), reads=[ka_, colp.name], writes=[ka_])
            S.op('act', lambda: nc.scalar.activation(out=ra, in_=ra, func=AF.Ln, bias=1.0), reads=[ka_], writes=[ka_])
            S.op('dve', lambda: nc.vector.tensor_scalar(out=ra, in0=ra, scalar1=colp[32:38, 2:3], scalar2=None, op0=ALU.mult),
                 reads=[ka_, colp.name + 'A'], writes=[ka_])
            if getattr(self, 'dn_stage', 0) == 12:
                return
            src, dst, sk, dk_ = ra, rt, ka_, xt.name
            for sft in (1, 2, 4, 8, 16, 32):
                sv_ = src.rearrange("p (n c) -> p n c", c=64)
                dv_ = dst.rearrange("p (n c) -> p n c", c=64)
                S.op('dve', lambda: nc.vector.tensor_tensor(out=dv_[:, :, sft:64], in0=sv_[:, :, sft:64], in1=sv_[:, :, 0:64 - sft], op=ALU.add),
                     reads=[sk], writes=[dk_])
                S.op('pool', lambda: g.tensor_copy(out=dv_[:, :, 0:sft], in_=sv_[:, :, 0:sft]), reads=[sk], pwrites=[dk_])
                src, dst, sk, dk_ = dst, src, dk_, sk
            assert sk == ka_
            if getattr(self, 'dn_stage', 0) == 13:
                return
            ps = [PS("dps%d" % i, [128, 512]) for i in range(7)]
            pTb = PS("dpTb", [128, 1024], BF16)
            dm = T("dm", [128, NB, 12])
            S.dma('sp', dm[:], self.projM[:, M_DB:M_DB + 12].rearrange("(n p) c -> p n c", p=128), reads=['projM'], writes=[dm.name])
            beta_tm = T("beta_tm", [128, NB, 6])
            lpos = T("lpos", [128, NB, 6])
            cs_tm = T("cs_tm", [128, NB, 6])
            negbeg = T("negbeg", [128, NB, 6])
            ekd = T("ekd", [128, NB, 6])
            egl = [T("egl%d" % i, [128, NB, 6]) for i in range(2)]
            if getattr(self, 'dn_stage', 0) == 14:
                return
            S.op('act', lambda: nc.scalar.activation(out=beta_tm[:], in_=dm[:, :, 0:6], func=AF.Exp, scale=-1.0), reads=[dm.name], writes=[beta_tm.name])
            S.op('pool', lambda: g.tensor_scalar_add(out=beta_tm[:], in0=beta_tm[:], scalar1=1.0), reads=[beta_tm.name], writes=[beta_tm.name])
            S.op('dve', lambda: nc.vector.reciprocal(out=beta_tm[:], in_=beta_tm[:]), reads=[beta_tm.name], writes=[beta_tm.name])
            S.op('dve', lambda: nc.vector.tensor_tensor(out=lpos[:], in0=dm[:, :, 6:12], in1=bcp[:, 1, :].unsqueeze(1).broadcast_to([128, NB, 6]), op=ALU.add),
                 reads=[dm.name, bcp.name], writes=[lpos.name])
            S.op('act', lambda: nc.scalar.activation(out=lpos[:], in_=lpos[:], func=AF.Exp), reads=[lpos.name], writes=[lpos.name])
            S.op('act', lambda: nc.scalar.activation(out=lpos[:], in_=lpos[:], func=AF.Ln, bias=1.0), reads=[lpos.name], writes=[lpos.name])
            S.op('dve', lambda: nc.vector.tensor_tensor(out=lpos[:], in0=lpos[:], in1=bcp[:, 2, :].unsqueeze(1).broadcast_to([128, NB, 6]), op=ALU.mult),
                 reads=[lpos.name, bcp.name + 'A'], writes=[lpos.name])
            if getattr(self, 'dn_stage', 0) == 15:
                return
            lpf = lpos[:].rearrange("p n c -> p (n c)")
            N6 = NB * 6
            for i, lt in enumerate((self.bd_f, bo_f, ind0, ind1)):
                S.op('pe', lambda: nc.tensor.matmul(ps[i][:, 0:N6], lhsT=lt[:], rhs=lpf, start=True, stop=True), reads=[lpos.name, lt.name], writes=[ps[i].name])
            if getattr(self, 'dn_stage', 0) == 16:
                return
            fl = lambda t: t[:].rearrange("p n c -> p (n c)")
            S.op('dve', lambda: nc.vector.tensor_copy(out=fl(cs_tm), in_=ps[0][:, 0:N6]), reads=[ps[0].name], writes=[cs_tm.name])
            S.op('act', lambda: nc.scalar.activation(out=fl(negbeg), in_=fl(cs_tm), func=AF.Exp, scale=-1.0), reads=[cs_tm.name], writes=[negbeg.name])
            S.op('dve', lambda: nc.vector.scalar_tensor_tensor(out=fl(negbeg), in0=fl(negbeg), scalar=-1.0, in1=fl(beta_tm), op0=ALU.mult, op1=ALU.mult),
                 reads=[negbeg.name, beta_tm.name], writes=[negbeg.name])
            if getattr(self, 'dn_stage', 0) == 17:
                return
            S.op('dve', lambda: nc.vector.tensor_tensor(out=fl(ekd), in0=fl(cs_tm), in1=ps[1][:, 0:N6], op=ALU.subtract), reads=[cs_tm.name, ps[1].name], writes=[ekd.name])
            S.op('act', lambda: nc.scalar.activation(out=fl(ekd), in_=fl(ekd), func=AF.Exp), reads=[ekd.name], writes=[ekd.name])
            if getattr(self, 'dn_stage', 0) == 18:
                return
            for i in range(2):
                S.op('act', lambda: nc.scalar.activation(out=fl(egl[i]), in_=ps[2 + i][:, 0:N6], func=AF.Exp, scale=-1.0), reads=[ps[2 + i].name], writes=[egl[i].name])
            if getattr(self, 'dn_stage', 0) == 19:
                return
            cs_bc = T("cs_bc", [128, P])
            bb = T("bb", [128, P])
            rn = T("rn", [128, 512])
            kT = T("kT", [128, P], BF16)
            qT = T("qT", [128, P], BF16)
            qg = T("qg", [128, P], BF16)
            Tt = T("Tt", [128, NB, 128], BF16)
            qkT = T("qkT", [128, NB, 128], BF16)
            kd_tm = T("dkd", [128, NB, 128], BF16)
            bv_tm = T("dbv", [128, NB, 128], BF16)
            gate = T("dgate", [128, NB, 128], BF16)
            ogh = T("dogh", [128, NB, 128], BF16)
            GB = getattr(self, 'dn_gb', 4)
            Ap = [T("Ap%d" % i, [128, GB * 128]) for i in range(2)]
            Bp = [T("Bp%d" % i, [128, GB * 128]) for i in range(2)]
            X = T("X", [128, GB * 128])
            G1 = T("G1", [128, GB * 128])
            G2 = T("G2", [128, GB * 128])
            G3 = T("G3", [128, GB * 128])
            Sst = T("dS", [128, 128])
            Sb = [T("dSb%d" % i, [128, 128], BF16) for i in range(2)]
            Rp = [T("Rp%d" % i, [128, 128], BF16) for i in range(2)]
            vn = [T("vn%d" % i, [128, 128], BF16) for i in range(2)]
            stat = T("dstat", [128, 3 * NB])
            junk = T("djunk", [128, 128], BF16)
            tgs = [(t0, min(512, P - t0)) for t0 in range(0, P, 512)]

            def conv_silu(ci):
                S.dma('sp', xt[:], self.projT[ci * 128:(ci + 1) * 128, :], reads=['projT'], writes=[xt.name])
                S.op('dve', lambda: nc.vector.tensor_scalar(out=yt[:], in0=xt[:], scalar1=cw[:, ci, 3:4], scalar2=None, op0=ALU.mult),
                     reads=[xt.name, cw.name], writes=[yt.name])
                for sft in (1, 2, 3):
                    S.op('dve', lambda: nc.vector.scalar_tensor_tensor(out=yt[:, sft:P], in0=xt[:, 0:P - sft], scalar=cw[:, ci, 3 - sft:4 - sft], in1=yt[:, sft:P],
                                                                       op0=ALU.mult, op1=ALU.add), reads=[xt.name, cw.name, yt.name], writes=[yt.name])
                S.op('act', lambda: nc.scalar.activation(out=yt[:], in_=yt[:], func=AF.Silu), reads=[yt.name], writes=[yt.name])

            def l2norm():
                S.op('act', lambda: nc.scalar.activation(out=xt[:], in_=yt[:], func=AF.Square), reads=[yt.name], writes=[xt.name])
                for gi, (t0, tn) in enumerate(tgs):
                    p_ = ps[gi % 2]
                    S.op('pe', lambda: nc.tensor.matmul(p_[:, 0:tn], lhsT=self.ones_f[:], rhs=xt[:, t0:t0 + tn], start=True, stop=True),
                         reads=[xt.name, 'ones_f'], writes=[p_.name])
                    S.op('act', lambda: nc.scalar.activation(out=rn[:, 0:tn], in_=p_[:, 0:tn], func=AF.Ln, bias=EPS), reads=[p_.name], writes=[rn.name])
                    S.op('act', lambda: nc.scalar.activation(out=rn[:, 0:tn], in_=rn[:, 0:tn], func=AF.Exp, scale=-0.5), reads=[rn.name], writes=[rn.name])
                    S.op('dve', lambda: nc.vector.tensor_tensor(out=yt[:, t0:t0 + tn], in0=yt[:, t0:t0 + tn], in1=rn[:, 0:tn], op=ALU.mult),
                         reads=[yt.name, rn.name], writes=[yt.name])

            def bcast_row(src_rows, skey, part0, dst):
                for gi, (t0, tn) in enumerate(tgs):
                    p_ = ps[gi % 2]
                    S.op('pe', lambda: nc.tensor.matmul(p_[:, 0:tn], lhsT=sel[part0:part0 + 6, h, :], rhs=src_rows[:, t0:t0 + tn], start=True, stop=True),
                         reads=[sel.name, skey], writes=[p_.name])
                    if gi % 2 == 0:
                        S.op('act', lambda: nc.scalar.copy(out=dst[:, t0:t0 + tn], in_=p_[:, 0:tn]), reads=[p_.name], pwrites=[dst.name])
                    else:
                        S.op('dve', lambda: nc.vector.tensor_copy(out=dst[:, t0:t0 + tn], in_=p_[:, 0:tn]), reads=[p_.name], pwrites=[dst.name])

            dn_stage = getattr(self, 'dn_stage', 0)
            if dn_stage == 1:
                return
            for h in range(getattr(self, 'dn_heads', 6)):
                bcast_row(ra, ka_, 32, cs_bc)
                S.op('act', lambda: nc.scalar.activation(out=bb[:], in_=cs_bc[:], func=AF.Exp, scale=-1.0), reads=[cs_bc.name], writes=[bb.name])
                conv_silu(h)
                l2norm()
                S.op('pool', lambda: g.tensor_scalar(out=qT[:], in0=yt[:], scalar1=QS, scalar2=None, op0=ALU.mult), reads=[yt.name], writes=[qT.name])
                S.op('dve', lambda: nc.vector.scalar_tensor_tensor(out=qg[:], in0=yt[:], scalar=QS, in1=bb[:], op0=ALU.mult, op1=ALU.mult),
                     reads=[yt.name, bb.name], writes=[qg.name])
                S.op('pool', lambda: g.memset(bb[:, 0:2], 0.0), reads=[bb.name], writes=[bb.name])
                bcast_row(rb, kb_, 0, bb)
                conv_silu(6 + h)
                l2norm()
                S.op('pool', lambda: g.tensor_copy(out=kT[:], in_=yt[:]), reads=[yt.name], writes=[kT.name])
                for b0 in range(0, NB, 8):
                    nb2 = min(8, NB - b0)
                    for b in range(b0, b0 + nb2):
                        S.op('pe', lambda: nc.tensor.transpose(pTb[:, (b - b0) * 128:(b - b0 + 1) * 128], kT[:, b * 128:(b + 1) * 128], self.ident_b[:]),
                             reads=[kT.name, 'ident_b'], writes=[pTb.name])
                    S.op('dve', lambda: nc.vector.tensor_tensor(out=kd_tm[:, b0:b0 + nb2, :], in0=pTb[:, 0:nb2 * 128].rearrange("p (n c) -> p n c", c=128),
                                                                in1=ekd[:, b0:b0 + nb2, h:h + 1].broadcast_to([128, nb2, 128]), op=ALU.mult),
                         reads=[pTb.name, ekd.name], pwrites=[kd_tm.name])
                conv_silu(12 + h)
                for b0 in range(0, NB, 4):
                    nb2 = min(4, NB - b0)
                    p_ = ps[4 + (b0 // 4) % 2]
                    for b in range(b0, b0 + nb2):
                        S.op('pe', lambda: nc.tensor.transpose(p_[:, (b - b0) * 128:(b - b0 + 1) * 128], yt[:, b * 128:(b + 1) * 128], self.ident_f[:]),
                             reads=[yt.name, 'ident_f'], writes=[p_.name])
                    S.op('dve', lambda: nc.vector.tensor_tensor(out=bv_tm[:, b0:b0 + nb2, :], in0=p_[:, 0:nb2 * 128].rearrange("p (n c) -> p n c", c=128),
                                                                in1=beta_tm[:, b0:b0 + nb2, h:h + 1].broadcast_to([128, nb2, 128]), op=ALU.mult),
                         reads=[p_.name, beta_tm.name], pwrites=[bv_tm.name])
                xv = xt[:].rearrange("p (n c) -> p n c", c=128)
                yv = yt[:].rearrange("p (n c) -> p n c", c=128)
                c0 = M_DZ + h * 128
                S.dma('act', xv, self.projM[:, c0:c0 + 128].rearrange("(n p) c -> p n c", p=128), reads=['projM'], writes=[xt.name])
                S.op('act', lambda: nc.scalar.activation(out=yt[:], in_=xt[:], func=AF.Silu), reads=[xt.name], writes=[yt.name])
                S.op('pool', lambda: g.tensor_tensor(out=gate[:], in0=yv, in1=nwt[:].unsqueeze(1).broadcast_to([128, NB, 128]), op=ALU.mult),
                     reads=[yt.name, nwt.name], writes=[gate.name])
                if dn_stage == 2:
                    continue
                for b0 in range(0, NB, GB):
                    nb = min(GB, NB - b0)
                    W = nb * 128
                    cols = slice(b0 * 128, b0 * 128 + W)
                    v3 = lambda t: t[:, 0:W].rearrange("p (n c) -> p n c", c=128)
                    bc3 = lambda t: t[:].unsqueeze(1).broadcast_to([128, nb, 128])
                    csb = cs_tm[:, b0:b0 + nb, h:h + 1].broadcast_to([128, nb, 128])
                    btb = beta_tm[:, b0:b0 + nb, h:h + 1].broadcast_to([128, nb, 128])
                    cs3 = cs_bc[:, cols].rearrange("p (n c) -> p n c", c=128)
                    S.op('dve', lambda: nc.vector.tensor_tensor(out=v3(G1), in0=cs3, in1=csb, op=ALU.subtract), reads=[cs_bc.name, cs_tm.name], writes=[G1.name])
                    S.op('pool', lambda: g.tensor_tensor(out=v3(G2), in0=v3(G1), in1=bc3(mB), op=ALU.add), reads=[G1.name, mB.name], writes=[G2.name])
                    S.op('dve', lambda: nc.vector.tensor_tensor(out=v3(G1), in0=v3(G1), in1=bc3(mA), op=ALU.add), reads=[G1.name, mA.name], writes=[G1.name])
                    S.op('act', lambda: nc.scalar.activation(out=G1[:, 0:W], in_=G1[:, 0:W], func=AF.Exp), reads=[G1.name], writes=[G1.name])
                    S.op('act', lambda: nc.scalar.activation(out=G2[:, 0:W], in_=G2[:, 0:W], func=AF.Exp, scale=-1.0), reads=[G2.name], writes=[G2.name])
                    for j in range(nb):
                        bc = slice((b0 + j) * 128, (b0 + j + 1) * 128)
                        S.op('pe', lambda: nc.tensor.matmul(ps[0][:, j * 128:(j + 1) * 128], lhsT=kT[:, bc], rhs=kT[:, bc], start=True, stop=True), reads=[kT.name], writes=[ps[0].name])
                        S.op('pe', lambda: nc.tensor.matmul(ps[2][:, j * 128:(j + 1) * 128], lhsT=kT[:, bc], rhs=qT[:, bc], start=True, stop=True), reads=[qT.name, kT.name], writes=[ps[2].name])
                    a_, b_ = Ap[0], Bp[0]
                    S.op('pool', lambda: g.tensor_tensor(out=v3(G1), in0=v3(G1), in1=btb, op=ALU.mult), reads=[G1.name, beta_tm.name], writes=[G1.name])
                    S.op('dve', lambda: nc.vector.tensor_tensor(out=a_[:, 0:W], in0=ps[0][:, 0:W], in1=G1[:, 0:W], op=ALU.mult), reads=[ps[0].name, G1.name], writes=[a_.name])
                    S.op('pool', lambda: g.tensor_tensor(out=G3[:, 0:W], in0=G2[:, 0:W], in1=bb[:, cols], op=ALU.mult), reads=[G2.name, bb.name], writes=[G3.name])
                    S.op('dve', lambda: nc.vector.tensor_tensor(out=b_[:, 0:W], in0=ps[0][:, 0:W], in1=G3[:, 0:W], op=ALU.mult), reads=[ps[0].name, G3.name], writes=[b_.name])
                    S.op('pool', lambda: g.tensor_tensor(out=v3(G2), in0=v3(G2), in1=bc3(self.ident_f), op=ALU.add), reads=[G2.name, 'ident_f'], writes=[G2.name])
                    S.op('dve', lambda: nc.vector.tensor_tensor(out=qkT[:, b0:b0 + nb, :], in0=v3(ps[2]), in1=v3(G2), op=ALU.mult), reads=[ps[2].name, G2.name], pwrites=[qkT.name])
                    S.op('pool', lambda: g.tensor_tensor(out=v3(X), in0=bc3(self.ident_f), in1=v3(b_), op=ALU.subtract), reads=[b_.name, 'ident_f'], writes=[X.name])
                    for lv in range(5):
                        a2, b2 = Ap[(lv + 1) % 2], Bp[(lv + 1) % 2]
                        for j in range(nb):
                            jc = slice(j * 128, (j + 1) * 128)
                            S.op('pe', lambda: nc.tensor.matmul(ps[3][:, jc], lhsT=b_[:, jc], rhs=a_[:, jc], start=True, stop=True), reads=[a_.name, b_.name], writes=[ps[3].name])
                            if lv < 4:
                                S.op('pe', lambda: nc.tensor.matmul(ps[4][:, jc], lhsT=a_[:, jc], rhs=b_[:, jc], start=True, stop=True), reads=[a_.name, b_.name], writes=[ps[4].name])
                        S.op('act', lambda: nc.scalar.copy(out=a2[:, 0:W], in_=ps[3][:, 0:W]), reads=[ps[3].name], writes=[a2.name])
                        if lv < 4:
                            S.op('dve', lambda: nc.vector.tensor_copy(out=b2[:, 0:W], in_=ps[4][:, 0:W]), reads=[ps[4].name], writes=[b2.name])
                        for j in range(nb):
                            jc = slice(j * 128, (j + 1) * 128)
                            S.op('pe', lambda: nc.tensor.matmul(ps[5][:, jc], lhsT=a2[:, jc], rhs=X[:, jc], start=True, stop=True), reads=[a2.name, X.name], writes=[ps[5].name])
                        S.op('dve', lambda: nc.vector.tensor_tensor(out=X[:, 0:W], in0=X[:, 0:W], in1=ps[5][:, 0:W], op=ALU.add), reads=[X.name, ps[5].name], writes=[X.name])
                        a_, b_ = a2, b2
                    S.op('pool', lambda: g.tensor_copy(out=Tt[:, b0:b0 + nb, :], in_=v3(X)), reads=[X.name], pwrites=[Tt.name])
                if dn_stage == 3:
                    continue
                S.op('pool', lambda: g.memset(Sst[:], 0.0), writes=[Sst.name])
                S.op('pool', lambda: g.memset(Sb[0][:], 0.0), writes=[Sb[0].name])
                for b in range(NB):
                    po_ = ps[3 + b % 2]
                    for half in range(2):
                        n = 2 * b + half
                        r_ = slice(half * 64, half * 64 + 64)
                        cur, nxt = Sb[n % 2], Sb[(n + 1) % 2]
                        R_, v_ = Rp[n % 2], vn[n % 2]
                        pk, pv, pd = ps[0], ps[1], ps[2]
                        S.op('pe', lambda: nc.tensor.matmul(pk[r_, 0:128], lhsT=kT[:, n * 64:(n + 1) * 64], rhs=cur[:], start=True, stop=True),
                             reads=[kT.name, cur.name], writes=[pk.name])
                        S.op('dve', lambda: nc.vector.scalar_tensor_tensor(out=R_[r_, :], in0=pk[r_, 0:128], scalar=negbeg[r_, b, h:h + 1], in1=bv_tm[r_, b, :],
                                                                           op0=ALU.mult, op1=ALU.add), reads=[pk.name, negbeg.name, bv_tm.name], writes=[R_.name])
                        S.op('pe', lambda: nc.tensor.matmul(pv[r_, 0:128], lhsT=Tt[r_, b, half * 64:half * 64 + 64], rhs=R_[r_, :], start=True, stop=True),
                             reads=[Tt.name, R_.name], writes=[pv.name])
                        S.op('act', lambda: nc.scalar.copy(out=v_[r_, :], in_=pv[r_, 0:128]), reads=[pv.name], writes=[v_.name])
                        S.op('pe', lambda: nc.tensor.matmul(po_[r_, 0:128], lhsT=qg[:, n * 64:(n + 1) * 64], rhs=cur[:], start=True, stop=False),
                             reads=[qg.name, cur.name], writes=[po_.name])
                        S.op('pe', lambda: nc.tensor.matmul(po_[r_, 0:128], lhsT=qkT[r_, b, half * 64:half * 64 + 64], rhs=v_[r_, :], start=False, stop=True),
                             reads=[qkT.name, v_.name], writes=[po_.name])
                        S.op('pe', lambda: nc.tensor.matmul(pd[:, 0:128], lhsT=kd_tm[r_, b, :], rhs=v_[r_, :], start=True, stop=True),
                             reads=[kd_tm.name, v_.name], writes=[pd.name])
                        S.op('dve', lambda: nc.vector.scalar_tensor_tensor(out=Sst[:], in0=Sst[:], scalar=egl[half][:, b, h:h + 1], in1=pd[:, 0:128],
                                                                           op0=ALU.mult, op1=ALU.add), reads=[Sst.name, egl[half].name, pd.name], writes=[Sst.name])
                        S.op('pool', lambda: g.tensor_copy(out=nxt[:], in_=Sst[:]), reads=[Sst.name], writes=[nxt.name])
                    self.head_norm(po_, stat, junk, b, gate, ogh)
                c0 = O_DN + h * 128
                S.dma('sp', self.og[:, c0:c0 + 128].rearrange("(n p) c -> p n c", p=128), ogh[:], reads=[ogh.name], pwrites=[self.ogk(b_) for b_ in range(NB)])


def kernel(x, meta_tokens, norm_w, w_in, conv_w, a_log, dt_bias, dn_norm_w, w_gk2, b_gk,
           gla_norm_w, b_f, w_out, final_norm_w):
    f = lambda a: np.ascontiguousarray(np.asarray(a, dtype=np.float32))
    x = f(x)
    B, S_LEN, _ = x.shape
    L = int(np.asarray(w_in).shape[0])
    prog = Prog(S_LEN, L)
    nc = prog.build()
    common = dict(meta_tokens=f(meta_tokens), norm_w=f(norm_w), w_in=f(w_in), conv_w=f(conv_w), a_log=f(a_log),
                  dt_bias=f(dt_bias), dn_norm_w=f(dn_norm_w), w_gk2=f(w_gk2), b_gk=f(b_gk), gla_norm_w=f(gla_norm_w),
                  b_f=f(b_f), w_out=f(w_out), final_norm_w=f(final_norm_w).reshape(1, -1))
    in_maps = [dict(common, x=np.ascontiguousarray(x[b])) for b in range(B)]
    res = run_bass_kernel_spmd(nc, in_maps, core_ids=list(range(B)))
    return np.stack([np.asarray(r["out"], dtype=np.float32) for r in res.results], 0)
```

```python
import contextlib
import numpy as np
import concourse.bass as bass
import concourse.mybir as mybir
from concourse.bass_utils import run_bass_kernel_spmd

F32 = mybir.dt.float32
BF16 = mybir.dt.bfloat16
AF = mybir.ActivationFunctionType
ALU = mybir.AluOpType

D = 2048
KC = 16
IN_COLS = 7585
EPS = 1e-6
C_DQ, C_DK, C_DV, C_DZ, C_DB, C_DA = 0, 768, 1536, 2304, 3072, 3078
C_GQ, C_GK, C_GV, C_GZ, C_GL = 3084, 3404, 3724, 4364, 5004
C_FQ, C_FK, C_FV, C_FZ, C_FF = 5020, 5660, 6300, 6940, 7580

FM_CHUNKS = []
for i in range(18):
    FM_CHUNKS.append((128, [(0, 128 * i, 128)]))
FM_CHUNKS.append((128, [(0, C_GQ, 128)]))
FM_CHUNKS.append((128, [(0, C_GQ + 128, 128)]))
FM_CHUNKS.append((64, [(0, C_GQ + 256, 64)]))
FM_CHUNKS.append((128, [(0, C_GK, 128)]))
FM_CHUNKS.append((128, [(0, C_GK + 128, 128)]))
FM_CHUNKS.append((64, [(0, C_GK + 256, 64)]))
FM_CHUNKS.append((64, [(0, C_GL, 16), (32, C_DB, 12)]))
for i in range(5):
    FM_CHUNKS.append((128, [(0, C_FQ + 128 * i, 128)]))
for i in range(5):
    FM_CHUNKS.append((128, [(0, C_FK + 128 * i, 128)]))
NFM = len(FM_CHUNKS)
FM_GROUPS = [[0, 1, 2, 3], [4, 5, 6, 7], [8, 9, 10, 11], [12, 13, 14, 15], [16, 17],
             [18, 19, 20, 21], [22, 23, 24], [25, 26, 27, 28], [29, 30, 31, 32], [33, 34]]
CH_GQ, CH_GK, CH_SM, CH_FQ, CH_FK = 18, 21, 24, 25, 30

TM_SEGS = [(0, C_DZ, 768), (768, C_GV, 1280), (2048, C_FV, 1280), (3328, C_DB, 12), (3340, C_FF, 5)]
TMW = 3345
M_DZ, M_GV, M_GZ, M_FV, M_FZ, M_DB, M_DA, M_FF = 0, 768, 1408, 2048, 2688, 3328, 3334, 3340
O_DN, O_GLA, O_FOX = 0, 768, 1408


class Sched:
    COMPUTE = ('pe', 'act', 'dve', 'pool')

    def __init__(self, nc, stack, n_dma_sems=20):
        self.nc = nc
        self.E = {'pe': nc.tensor, 'act': nc.scalar, 'dve': nc.vector, 'pool': nc.gpsimd, 'sp': nc.sync}
        self.psem = {e: stack.enter_context(nc.semaphore('prog_' + e)) for e in self.COMPUTE}
        self.cnt = {e: 0 for e in self.COMPUTE}
        self.queues = ('sp', 'act', 'pool')
        self.dsem = {q: [stack.enter_context(nc.semaphore('d%s%d' % (q, i))) for i in range(n_dma_sems)]
                     for q in self.queues}
        self.dval = {q: [0] * n_dma_sems for q in self.queues}
        self.dnext = {q: 0 for q in self.queues}
        self.waited = {}
        self.xw = {}
        self.pw = {}
        self.rd = {}
        self.n_wait = 0
        self.n_ins = 0

    def _wait(self, eng, src, val, same_ok):
        if src[0] == 'c':
            e = src[1]
            if e == eng and (same_ok or e == 'pe'):
                return
            sem = self.psem[e]
        else:
            sem = self.dsem[src[1]][src[2]]
        key = (eng, src)
        if self.waited.get(key, 0) >= val:
            return
        self.waited[key] = val
        self.E[eng].wait_ge(sem, val)
        self.n_wait += 1

    def _deps(self, eng, reads, writes, pwrites):
        for k in reads:
            for m in (self.xw.get(k), self.pw.get(k)):
                if m:
                    for src, v in m.items():
                        self._wait(eng, src, v, False)
        for k in writes:
            for m in (self.xw.get(k), self.pw.get(k)):
                if m:
                    for src, v in m.items():
                        self._wait(eng, src, v, False)
            m = self.rd.get(k)
            if m:
                for src, v in m.items():
                    self._wait(eng, src, v, True)
        for k in pwrites:
            m = self.xw.get(k)
            if m:
                for src, v in m.items():
                    self._wait(eng, src, v, False)
            m = self.rd.get(k)
            if m:
                for src, v in m.items():
                    self._wait(eng, src, v, True)

    def _record(self, src, val, reads, writes, pwrites):
        for k in reads:
            if k not in writes:
                self.rd.setdefault(k, {})[src] = val
        for k in writes:
            self.xw[k] = {src: val}
            self.pw[k] = {}
            self.rd[k] = {}
        for k in pwrites:
            self.pw.setdefault(k, {})[src] = val

    def op(self, eng, fn, reads=(), writes=(), pwrites=()):
        self._deps(eng, reads, writes, pwrites)
        ins = fn()
        self.cnt[eng] += 1
        ins.then_inc(self.psem[eng], 1)
        self._record(('c', eng), self.cnt[eng], reads, writes, pwrites)
        self.n_ins += 1
        return ins

    def dma(self, q, out, in_, reads=(), writes=(), pwrites=(), **kw):
        if q == 'act':
            q = 'sp'
        eng = q
        self._deps(eng, reads, writes, pwrites)
        i = self.dnext[q]
        self.dnext[q] = (i + 1) % len(self.dsem[q])
        if self.dval[q][i] > 0:
            self._wait(eng, ('d', q, i), self.dval[q][i], False)
        ins = self.E[eng].dma_start(out=out, in_=in_, **kw)
        self.dval[q][i] += 16
        ins.then_inc(self.dsem[q][i], 16)
        self._record(('d', q, i), self.dval[q][i], reads, writes, pwrites)
        self.n_ins += 1
        return ins

    def fence(self):
        for eng in ('pe', 'act', 'dve', 'pool', 'sp'):
            self.finish(eng)

    def finish(self, eng='sp'):
        for e in self.COMPUTE:
            if self.cnt[e] > 0:
                self._wait(eng, ('c', e), self.cnt[e], False)
        for q in self.queues:
            for i in range(len(self.dsem[q])):
                if self.dval[q][i] > 0:
                    self._wait(eng, ('d', q, i), self.dval[q][i], False)


class Prog:
    def __init__(self, S_LEN, L, debug=False, stop_after=None):
        self.S_LEN, self.L, self.debug = S_LEN, L, debug
        self.NB = S_LEN // 128 + 1
        self.P = self.NB * 128
        self.NCH = self.NB * 2
        self.stop_after = stop_after
        nc = self.nc = bass.Bass("TRN2", target_bir_lowering=False)
        P = self.P
        I = lambda n, s: nc.dram_tensor(n, s, F32, kind="ExternalInput").ap()
        self.x = I("x", [S_LEN, D])
        self.meta = I("meta_tokens", [16, D])
        self.norm_w = I("norm_w", [L, D])
        self.w_in = I("w_in", [L, D, IN_COLS])
        self.conv_w = I("conv_w", [L, 4, 2304])
        self.a_log = I("a_log", [L, 6])
        self.dt_bias = I("dt_bias", [L, 6])
        self.dn_norm_w = I("dn_norm_w", [L, 128])
        self.w_gk2 = I("w_gk2", [L, 16, 320])
        self.b_gk = I("b_gk", [L, 320])
        self.gla_norm_w = I("gla_norm_w", [L, 128])
        self.b_f = I("b_f", [L, 5])
        self.w_out = I("w_out", [L, D, D])
        self.final_norm_w = I("final_norm_w", [1, D])
        self.out = nc.dram_tensor("out", [S_LEN, D], F32, kind="ExternalOutput").ap()
        kind = "ExternalOutput" if debug else "Internal"
        self.hres = nc.dram_tensor("hres", [P, D], F32, kind=kind).ap()
        self.projT = nc.dram_tensor("projT", [NFM * 128, P], F32, kind=kind).ap()
        self.projM = nc.dram_tensor("projM", [P, TMW], F32, kind=kind).ap()
        self.og = nc.dram_tensor("og", [P, D], BF16, kind=kind).ap()
        if debug:
            self.hT_dbg = nc.dram_tensor("hT_dbg", [128, KC * P], BF16, kind="ExternalOutput").ap()
        self.uid = 0

    def key(self, s):
        self.uid += 1
        return '%s#%d' % (s, self.uid)

    def build(self):
        nc = self.nc
        with contextlib.ExitStack() as st:
            self.S = Sched(nc, st)
            self.consts(st)
            self.init_hres()
            for l in range(self.L):
                self.phase_a(l)
                self.S.fence()
                if self.stop_after == ('a', l):
                    break
                self.phase_b(l)
                if self.stop_after == ('b', l):
                    break
                self.phase_c(l)
                self.S.fence()
            self.S.finish()
        return nc

    def consts(self, st):
        nc, S = self.nc, self.S
        T = lambda n, s, d=F32: st.enter_context(nc.sbuf_tensor(n, s, d))
        self.ident_b = T("ident_b", [128, 128], BF16)
        self.ident_f = T("ident_f", [128, 128], F32)
        self.ones_f = T("ones_f", [128, 128], F32)
        self.tri_f = T("tri_f", [128, 128], F32)
        self.tri_b = T("tri_b", [128, 128], BF16)
        self.bd_b = T("bd_b", [128, 128], BF16)
        self.bd_f = T("bd_f", [128, 128], F32)
        self.zeros_f = T("zeros_f", [128, 2048], F32)
        g = nc.gpsimd
        for t in (self.ident_b, self.ident_f):
            S.op('pool', lambda: g.memset(t[:], 1.0), writes=[t.name])
            S.op('pool', lambda: g.affine_select(out=t[:], in_=t[:], pattern=[[-1, 128]], compare_op=ALU.is_equal,
                                                 fill=0.0, base=0, channel_multiplier=1), reads=[t.name], writes=[t.name])
        S.op('pool', lambda: g.memset(self.ones_f[:], 1.0), writes=['ones_f'])
        S.op('pool', lambda: g.memset(self.zeros_f[:], 0.0), writes=['zeros_f'])
        for t in (self.tri_f, self.tri_b, self.bd_f, self.bd_b):
            S.op('pool', lambda: g.memset(t[:], 1.0), writes=[t.name])
            S.op('pool', lambda: g.affine_select(out=t[:], in_=t[:], pattern=[[1, 128]], compare_op=ALU.is_ge,
                                                 fill=0.0, base=0, channel_multiplier=-1), reads=[t.name], writes=[t.name])
        for t in (self.bd_f, self.bd_b):
            S.op('pool', lambda: g.memset(t[0:64, 64:128], 0.0), reads=[t.name], writes=[t.name])

    def hk(self, b):
        return 'hres_%d' % b

    def ogk(self, b):
        return 'og_%d' % b

    def init_hres(self):
        S = self.S
        S.dma('sp', self.hres[0:112, :], self.zeros_f[0:112, :], reads=['zeros_f'], pwrites=[self.hk(0)])
        S.dma('sp', self.hres[112:128, :], self.meta[:, :], pwrites=[self.hk(0)])
        for b in range(1, self.NB):
            S.dma('sp' if b % 2 else 'act', self.hres[b * 128:(b + 1) * 128, :], self.x[(b - 1) * 128:b * 128, :], writes=[self.hk(b)])

    def phase_a(self, l):
        nc, S, NB, P = self.nc, self.S, self.NB, self.P
        with contextlib.ExitStack() as st:
            T = lambda n, s, d=F32: st.enter_context(nc.sbuf_tensor(self.key(n), s, d))
            hT = T("hT", [128, KC, P], BF16)
            kh = self.key('hT')
            hkeys = [kh + '_%d' % b for b in range(NB)]
            with contextlib.ExitStack() as st1:
                T1 = lambda n, s, d=F32: st1.enter_context(nc.sbuf_tensor(self.key(n), s, d))
                nw = T1("nw", [128, D])
                xb = [T1("xb%d" % i, [128, D]) for i in range(2)]
                hn = [T1("hn%d" % i, [128, D], BF16) for i in range(2)]
                junk = T1("junk", [128, D], BF16)
                st_ = T1("stat", [128, 3 * NB])
                pT = [st1.enter_context(nc.psum_tensor(self.key("pTa%d" % i), [128, D], BF16)) for i in range(2)]
                S.dma('sp', nw[:], self.norm_w[l:l + 1, :].partition_broadcast(128), writes=[nw.name])
                for b in range(NB):
                    x_ = xb[b % 2]
                    h_ = hn[b % 2]
                    p_ = pT[b % 2]
                    S.dma('sp' if b % 2 == 0 else 'act', x_[:], self.hres[b * 128:(b + 1) * 128, :], reads=[self.hk(b)], writes=[x_.name])
                    ks = [self.key('st') for _ in range(3)]
                    S.op('act', lambda: nc.scalar.activation(out=junk[:], in_=x_[:], func=AF.Square,
                                                             accum_out=st_[:, 3 * b:3 * b + 1]), reads=[x_.name], writes=[junk.name, ks[0]])
                    S.op('act', lambda: nc.scalar.activation(out=st_[:, 3 * b + 1:3 * b + 2], in_=st_[:, 3 * b:3 * b + 1], func=AF.Ln,
                                                             scale=1.0 / D, bias=EPS), reads=[ks[0]], writes=[ks[1]])
                    S.op('act', lambda: nc.scalar.activation(out=st_[:, 3 * b + 2:3 * b + 3], in_=st_[:, 3 * b + 1:3 * b + 2], func=AF.Exp,
                                                             scale=-0.5), reads=[ks[1]], writes=[ks[2]])
                    S.op('dve', lambda: nc.vector.scalar_tensor_tensor(out=h_[:], in0=x_[:], scalar=st_[:, 3 * b + 2:3 * b + 3], in1=nw[:],
                                                                       op0=ALU.mult, op1=ALU.mult), reads=[x_.name, ks[2], nw.name], writes=[h_.name])
                    for kc in range(KC):
                        S.op('pe', lambda: nc.tensor.transpose(p_[:, kc * 128:(kc + 1) * 128], h_[:, kc * 128:(kc + 1) * 128], self.ident_b[:]),
                             reads=[h_.name, 'ident_b'], writes=[p_.name])
                    pv = p_[:].rearrange("p (k t) -> p k t", t=128)
                    S.op('act', lambda: nc.scalar.copy(out=hT[:, 0:8, b * 128:(b + 1) * 128], in_=pv[:, 0:8, :]), reads=[p_.name], writes=[hkeys[b] + 'a'])
                    S.op('dve', lambda: nc.vector.tensor_copy(out=hT[:, 8:16, b * 128:(b + 1) * 128], in_=pv[:, 8:16, :]), reads=[p_.name], writes=[hkeys[b] + 'b'])
            S.fence()
            if self.debug:
                S.dma('sp', self.hT_dbg[:, :], hT[:].rearrange("p k t -> p (k t)"), reads=[k + s_ for k in hkeys for s_ in 'ab'], writes=['hT_dbg'])
            wt = [T("wt%d" % i, [128, KC, 512], BF16) for i in range(2)]
            stg = [T("stg%d" % i, [128, 512]) for i in range(4)]
            ps = [st.enter_context(nc.psum_tensor(self.key("psA%d" % i), [128, 512], F32)) for i in range(4)]
            wsrc = self.w_in
            gi = 0
            cnt = 0
            tgs = [(t0, min(512, P - t0)) for t0 in range(0, P, 512)]

            def load_w(w, segs, need_zero):
                S.op('pool', lambda: nc.gpsimd.memset(w[:, 0, 0:2] if not need_zero else w[:], 0.0), writes=[w.name])
                for (d0, s0, n) in segs:
                    S.dma('pool', w[:, :, d0:d0 + n], wsrc[l, :, s0:s0 + n].rearrange("(k p) c -> p k c", p=128), pwrites=[w.name])

            for grp in FM_GROUPS:
                w = wt[gi % 2]
                gi += 1
                segs = []
                need_zero = False
                for j, ci in enumerate(grp):
                    M, sg = FM_CHUNKS[ci]
                    tot = sum(n for _, _, n in sg)
                    if tot != M:
                        need_zero = True
                    for (d0, s0, n) in sg:
                        if segs and segs[-1][0] + segs[-1][2] == j * 128 + d0 and segs[-1][1] + segs[-1][2] == s0:
                            segs[-1] = (segs[-1][0], segs[-1][1], segs[-1][2] + n)
                        else:
                            segs.append((j * 128 + d0, s0, n))
                load_w(w, segs, need_zero)
                for (t0, tn) in tgs:
                    b0, b1 = t0 // 128, (t0 + tn) // 128
                    hk = [hkeys[b] + s for b in range(b0, b1) for s in 'ab']
                    for j, ci in enumerate(grp):
                        M = FM_CHUNKS[ci][0]
                        p_ = ps[cnt % 4]
                        s_ = stg[cnt % 4]
                        for kc in range(KC):
                            S.op('pe', lambda: nc.tensor.matmul(p_[0:M, 0:tn], lhsT=w[:, kc, j * 128:j * 128 + M], rhs=hT[:, kc, t0:t0 + tn],
                                                                start=(kc == 0), stop=(kc == KC - 1)),
                                 reads=[w.name] + hk, writes=[p_.name])
                        if cnt % 2 == 0:
                            S.op('act', lambda: nc.scalar.copy(out=s_[0:M, 0:tn], in_=p_[0:M, 0:tn]), reads=[p_.name], writes=[s_.name])
                        else:
                            S.op('dve', lambda: nc.vector.tensor_copy(out=s_[0:M, 0:tn], in_=p_[0:M, 0:tn]), reads=[p_.name], writes=[s_.name])
                        S.dma('sp', self.projT[ci * 128:ci * 128 + M, t0:t0 + tn], s_[0:M, 0:tn], reads=[s_.name], pwrites=['projT'])
                        cnt += 1
            for c0 in range(0, TMW, 512):
                cn = min(512, TMW - c0)
                w = wt[gi % 2]
                gi += 1
                segs = []
                for (d0, s0, n) in TM_SEGS:
                    lo, hi = max(c0, d0), min(c0 + cn, d0 + n)
                    if lo < hi:
                        segs.append((lo - c0, s0 + (lo - d0), hi - lo))
                load_w(w, segs, False)
                for b in range(NB):
                    p_ = ps[cnt % 4]
                    s_ = stg[cnt % 4]
                    hk = [hkeys[b] + 'a', hkeys[b] + 'b']
                    for kc in range(KC):
                        S.op('pe', lambda: nc.tensor.matmul(p_[:, 0:cn], lhsT=hT[:, kc, b * 128:(b + 1) * 128], rhs=w[:, kc, 0:cn],
                                                            start=(kc == 0), stop=(kc == KC - 1)),
                             reads=[w.name] + hk, writes=[p_.name])
                    if cnt % 2 == 0:
                        S.op('act', lambda: nc.scalar.copy(out=s_[:, 0:cn], in_=p_[:, 0:cn]), reads=[p_.name], writes=[s_.name])
                    else:
                        S.op('dve', lambda: nc.vector.tensor_copy(out=s_[:, 0:cn], in_=p_[:, 0:cn]), reads=[p_.name], writes=[s_.name])
                    S.dma('sp', self.projM[b * 128:(b + 1) * 128, c0:c0 + cn], s_[:, 0:cn], reads=[s_.name], pwrites=['projM'])
                    cnt += 1

    def phase_c(self, l):
        nc, S, NB, P = self.nc, self.S, self.NB, self.P
        last = (l == self.L - 1)
        with contextlib.ExitStack() as st:
            T = lambda n, s, d=F32: st.enter_context(nc.sbuf_tensor(self.key(n), s, d))
            wo = T("wo", [128, KC, D], BF16)
            for q in range(4):
                S.dma('pool', wo[:, 4 * q:4 * q + 4, :], self.w_out[l, 512 * q:512 * (q + 1), :].rearrange("(k p) c -> p k c", p=128),
                      writes=[wo.name + str(q)])
            wkeys = [wo.name + str(q) for q in range(4)]
            ob = [T("ob%d" % i, [128, D], BF16) for i in range(2)]
            hb = [T("hb%d" % i, [128, D]) for i in range(2)]
            oT = [T("oT%d" % i, [128, KC, 128], BF16) for i in range(2)]
            hnew = [T("hnew%d" % i, [128, D]) for i in range(2)]
            junk = T("junkc", [128, D], BF16)
            st_ = T("statc", [128, 3 * NB])
            pT = [st.enter_context(nc.psum_tensor(self.key("pTc%d" % i), [128, D], BF16)) for i in range(2)]
            ps = [st.enter_context(nc.psum_tensor(self.key("psC%d" % i), [128, 512], F32)) for i in range(4)]
            if last:
                fw = T("fw", [128, D])
                S.dma('sp', fw[:], self.final_norm_w[0:1, :].partition_broadcast(128), writes=[fw.name])
                yo = [T("yo%d" % i, [128, D]) for i in range(2)]
            for b in range(NB):
                if last and b == 0:
                    continue
                o_, h_, t_, n_, p_ = ob[b % 2], hb[b % 2], oT[b % 2], hnew[b % 2], pT[b % 2]
                S.dma('sp', o_[:], self.og[b * 128:(b + 1) * 128, :], reads=[self.ogk(b)], writes=[o_.name])
                S.dma('act', h_[:], self.hres[b * 128:(b + 1) * 128, :], reads=[self.hk(b)], writes=[h_.name])
                if b == 0:
                    S.op('pool', lambda: nc.gpsimd.memset(o_[0:112, :], 0.0), reads=[o_.name], writes=[o_.name])
                for mc in range(KC):
                    S.op('pe', lambda: nc.tensor.transpose(p_[:, mc * 128:(mc + 1) * 128], o_[:, mc * 128:(mc + 1) * 128], self.ident_b[:]),
                         reads=[o_.name, 'ident_b'], writes=[p_.name])
                pv = p_[:].rearrange("p (k t) -> p k t", t=128)
                S.op('act', lambda: nc.scalar.copy(out=t_[:, 0:8, :], in_=pv[:, 0:8, :]), reads=[p_.name], writes=[t_.name + 'a'])
                S.op('dve', lambda: nc.vector.tensor_copy(out=t_[:, 8:16, :], in_=pv[:, 8:16, :]), reads=[p_.name], writes=[t_.name + 'b'])
                for cg in range(4):
                    pp = ps[cg]
                    for mc in range(KC):
                        S.op('pe', lambda: nc.tensor.matmul(pp[:], lhsT=t_[:, mc, :], rhs=wo[:, mc, cg * 512:(cg + 1) * 512],
                                                            start=(mc == 0), stop=(mc == KC - 1)),
                             reads=[t_.name + 'a', t_.name + 'b'] + wkeys, writes=[pp.name])
                    S.op('dve', lambda: nc.vector.tensor_tensor(out=n_[:, cg * 512:(cg + 1) * 512], in0=pp[:], in1=h_[:, cg * 512:(cg + 1) * 512], op=ALU.add),
                         reads=[pp.name, h_.name], writes=[n_.name + str(cg)])
                nk = [n_.name + str(cg) for cg in range(4)]
                if not last:
                    if b == 0:
                        S.dma('sp', self.hres[112:128, :], n_[112:128, :], reads=nk, pwrites=[self.hk(0)])
                    else:
                        S.dma('sp', self.hres[b * 128:(b + 1) * 128, :], n_[:], reads=nk, writes=[self.hk(b)])
                else:
                    y_ = yo[b % 2]
                    ks = [self.key('stc') for _ in range(3)]
                    S.op('act', lambda: nc.scalar.activation(out=junk[:], in_=n_[:], func=AF.Square, accum_out=st_[:, 3 * b:3 * b + 1]),
                         reads=nk, writes=[junk.name, ks[0]])
                    S.op('act', lambda: nc.scalar.activation(out=st_[:, 3 * b + 1:3 * b + 2], in_=st_[:, 3 * b:3 * b + 1], func=AF.Ln,
                                                             scale=1.0 / D, bias=EPS), reads=[ks[0]], writes=[ks[1]])
                    S.op('act', lambda: nc.scalar.activation(out=st_[:, 3 * b + 2:3 * b + 3], in_=st_[:, 3 * b + 1:3 * b + 2], func=AF.Exp,
                                                             scale=-0.5), reads=[ks[1]], writes=[ks[2]])
                    S.op('dve', lambda: nc.vector.scalar_tensor_tensor(out=y_[:], in0=n_[:], scalar=st_[:, 3 * b + 2:3 * b + 3], in1=fw[:],
                                                                        op0=ALU.mult, op1=ALU.mult), reads=nk + [ks[2], fw.name], writes=[y_.name])
                    S.dma('sp', self.out[(b - 1) * 128:b * 128, :], y_[:], reads=[y_.name], pwrites=['out'])

    def phase_b(self, l):
        with contextlib.ExitStack() as st:
            gf, gg = self.mix_fox(l, st), self.mix_gla(l, st)
            alive = [gf, gg]
            while alive:
                for g_, n in ((gf, 3), (gg, 1)):
                    if g_ in alive:
                        for _ in range(n):
                            try:
                                next(g_)
                            except StopIteration:
                                alive.remove(g_)
                                break
        self.S.fence()
        self.mix_dn(l)
        self.S.fence()

    def silu_into(self, z, stage, src_cols, nwt=None):
        nc, S, NB = self.nc, self.S, self.NB
        src = self.projM[:, src_cols:src_cols + 128].rearrange("(n p) c -> p n c", p=128)
        if stage is not None:
            S.dma('act', stage, src, reads=['projM'], writes=[stage.name])
            S.op('act', lambda: nc.scalar.activation(out=z[:], in_=stage, func=AF.Silu), reads=[stage.name], writes=[z.name])
        else:
            S.dma('pool', z[:], src, reads=['projM'], writes=[z.name])
            S.op('act', lambda: nc.scalar.activation(out=z[:], in_=z[:], func=AF.Silu), reads=[z.name], writes=[z.name])
        if nwt is not None:
            S.op('pool', lambda: nc.gpsimd.tensor_tensor(out=z[:], in0=z[:], in1=nwt[:].unsqueeze(1).broadcast_to([128, NB, 128]), op=ALU.mult),
                 reads=[z.name, nwt.name], writes=[z.name])

    def mix_fox(self, l, ost):
        nc, S, NB, P = self.nc, self.S, self.NB, self.P
        QS = 128.0 ** -0.5
        with contextlib.ExitStack() as st:
            T = lambda n, s, d=F32: ost.enter_context(nc.sbuf_tensor(self.key(n), s, d))
            PS = lambda n, s, d=F32: ost.enter_context(nc.psum_tensor(self.key(n), s, d))
            bf = T("bf", [128, 5])
            ff = T("ff", [128, NB, 5])
            sp_ = T("sp", [128, NB, 5])
            tot = T("tot", [128, NB, 5])
            offs = T("offs", [128, NB + 1, 5])
            csc = T("csc", [128, NB, 5])
            S.dma('sp', bf[:], self.b_f[l:l + 1, :].partition_broadcast(128), writes=[bf.name])
            S.dma('sp', ff[:], self.projM[:, M_FF:M_FF + 5].rearrange("(n p) c -> p n c", p=128), reads=['projM'], writes=[ff.name])
            S.op('dve', lambda: nc.vector.tensor_tensor(out=ff[:], in0=ff[:], in1=bf[:].unsqueeze(1).broadcast_to([128, NB, 5]), op=ALU.add),
                 reads=[ff.name, bf.name], writes=[ff.name])
            S.op('act', lambda: nc.scalar.activation(out=sp_[:], in_=ff[:], func=AF.Exp, scale=-1.0), reads=[ff.name], writes=[sp_.name])
            S.op('act', lambda: nc.scalar.activation(out=sp_[:], in_=sp_[:], func=AF.Ln, bias=1.0), reads=[sp_.name], writes=[sp_.name])
            sb = [PS("sb%d" % i, [128, 512]) for i in range(2)]
            acc = [PS("acc%d" % i, [128, 512]) for i in range(2)]
            pc, pt_ = sb[0], sb[1]
            spf = sp_[:].rearrange("p n c -> p (n c)")
            S.op('pe', lambda: nc.tensor.matmul(pc[:, 0:NB * 5], lhsT=self.tri_f[:], rhs=spf, start=True, stop=True), reads=[sp_.name, 'tri_f'], writes=[pc.name])
            S.op('pe', lambda: nc.tensor.matmul(pt_[:, 0:NB * 5], lhsT=self.ones_f[:], rhs=spf, start=True, stop=True), reads=[sp_.name, 'ones_f'], writes=[pt_.name])
            S.op('act', lambda: nc.scalar.copy(out=tot[:].rearrange("p n c -> p (n c)"), in_=pt_[:, 0:NB * 5]), reads=[pt_.name], writes=[tot.name])
            S.op('dve', lambda: nc.vector.memset(offs[:, 0, :], 0.0), writes=[offs.name])
            for j in range(NB):
                S.op('dve', lambda: nc.vector.tensor_tensor(out=offs[:, j + 1, :], in0=offs[:, j, :], in1=tot[:, j, :], op=ALU.add),
                     reads=[offs.name, tot.name], writes=[offs.name])
            S.op('dve', lambda: nc.vector.tensor_tensor(out=csc[:].rearrange("p n c -> p (n c)"), in0=pc[:, 0:NB * 5],
                                                        in1=offs[:, 0:NB, :].rearrange("p n c -> p (n c)"), op=ALU.add),
                 reads=[pc.name, offs.name], writes=[csc.name])
            if getattr(self, 'fox_stage', 0) == 1:
                return
            yield
            qb = T("qb", [128, P], BF16)
            kb = T("kb", [128, P], BF16)
            vb = T("vb", [128, NB, 132], BF16)
            btab = T("btab", [128, NB, NB])
            ogh = T("ogh", [128, NB, 128], BF16)
            gate = T("gate", [128, NB, 128], BF16)
            gate_e = None
            rec = T("rec", [128, 2 * NB])
            pts = [T("pt%d" % i, [128, 4, 128], BF16) for i in range(2)]
            S.op('pool', lambda: nc.gpsimd.memset(vb[:, :, 128:129], 1.0), writes=[vb.name + 'one'])
            S.op('pool', lambda: nc.gpsimd.memset(vb[0:112, 0, 128:129], 0.0), reads=[vb.name + 'one'], writes=[vb.name + 'one'])
            cnt = 0
            for h in range(5):
                S.dma('pool', qb[:], self.projT[(CH_FQ + h) * 128:(CH_FQ + h + 1) * 128, :], reads=['projT'], writes=[qb.name])
                S.dma('pool', kb[:], self.projT[(CH_FK + h) * 128:(CH_FK + h + 1) * 128, :], reads=['projT'], writes=[kb.name])
                S.dma('pool', vb[:, :, 0:128], self.projM[:, M_FV + h * 128:M_FV + (h + 1) * 128].rearrange("(n p) c -> p n c", p=128), reads=['projM'], writes=[vb.name])
                self.silu_into(gate, gate_e, M_FZ + h * 128)
                if getattr(self, 'fox_stage', 0) == 2:
                    continue
                for i in range(NB):
                    S.op('dve', lambda: nc.vector.tensor_scalar(out=btab[:, i, 0:i + 1], in0=csc[:, 0:i + 1, h], scalar1=offs[:, i + 1, h:h + 1],
                                                                scalar2=None, op0=ALU.subtract), reads=[csc.name, offs.name], pwrites=[btab.name])
                if getattr(self, 'fox_stage', 0) == 3:
                    continue
                groups = [(i, jb, list(range(jb, min(jb + 4, i + 1)))) for i in range(NB) for jb in range(0, i + 1, 4)]

                def emit_qk(gx):
                    i, jb, js = groups[gx]
                    s_ = sb[gx % 2]
                    for j in js:
                        S.op('pe', lambda: nc.tensor.matmul(s_[:, (j - jb) * 128:(j - jb + 1) * 128], lhsT=kb[:, j * 128:(j + 1) * 128],
                                                            rhs=qb[:, i * 128:(i + 1) * 128], start=True, stop=True),
                             reads=[kb.name, qb.name], writes=[s_.name])

                emit_qk(0)
                for gx, (i, jb, js) in enumerate(groups):
                    yield
                    if gx + 1 < len(groups):
                        emit_qk(gx + 1)
                    a_ = acc[i % 2]
                    s_ = sb[gx % 2]
                    p_ = pts[gx % 2]
                    for j in js:
                        S.op('act', lambda: nc.scalar.activation(out=p_[:, j - jb, :], in_=s_[:, (j - jb) * 128:(j - jb + 1) * 128], func=AF.Exp,
                                                                 bias=btab[:, i, j:j + 1], scale=QS),
                             reads=[s_.name, btab.name], writes=[p_.name])
                    if i in js:
                        S.op('pool', lambda: nc.gpsimd.tensor_tensor(out=p_[:, i - jb, :], in0=p_[:, i - jb, :], in1=self.tri_b[:], op=ALU.mult),
                             reads=[p_.name, 'tri_b'], writes=[p_.name])
                    for j in js:
                        S.op('pe', lambda: nc.tensor.matmul(a_[:, 0:129], lhsT=p_[:, j - jb, :], rhs=vb[:, j, 0:129], start=(j == 0), stop=(j == i)),
                             reads=[p_.name, vb.name, vb.name + 'one'], writes=[a_.name])
                    if js[-1] == i:
                        S.op('dve', lambda: nc.vector.tensor_scalar_max(out=rec[:, 2 * i:2 * i + 1], in0=a_[:, 128:129], scalar1=1e-30), reads=[a_.name], writes=[rec.name + 'a'])
                        S.op('dve', lambda: nc.vector.reciprocal(out=rec[:, 2 * i + 1:2 * i + 2], in_=rec[:, 2 * i:2 * i + 1]), reads=[rec.name + 'a'], writes=[rec.name + 'b'])
                        S.op('dve', lambda: nc.vector.scalar_tensor_tensor(out=ogh[:, i, :], in0=a_[:, 0:128], scalar=rec[:, 2 * i + 1:2 * i + 2], in1=gate[:, i, :],
                                                                           op0=ALU.mult, op1=ALU.mult), reads=[a_.name, rec.name + 'b', gate.name], pwrites=[ogh.name])
                c0 = O_FOX + h * 128
                S.dma('sp', self.og[:, c0:c0 + 128].rearrange("(n p) c -> p n c", p=128), ogh[:], reads=[ogh.name], pwrites=[self.ogk(b_) for b_ in range(NB)])

    def mix_gla(self, l, ost):
        nc, S, NB, P, NCH = self.nc, self.S, self.NB, self.P, self.NCH
        with contextlib.ExitStack() as st:
            T = lambda n, s, d=F32: ost.enter_context(nc.sbuf_tensor(self.key(n), s, d))
            PS = lambda n, s, d=F32: ost.enter_context(nc.psum_tensor(self.key(n), s, d))
            gl16 = T("gl16", [16, P])
            nwt = T("gnw", [128, 128])
            wg = T("wg", [16, 64])
            nb_ = T("gnb", [64, 2])
            csA = T("csA", [64, P])
            csB_full = T("csB", [128, P])
            csB = csB_full[0:64, :]
            eg_full = T("geg", [128, P])
            eg = eg_full[0:64, :]
            qd = T("gqd", [64, P], BF16)
            ki = T("gki", [64, P], BF16)
            kdT = T("gkdT", [64, P], BF16)
            cl = T("gcl", [64, NCH])
            egl = T("gegl", [64, NCH])
            kd_tm = T("gkdtm", [128, NB, 64], BF16)
            vf = eg_full[:].rearrange("p (n c) -> p n c", c=128)
            vb = T("gvb", [128, NB, 128], BF16)
            gate = T("ggate", [128, NB, 128], BF16)
            gate_e = vf
            at_all = T("gat", [128, NB, 128], BF16)
            Sst = T("gS", [64, 128])
            Sb = [T("gSb%d" % i, [64, 128], BF16) for i in range(2)]
            ogh = T("gogh", [128, NB, 128], BF16)
            stat = T("gstat", [128, 3 * NB])
            junk = T("gjunk", [128, 128], BF16)
            pz = PS("gpz", [128, 512])
            pat = pz
            pkd = PS("gpkd", [128, 1024], BF16)
            po = [PS("gpo", [128, 512])] * 2
            pds = [PS("gpds", [128, 512])] * 2
            S.dma('sp', gl16[:], self.projT[CH_SM * 128:CH_SM * 128 + 16, :], reads=['projT'], writes=[gl16.name])
            S.dma('sp', nwt[:], self.gla_norm_w[l:l + 1, :].partition_broadcast(128), writes=[nwt.name])
            tgs = [(t0, min(512, P - t0)) for t0 in range(0, P, 512)]
            csAv = csA[:].rearrange("p (n c) -> p n c", c=64)
            csBv = csB[:].rearrange("p (n c) -> p n c", c=64)
            for h in range(5):
                r0 = (CH_GQ + h // 2) * 128 + (h % 2) * 64
                S.dma('pool', qd[:], self.projT[r0:r0 + 64, :], reads=['projT'], writes=[qd.name])
                r0 = (CH_GK + h // 2) * 128 + (h % 2) * 64
                S.dma('pool', ki[:], self.projT[r0:r0 + 64, :], reads=['projT'], writes=[ki.name])
                S.dma('sp', wg[:], self.w_gk2[l, :, h * 64:(h + 1) * 64], writes=[wg.name])
                S.dma('sp', nb_[:, 0:1], self.b_gk[l:l + 1, h * 64:(h + 1) * 64].rearrange("o c -> c o"), writes=[nb_.name])
                S.op('dve', lambda: nc.vector.tensor_scalar(out=nb_[:, 1:2], in0=nb_[:, 0:1], scalar1=-1.0, scalar2=None, op0=ALU.mult),
                     reads=[nb_.name], writes=[nb_.name + 'n'])
                S.dma('sp', vf[:], self.projM[:, M_GV + h * 128:M_GV + (h + 1) * 128].rearrange("(n p) c -> p n c", p=128), reads=['projM'], writes=[vf.name])
                S.op('pool', lambda: nc.gpsimd.tensor_copy(out=vb[:], in_=vf[:]), reads=[vf.name], writes=[vb.name])
                self.silu_into(gate, gate_e, M_GZ + h * 128, nwt)
                yield
                for (t0, tn) in tgs:
                    yield
                    S.op('pe', lambda: nc.tensor.matmul(pz[0:64, 0:tn], lhsT=wg[:, :], rhs=gl16[:, t0:t0 + tn], start=True, stop=True),
                         reads=[wg.name, gl16.name], writes=[pz.name])
                    S.op('act', lambda: nc.scalar.activation(out=csA[:, t0:t0 + tn], in_=pz[0:64, 0:tn], func=AF.Exp, scale=-1.0, bias=nb_[:, 1:2]),
                         reads=[pz.name, nb_.name + 'n'], pwrites=[csA.name])
                S.op('act', lambda: nc.scalar.activation(out=csA[:], in_=csA[:], func=AF.Ln, bias=1.0), reads=[csA.name], writes=[csA.name])
                src, dst, srcv, dstv = csA, csB, csAv, csBv
                for sft in (1, 2, 4, 8, 16, 32):
                    yield
                    S.op('dve', lambda: nc.vector.tensor_tensor(out=dstv[:, :, sft:64], in0=srcv[:, :, sft:64], in1=srcv[:, :, 0:64 - sft], op=ALU.add),
                         reads=[src.name], writes=[dst.name])
                    S.op('pool', lambda: nc.gpsimd.tensor_copy(out=dstv[:, :, 0:sft], in_=srcv[:, :, 0:sft]), reads=[src.name], pwrites=[dst.name])
                    src, dst, srcv, dstv = dst, src, dstv, srcv
                assert src is csA
                yield
                S.op('pool', lambda: nc.gpsimd.tensor_copy(out=cl[:], in_=csAv[:, :, 63]), reads=[csA.name], writes=[cl.name])
                S.op('act', lambda: nc.scalar.activation(out=eg[:], in_=csA[:], func=AF.Exp, scale=-1.0 / 16), reads=[csA.name], writes=[eg.name])
                S.op('dve', lambda: nc.vector.scalar_tensor_tensor(out=qd[:], in0=qd[:], scalar=0.125, in1=eg[:], op0=ALU.mult, op1=ALU.mult),
                     reads=[qd.name, eg.name], writes=[qd.name])
                yield
                S.op('dve', lambda: nc.vector.tensor_tensor(out=csBv, in0=csAv, in1=cl[:].unsqueeze(2).broadcast_to([64, NCH, 64]), op=ALU.subtract),
                     reads=[csA.name, cl.name], writes=[csB.name])
                S.op('act', lambda: nc.scalar.activation(out=csB[:], in_=csB[:], func=AF.Exp, scale=1.0 / 16), reads=[csB.name], writes=[csB.name])
                S.op('dve', lambda: nc.vector.tensor_tensor(out=kdT[:], in0=ki[:], in1=csB[:], op=ALU.mult), reads=[ki.name, csB.name], writes=[kdT.name])
                yield
                S.op('act', lambda: nc.scalar.activation(out=eg[:], in_=csA[:], func=AF.Exp, scale=1.0 / 16), reads=[csA.name], writes=[eg.name])
                S.op('pool', lambda: nc.gpsimd.tensor_tensor(out=ki[:], in0=ki[:], in1=eg[:], op=ALU.mult), reads=[ki.name, eg.name], writes=[ki.name])
                S.op('act', lambda: nc.scalar.activation(out=egl[:], in_=cl[:], func=AF.Exp, scale=-1.0 / 16), reads=[cl.name], writes=[egl.name])
                for b0 in range(0, NB, 16):
                    yield
                    nb2 = min(16, NB - b0)
                    for b in range(b0, b0 + nb2):
                        S.op('pe', lambda: nc.tensor.transpose(pkd[:, (b - b0) * 64:(b - b0 + 1) * 64], kdT[:, b * 128:(b + 1) * 128], self.ident_b[0:64, 0:64]),
                             reads=[kdT.name, 'ident_b'], writes=[pkd.name])
                    S.op('act', lambda: nc.scalar.copy(out=kd_tm[:, b0:b0 + nb2, :], in_=pkd[:, 0:nb2 * 64].rearrange("p (n c) -> p n c", c=64)),
                         reads=[pkd.name], pwrites=[kd_tm.name])
                for b0 in range(0, NB, 4):
                    yield
                    nb2 = min(4, NB - b0)
                    for b in range(b0, b0 + nb2):
                        S.op('pe', lambda: nc.tensor.matmul(pat[:, (b - b0) * 128:(b - b0 + 1) * 128], lhsT=ki[:, b * 128:(b + 1) * 128],
                                                            rhs=qd[:, b * 128:(b + 1) * 128], start=True, stop=True),
                             reads=[ki.name, qd.name], writes=[pat.name])
                    S.op('dve', lambda: nc.vector.tensor_tensor(out=at_all[:, b0:b0 + nb2, :], in0=pat[:, 0:nb2 * 128].rearrange("p (n c) -> p n c", c=128),
                                                                in1=self.bd_b[:].unsqueeze(1).broadcast_to([128, nb2, 128]), op=ALU.mult),
                         reads=[pat.name, 'bd_b'], pwrites=[at_all.name])
                S.op('pool', lambda: nc.gpsimd.memset(Sst[:], 0.0), writes=[Sst.name])
                S.op('pool', lambda: nc.gpsimd.memset(Sb[0][:], 0.0), writes=[Sb[0].name])
                for b in range(NB):
                    yield
                    po_ = po[b % 2]
                    S.op('pe', lambda: nc.tensor.matmul(po_[:, 0:128], lhsT=at_all[:, b, :], rhs=vb[:, b, :], start=True, stop=False),
                         reads=[at_all.name, vb.name], writes=[po_.name])
                    for half in range(2):
                        n = 2 * b + half
                        r_ = slice(half * 64, half * 64 + 64)
                        cur, nxt = Sb[n % 2], Sb[(n + 1) % 2]
                        pd_ = pds[n % 2]
                        S.op('pe', lambda: nc.tensor.matmul(po_[r_, 0:128], lhsT=qd[:, n * 64:(n + 1) * 64], rhs=cur[:, :], start=False, stop=(half == 1)),
                             reads=[qd.name, cur.name], writes=[po_.name])
                        S.op('pe', lambda: nc.tensor.matmul(pd_[0:64, 0:128], lhsT=kd_tm[r_, b, :], rhs=vb[r_, b, :], start=True, stop=True),
                             reads=[kd_tm.name, vb.name], writes=[pd_.name])
                        S.op('dve', lambda: nc.vector.scalar_tensor_tensor(out=Sst[:], in0=Sst[:], scalar=egl[:, n:n + 1], in1=pd_[0:64, 0:128],
                                                                           op0=ALU.mult, op1=ALU.add), reads=[Sst.name, egl.name, pd_.name], writes=[Sst.name])
                        S.op('pool', lambda: nc.gpsimd.tensor_copy(out=nxt[:], in_=Sst[:]), reads=[Sst.name], writes=[nxt.name])
                    self.head_norm(po_, stat, junk, b, gate, ogh)
                c0 = O_GLA + h * 128
                S.dma('sp', self.og[:, c0:c0 + 128].rearrange("(n p) c -> p n c", p=128), ogh[:], reads=[ogh.name], pwrites=[self.ogk(b_) for b_ in range(NB)])

    def head_norm(self, po_, stat, junk, b, gate, ogh):
        nc, S = self.nc, self.S
        ks = [self.key('hn') for _ in range(3)]
        S.op('act', lambda: nc.scalar.activation(out=junk[:], in_=po_[:, 0:128], func=AF.Square, accum_out=stat[:, 3 * b:3 * b + 1]),
             reads=[po_.name], writes=[junk.name, ks[0]])
        S.op('act', lambda: nc.scalar.activation(out=stat[:, 3 * b + 1:3 * b + 2], in_=stat[:, 3 * b:3 * b + 1], func=AF.Ln, scale=1.0 / 128, bias=EPS),
             reads=[ks[0]], writes=[ks[1]])
        S.op('act', lambda: nc.scalar.activation(out=stat[:, 3 * b + 2:3 * b + 3], in_=stat[:, 3 * b + 1:3 * b + 2], func=AF.Exp, scale=-0.5),
             reads=[ks[1]], writes=[ks[2]])
        S.op('dve', lambda: nc.vector.scalar_tensor_tensor(out=ogh[:, b, :], in0=po_[:, 0:128], scalar=stat[:, 3 * b + 2:3 * b + 3], in1=gate[:, b, :],
                                                           op0=ALU.mult, op1=ALU.mult), reads=[po_.name, ks[2], gate.name], pwrites=[ogh.name])


    def mix_dn(self, l):
        nc, S, NB, P, NCH = self.nc, self.S, self.NB, self.P, self.NCH
        BIG = 30000.0
        QS = 128.0 ** -0.5
        with contextlib.ExitStack() as st:
            T = lambda n, s, d=F32: st.enter_context(nc.sbuf_tensor(self.key(n), s, d))
            PS = lambda n, s, d=F32: st.enter_context(nc.psum_tensor(self.key(n), s, d))
            g = nc.gpsimd
            mA = T("mA", [128, 128])
            mB = T("mB", [128, 128])
            bo_f = T("bo_f", [128, 128])
            ind0 = T("ind0", [128, 128])
            ind1 = T("ind1", [128, 128])
            sel = T("sel", [64, 6, 128])
            S.op('pool', lambda: g.memset(mA[:], 0.0), writes=[mA.name])
            S.op('pool', lambda: g.affine_select(out=mA[:], in_=mA[:], pattern=[[-1, 128]], compare_op=ALU.is_gt, fill=-BIG, base=0, channel_multiplier=1),
                 reads=[mA.name], writes=[mA.name])
            S.op('pool', lambda: g.memset(mA[64:128, 0:64], -BIG), reads=[mA.name], writes=[mA.name])
            S.op('pool', lambda: g.memset(mB[:], 0.0), writes=[mB.name])
            S.op('pool', lambda: g.affine_select(out=mB[:], in_=mB[:], pattern=[[1, 128]], compare_op=ALU.is_gt, fill=BIG, base=0, channel_multiplier=-1),
                 reads=[mB.name], writes=[mB.name])
            S.op('pool', lambda: g.memset(mB[0:64, 64:128], BIG), reads=[mB.name], writes=[mB.name])
            S.op('pool', lambda: g.memset(bo_f[:], 1.0), writes=[bo_f.name])
            S.op('pool', lambda: g.memset(bo_f[0:64, 64:128], 0.0), reads=[bo_f.name], writes=[bo_f.name])
            S.op('pool', lambda: g.memset(bo_f[64:128, 0:64], 0.0), reads=[bo_f.name], writes=[bo_f.name])
            S.op('pool', lambda: g.memset(ind0[:], 0.0), writes=[ind0.name])
            S.op('pool', lambda: g.memset(ind0[0:64, :], 1.0), reads=[ind0.name], writes=[ind0.name])
            S.op('pool', lambda: g.memset(ind1[:], 0.0), writes=[ind1.name])
            S.op('pool', lambda: g.memset(ind1[64:128, :], 1.0), reads=[ind1.name], writes=[ind1.name])
            S.op('pool', lambda: g.memset(sel[:], 1.0), writes=[sel.name])
            S.op('pool', lambda: g.affine_select(out=sel[0:32], in_=sel[0:32], pattern=[[1, 6], [0, 128]], compare_op=ALU.is_equal, fill=0.0, base=0,
                                                 channel_multiplier=-1), reads=[sel.name], writes=[sel.name])
            S.op('pool', lambda: g.affine_select(out=sel[32:64], in_=sel[32:64], pattern=[[1, 6], [0, 128]], compare_op=ALU.is_equal, fill=0.0, base=0,
                                                 channel_multiplier=-1), reads=[sel.name], writes=[sel.name])
            cw = T("cw", [128, 18, 4])
            for j in range(4):
                S.dma('sp', cw[:, :, j], self.conv_w[l, j:j + 1, :].rearrange("o (n c) -> c (o n)", c=128), pwrites=[cw.name],
                      allow_slow_non_contiguous=True)
            nwt = T("dnw", [128, 128])
            S.dma('sp', nwt[:], self.dn_norm_w[l:l + 1, :].partition_broadcast(128), writes=[nwt.name])
            colp = T("colp", [64, 4])
            S.dma('sp', colp[32:38, 0:1], self.a_log[l:l + 1, :].rearrange("o c -> c o"), pwrites=[colp.name])
            S.dma('sp', colp[32:38, 1:2], self.dt_bias[l:l + 1, :].rearrange("o c -> c o"), pwrites=[colp.name])
            S.op('act', lambda: nc.scalar.activation(out=colp[32:38, 2:3], in_=colp[32:38, 0:1], func=AF.Exp), reads=[colp.name], writes=[colp.name + 'A'])
            bcp = T("bcp", [128, 3, 6])
            S.dma('sp', bcp[:, 0, :], self.a_log[l:l + 1, :].partition_broadcast(128), pwrites=[bcp.name])
            S.dma('sp', bcp[:, 1, :], self.dt_bias[l:l + 1, :].partition_broadcast(128), pwrites=[bcp.name])
            S.op('act', lambda: nc.scalar.activation(out=bcp[:, 2, :], in_=bcp[:, 0, :], func=AF.Exp), reads=[bcp.name], writes=[bcp.name + 'A'])
            if getattr(self, 'dn_stage', 0) == 11:
                return
            xt = T("dx", [128, P])
            yt = T("dy", [128, P])
            rows = T("rows", [64, P])
            rb = rows[0:6, :]
            ra = rows[32:38, :]
            rt = xt[32:38, :]
            r0 = CH_SM * 128 + 32
            kb_, ka_ = rows.name + 'b', rows.name + 'a'
            S.dma('sp', rb, self.projT[r0:r0 + 6, :], reads=['projT'], writes=[kb_])
            S.dma('sp', ra, self.projT[r0 + 6:r0 + 12, :], reads=['projT'], writes=[ka_])
            S.op('act', lambda: nc.scalar.activation(out=rb, in_=rb, func=AF.Exp, scale=-1.0), reads=[kb_], writes=[kb_])
            S.op('pool', lambda: g.tensor_scalar_add(out=rb, in0=rb, scalar1=1.0), reads=[kb_], writes=[kb_])
            S.op('dve', lambda: nc.vector.reciprocal(out=rb, in_=rb), reads=[kb_], writes=[kb_])
            S.op('act', lambda: nc.scalar.activation(out=ra, in_=ra, func=AF.Exp, bias=colp[32:38, 1:2], scale=1.0), reads=[ka_, colp.name], writes=[ka_])
            S.op('act', lambda: nc.scalar.activation(out=ra, in_=ra, func=AF.Ln, bias=1.0), reads=[ka_], writes=[ka_])
            S.op('dve', lambda: nc.vector.tensor_scalar(out=ra, in0=ra, scalar1=colp[32:38, 2:3], scalar2=None, op0=ALU.mult),
                 reads=[ka_, colp.name + 'A'], writes=[ka_])
            if getattr(self, 'dn_stage', 0) == 12:
                return
            src, dst, sk, dk_ = ra, rt, ka_, xt.name
            for sft in (1, 2, 4, 8, 16, 32):
                sv_ = src.rearrange("p (n c) -> p n c", c=64)
                dv_ = dst.rearrange("p (n c) -> p n c", c=64)
                S.op('dve', lambda: nc.vector.tensor_tensor(out=dv_[:, :, sft:64], in0=sv_[:, :, sft:64], in1=sv_[:, :, 0:64 - sft], op=ALU.add),
                     reads=[sk], writes=[dk_])
                S.op('pool', lambda: g.tensor_copy(out=dv_[:, :, 0:sft], in_=sv_[:, :, 0:sft]), reads=[sk], pwrites=[dk_])
                src, dst, sk, dk_ = dst, src, dk_, sk
            assert sk == ka_
            if getattr(self, 'dn_stage', 0) == 13:
                return
            ps = [PS("dps%d" % i, [128, 512]) for i in range(7)]
            pTb = PS("dpTb", [128, 1024], BF16)
            dm = T("dm", [128, NB, 12])
            S.dma('sp', dm[:], self.projM[:, M_DB:M_DB + 12].rearrange("(n p) c -> p n c", p=128), reads=['projM'], writes=[dm.name])
            beta_tm = T("beta_tm", [128, NB, 6])
            lpos = T("lpos", [128, NB, 6])
            cs_tm = T("cs_tm", [128, NB, 6])
            negbeg = T("negbeg", [128, NB, 6])
            ekd = T("ekd", [128, NB, 6])
            egl = [T("egl%d" % i, [128, NB, 6]) for i in range(2)]
            if getattr(self, 'dn_stage', 0) == 14:
                return
            S.op('act', lambda: nc.scalar.activation(out=beta_tm[:], in_=dm[:, :, 0:6], func=AF.Exp, scale=-1.0), reads=[dm.name], writes=[beta_tm.name])
            S.op('pool', lambda: g.tensor_scalar_add(out=beta_tm[:], in0=beta_tm[:], scalar1=1.0), reads=[beta_tm.name], writes=[beta_tm.name])
            S.op('dve', lambda: nc.vector.reciprocal(out=beta_tm[:], in_=beta_tm[:]), reads=[beta_tm.name], writes=[beta_tm.name])
            S.op('dve', lambda: nc.vector.tensor_tensor(out=lpos[:], in0=dm[:, :, 6:12], in1=bcp[:, 1, :].unsqueeze(1).broadcast_to([128, NB, 6]), op=ALU.add),
                 reads=[dm.name, bcp.name], writes=[lpos.name])
            S.op('act', lambda: nc.scalar.activation(out=lpos[:], in_=lpos[:], func=AF.Exp), reads=[lpos.name], writes=[lpos.name])
            S.op('act', lambda: nc.scalar.activation(out=lpos[:], in_=lpos[:], func=AF.Ln, bias=1.0), reads=[lpos.name], writes=[lpos.name])
            S.op('dve', lambda: nc.vector.tensor_tensor(out=lpos[:], in0=lpos[:], in1=bcp[:, 2, :].unsqueeze(1).broadcast_to([128, NB, 6]), op=ALU.mult),
                 reads=[lpos.name, bcp.name + 'A'], writes=[lpos.name])
            if getattr(self, 'dn_stage', 0) == 15:
                return
            lpf = lpos[:].rearrange("p n c -> p (n c)")
            N6 = NB * 6
            for i, lt in enumerate((self.bd_f, bo_f, ind0, ind1)):
                S.op('pe', lambda: nc.tensor.matmul(ps[i][:, 0:N6], lhsT=lt[:], rhs=lpf, start=True, stop=True), reads=[lpos.name, lt.name], writes=[ps[i].name])
            if getattr(self, 'dn_stage', 0) == 16:
                return
            fl = lambda t: t[:].rearrange("p n c -> p (n c)")
            S.op('dve', lambda: nc.vector.tensor_copy(out=fl(cs_tm), in_=ps[0][:, 0:N6]), reads=[ps[0].name], writes=[cs_tm.name])
            S.op('act', lambda: nc.scalar.activation(out=fl(negbeg), in_=fl(cs_tm), func=AF.Exp, scale=-1.0), reads=[cs_tm.name], writes=[negbeg.name])
            S.op('dve', lambda: nc.vector.scalar_tensor_tensor(out=fl(negbeg), in0=fl(negbeg), scalar=-1.0, in1=fl(beta_tm), op0=ALU.mult, op1=ALU.mult),
                 reads=[negbeg.name, beta_tm.name], writes=[negbeg.name])
            if getattr(self, 'dn_stage', 0) == 17:
                return
            S.op('dve', lambda: nc.vector.tensor_tensor(out=fl(ekd), in0=fl(cs_tm), in1=ps[1][:, 0:N6], op=ALU.subtract), reads=[cs_tm.name, ps[1].name], writes=[ekd.name])
            S.op('act', lambda: nc.scalar.activation(out=fl(ekd), in_=fl(ekd), func=AF.Exp), reads=[ekd.name], writes=[ekd.name])
            if getattr(self, 'dn_stage', 0) == 18:
                return
            for i in range(2):
                S.op('act', lambda: nc.scalar.activation(out=fl(egl[i]), in_=ps[2 + i][:, 0:N6], func=AF.Exp, scale=-1.0), reads=[ps[2 + i].name], writes=[egl[i].name])
            if getattr(self, 'dn_stage', 0) == 19:
                return
            cs_bc = T("cs_bc", [128, P])
            bb = T("bb", [128, P])
            rn = T("rn", [128, 512])
            kT = T("kT", [128, P], BF16)
            qT = T("qT", [128, P], BF16)
            qg = T("qg", [128, P], BF16)
            Tt = T("Tt", [128, NB, 128], BF16)
            qkT = T("qkT", [128, NB, 128], BF16)
            kd_tm = T("dkd", [128, NB, 128], BF16)
            bv_tm = T("dbv", [128, NB, 128], BF16)
            gate = T("dgate", [128, NB, 128], BF16)
            ogh = T("dogh", [128, NB, 128], BF16)
            GB = getattr(self, 'dn_gb', 4)
            Ap = [T("Ap%d" % i, [128, GB * 128]) for i in range(2)]
            Bp = [T("Bp%d" % i, [128, GB * 128]) for i in range(2)]
            X = T("X", [128, GB * 128])
            G1 = T("G1", [128, GB * 128])
            G2 = T("G2", [128, GB * 128])
            G3 = T("G3", [128, GB * 128])
            Sst = T("dS", [128, 128])
            Sb = [T("dSb%d" % i, [128, 128], BF16) for i in range(2)]
            Rp = [T("Rp%d" % i, [128, 128], BF16) for i in range(2)]
            vn = [T("vn%d" % i, [128, 128], BF16) for i in range(2)]
            stat = T("dstat", [128, 3 * NB])
            junk = T("djunk", [128, 128], BF16)
            tgs = [(t0, min(512, P - t0)) for t0 in range(0, P, 512)]

            def conv_silu(ci):
                S.dma('sp', xt[:], self.projT[ci * 128:(ci + 1) * 128, :], reads=['projT'], writes=[xt.name])
                S.op('dve', lambda: nc.vector.tensor_scalar(out=yt[:], in0=xt[:], scalar1=cw[:, ci, 3:4], scalar2=None, op0=ALU.mult),
                     reads=[xt.name, cw.name], writes=[yt.name])
                for sft in (1, 2, 3):
                    S.op('dve', lambda: nc.vector.scalar_tensor_tensor(out=yt[:, sft:P], in0=xt[:, 0:P - sft], scalar=cw[:, ci, 3 - sft:4 - sft], in1=yt[:, sft:P],
                                                                       op0=ALU.mult, op1=ALU.add), reads=[xt.name, cw.name, yt.name], writes=[yt.name])
                S.op('act', lambda: nc.scalar.activation(out=yt[:], in_=yt[:], func=AF.Silu), reads=[yt.name], writes=[yt.name])

            def l2norm():
                S.op('act', lambda: nc.scalar.activation(out=xt[:], in_=yt[:], func=AF.Square), reads=[yt.name], writes=[xt.name])
                for gi, (t0, tn) in enumerate(tgs):
                    p_ = ps[gi % 2]
                    S.op('pe', lambda: nc.tensor.matmul(p_[:, 0:tn], lhsT=self.ones_f[:], rhs=xt[:, t0:t0 + tn], start=True, stop=True),
                         reads=[xt.name, 'ones_f'], writes=[p_.name])
                    S.op('act', lambda: nc.scalar.activation(out=rn[:, 0:tn], in_=p_[:, 0:tn], func=AF.Ln, bias=EPS), reads=[p_.name], writes=[rn.name])
                    S.op('act', lambda: nc.scalar.activation(out=rn[:, 0:tn], in_=rn[:, 0:tn], func=AF.Exp, scale=-0.5), reads=[rn.name], writes=[rn.name])
                    S.op('dve', lambda: nc.vector.tensor_tensor(out=yt[:, t0:t0 + tn], in0=yt[:, t0:t0 + tn], in1=rn[:, 0:tn], op=ALU.mult),
                         reads=[yt.name, rn.name], writes=[yt.name])

            def bcast_row(src_rows, skey, part0, dst):
                for gi, (t0, tn) in enumerate(tgs):
                    p_ = ps[gi % 2]
                    S.op('pe', lambda: nc.tensor.matmul(p_[:, 0:tn], lhsT=sel[part0:part0 + 6, h, :], rhs=src_rows[:, t0:t0 + tn], start=True, stop=True),
                         reads=[sel.name, skey], writes=[p_.name])
                    if gi % 2 == 0:
                        S.op('act', lambda: nc.scalar.copy(out=dst[:, t0:t0 + tn], in_=p_[:, 0:tn]), reads=[p_.name], pwrites=[dst.name])
                    else:
                        S.op('dve', lambda: nc.vector.tensor_copy(out=dst[:, t0:t0 + tn], in_=p_[:, 0:tn]), reads=[p_.name], pwrites=[dst.name])

            dn_stage = getattr(self, 'dn_stage', 0)
            if dn_stage == 1:
                return
            for h in range(getattr(self, 'dn_heads', 6)):
                bcast_row(ra, ka_, 32, cs_bc)
                S.op('act', lambda: nc.scalar.activation(out=bb[:], in_=cs_bc[:], func=AF.Exp, scale=-1.0), reads=[cs_bc.name], writes=[bb.name])
                conv_silu(h)
                l2norm()
                S.op('pool', lambda: g.tensor_scalar(out=qT[:], in0=yt[:], scalar1=QS, scalar2=None, op0=ALU.mult), reads=[yt.name], writes=[qT.name])
                S.op('dve', lambda: nc.vector.scalar_tensor_tensor(out=qg[:], in0=yt[:], scalar=QS, in1=bb[:], op0=ALU.mult, op1=ALU.mult),
                     reads=[yt.name, bb.name], writes=[qg.name])
                S.op('pool', lambda: g.memset(bb[:, 0:2], 0.0), reads=[bb.name], writes=[bb.name])
                bcast_row(rb, kb_, 0, bb)
                conv_silu(6 + h)
                l2norm()
                S.op('pool', lambda: g.tensor_copy(out=kT[:], in_=yt[:]), reads=[yt.name], writes=[kT.name])
                for b0 in range(0, NB, 8):
                    nb2 = min(8, NB - b0)
                    for b in range(b0, b0 + nb2):
                        S.op('pe', lambda: nc.tensor.transpose(pTb[:, (b - b0) * 128:(b - b0 + 1) * 128], kT[:, b * 128:(b + 1) * 128], self.ident_b[:]),
                             reads=[kT.name, 'ident_b'], writes=[pTb.name])
                    S.op('dve', lambda: nc.vector.tensor_tensor(out=kd_tm[:, b0:b0 + nb2, :], in0=pTb[:, 0:nb2 * 128].rearrange("p (n c) -> p n c", c=128),
                                                                in1=ekd[:, b0:b0 + nb2, h:h + 1].broadcast_to([128, nb2, 128]), op=ALU.mult),
                         reads=[pTb.name, ekd.name], pwrites=[kd_tm.name])
                conv_silu(12 + h)
                for b0 in range(0, NB, 4):
                    nb2 = min(4, NB - b0)
                    p_ = ps[4 + (b0 // 4) % 2]
                    for b in range(b0, b0 + nb2):
                        S.op('pe', lambda: nc.tensor.transpose(p_[:, (b - b0) * 128:(b - b0 + 1) * 128], yt[:, b * 128:(b + 1) * 128], self.ident_f[:]),
                             reads=[yt.name, 'ident_f'], writes=[p_.name])
                    S.op('dve', lambda: nc.vector.tensor_tensor(out=bv_tm[:, b0:b0 + nb2, :], in0=p_[:, 0:nb2 * 128].rearrange("p (n c) -> p n c", c=128),
                                                                in1=beta_tm[:, b0:b0 + nb2, h:h + 1].broadcast_to([128, nb2, 128]), op=ALU.mult),
                         reads=[p_.name, beta_tm.name], pwrites=[bv_tm.name])
                xv = xt[:].rearrange("p (n c) -> p n c", c=128)
                yv = yt[:].rearrange("p (n c) -> p n c", c=128)
                c0 = M_DZ + h * 128
                S.dma('act', xv, self.projM[:, c0:c0 + 128].rearrange("(n p) c -> p n c", p=128), reads=['projM'], writes=[xt.name])
                S.op('act', lambda: nc.scalar.activation(out=yt[:], in_=xt[:], func=AF.Silu), reads=[xt.name], writes=[yt.name])
                S.op('pool', lambda: g.tensor_tensor(out=gate[:], in0=yv, in1=nwt[:].unsqueeze(1).broadcast_to([128, NB, 128]), op=ALU.mult),
                     reads=[yt.name, nwt.name], writes=[gate.name])
                if dn_stage == 2:
                    continue
                for b0 in range(0, NB, GB):
                    nb = min(GB, NB - b0)
                    W = nb * 128
                    cols = slice(b0 * 128, b0 * 128 + W)
                    v3 = lambda t: t[:, 0:W].rearrange("p (n c) -> p n c", c=128)
                    bc3 = lambda t: t[:].unsqueeze(1).broadcast_to([128, nb, 128])
                    csb = cs_tm[:, b0:b0 + nb, h:h + 1].broadcast_to([128, nb, 128])
                    btb = beta_tm[:, b0:b0 + nb, h:h + 1].broadcast_to([128, nb, 128])
                    cs3 = cs_bc[:, cols].rearrange("p (n c) -> p n c", c=128)
                    S.op('dve', lambda: nc.vector.tensor_tensor(out=v3(G1), in0=cs3, in1=csb, op=ALU.subtract), reads=[cs_bc.name, cs_tm.name], writes=[G1.name])
                    S.op('pool', lambda: g.tensor_tensor(out=v3(G2), in0=v3(G1), in1=bc3(mB), op=ALU.add), reads=[G1.name, mB.name], writes=[G2.name])
                    S.op('dve', lambda: nc.vector.tensor_tensor(out=v3(G1), in0=v3(G1), in1=bc3(mA), op=ALU.add), reads=[G1.name, mA.name], writes=[G1.name])
                    S.op('act', lambda: nc.scalar.activation(out=G1[:, 0:W], in_=G1[:, 0:W], func=AF.Exp), reads=[G1.name], writes=[G1.name])
                    S.op('act', lambda: nc.scalar.activation(out=G2[:, 0:W], in_=G2[:, 0:W], func=AF.Exp, scale=-1.0), reads=[G2.name], writes=[G2.name])
                    for j in range(nb):
                        bc = slice((b0 + j) * 128, (b0 + j + 1) * 128)
                        S.op('pe', lambda: nc.tensor.matmul(ps[0][:, j * 128:(j + 1) * 128], lhsT=kT[:, bc], rhs=kT[:, bc], start=True, stop=True), reads=[kT.name], writes=[ps[0].name])
                        S.op('pe', lambda: nc.tensor.matmul(ps[2][:, j * 128:(j + 1) * 128], lhsT=kT[:, bc], rhs=qT[:, bc], start=True, stop=True), reads=[qT.name, kT.name], writes=[ps[2].name])
                    a_, b_ = Ap[0], Bp[0]
                    S.op('pool', lambda: g.tensor_tensor(out=v3(G1), in0=v3(G1), in1=btb, op=ALU.mult), reads=[G1.name, beta_tm.name], writes=[G1.name])
                    S.op('dve', lambda: nc.vector.tensor_tensor(out=a_[:, 0:W], in0=ps[0][:, 0:W], in1=G1[:, 0:W], op=ALU.mult), reads=[ps[0].name, G1.name], writes=[a_.name])
                    S.op('pool', lambda: g.tensor_tensor(out=G3[:, 0:W], in0=G2[:, 0:W], in1=bb[:, cols], op=ALU.mult), reads=[G2.name, bb.name], writes=[G3.name])
                    S.op('dve', lambda: nc.vector.tensor_tensor(out=b_[:, 0:W], in0=ps[0][:, 0:W], in1=G3[:, 0:W], op=ALU.mult), reads=[ps[0].name, G3.name], writes=[b_.name])
                    S.op('pool', lambda: g.tensor_tensor(out=v3(G2), in0=v3(G2), in1=bc3(self.ident_f), op=ALU.add), reads=[G2.name, 'ident_f'], writes=[G2.name])
                    S.op('dve', lambda: nc.vector.tensor_tensor(out=qkT[:, b0:b0 + nb, :], in0=v3(ps[2]), in1=v3(G2), op=ALU.mult), reads=[ps[2].name, G2.name], pwrites=[qkT.name])
                    S.op('pool', lambda: g.tensor_tensor(out=v3(X), in0=bc3(self.ident_f), in1=v3(b_), op=ALU.subtract), reads=[b_.name, 'ident_f'], writes=[X.name])
                    for lv in range(5):
                        a2, b2 = Ap[(lv + 1) % 2], Bp[(lv + 1) % 2]
                        for j in range(nb):
                            jc = slice(j * 128, (j + 1) * 128)
                            S.op('pe', lambda: nc.tensor.matmul(ps[3][:, jc], lhsT=b_[:, jc], rhs=a_[:, jc], start=True, stop=True), reads=[a_.name, b_.name], writes=[ps[3].name])
                            if lv < 4:
                                S.op('pe', lambda: nc.tensor.matmul(ps[4][:, jc], lhsT=a_[:, jc], rhs=b_[:, jc], start=True, stop=True), reads=[a_.name, b_.name], writes=[ps[4].name])
                        S.op('act', lambda: nc.scalar.copy(out=a2[:, 0:W], in_=ps[3][:, 0:W]), reads=[ps[3].name], writes=[a2.name])
                        if lv < 4:
                            S.op('dve', lambda: nc.vector.tensor_copy(out=b2[:, 0:W], in_=ps[4][:, 0:W]), reads=[ps[4].name], writes=[b2.name])
                        for j in range(nb):
                            jc = slice(j * 128, (j + 1) * 128)
                            S.op('pe', lambda: nc.tensor.matmul(ps[5][:, jc], lhsT=a2[:, jc], rhs=X[:, jc], start=True, stop=True), reads=[a2.name, X.name], writes=[ps[5].name])
                        S.op('dve', lambda: nc.vector.tensor_tensor(out=X[:, 0:W], in0=X[:, 0:W], in1=ps[5][:, 0:W], op=ALU.add), reads=[X.name, ps[5].name], writes=[X.name])
                        a_, b_ = a2, b2
                    S.op('pool', lambda: g.tensor_copy(out=Tt[:, b0:b0 + nb, :], in_=v3(X)), reads=[X.name], pwrites=[Tt.name])
                if dn_stage == 3:
                    continue
                S.op('pool', lambda: g.memset(Sst[:], 0.0), writes=[Sst.name])
                S.op('pool', lambda: g.memset(Sb[0][:], 0.0), writes=[Sb[0].name])
                for b in range(NB):
                    po_ = ps[3 + b % 2]
                    for half in range(2):
                        n = 2 * b + half
                        r_ = slice(half * 64, half * 64 + 64)
                        cur, nxt = Sb[n % 2], Sb[(n + 1) % 2]
                        R_, v_ = Rp[n % 2], vn[n % 2]
                        pk, pv, pd = ps[0], ps[1], ps[2]
                        S.op('pe', lambda: nc.tensor.matmul(pk[r_, 0:128], lhsT=kT[:, n * 64:(n + 1) * 64], rhs=cur[:], start=True, stop=True),
                             reads=[kT.name, cur.name], writes=[pk.name])
                        S.op('dve', lambda: nc.vector.scalar_tensor_tensor(out=R_[r_, :], in0=pk[r_, 0:128], scalar=negbeg[r_, b, h:h + 1], in1=bv_tm[r_, b, :],
                                                                           op0=ALU.mult, op1=ALU.add), reads=[pk.name, negbeg.name, bv_tm.name], writes=[R_.name])
                        S.op('pe', lambda: nc.tensor.matmul(pv[r_, 0:128], lhsT=Tt[r_, b, half * 64:half * 64 + 64], rhs=R_[r_, :], start=True, stop=True),
                             reads=[Tt.name, R_.name], writes=[pv.name])
                        S.op('act', lambda: nc.scalar.copy(out=v_[r_, :], in_=pv[r_, 0:128]), reads=[pv.name], writes=[v_.name])
                        S.op('pe', lambda: nc.tensor.matmul(po_[r_, 0:128], lhsT=qg[:, n * 64:(n + 1) * 64], rhs=cur[:], start=True, stop=False),
                             reads=[qg.name, cur.name], writes=[po_.name])
                        S.op('pe', lambda: nc.tensor.matmul(po_[r_, 0:128], lhsT=qkT[r_, b, half * 64:half * 64 + 64], rhs=v_[r_, :], start=False, stop=True),
                             reads=[qkT.name, v_.name], writes=[po_.name])
                        S.op('pe', lambda: nc.tensor.matmul(pd[:, 0:128], lhsT=kd_tm[r_, b, :], rhs=v_[r_, :], start=True, stop=True),
                             reads=[kd_tm.name, v_.name], writes=[pd.name])
                        S.op('dve', lambda: nc.vector.scalar_tensor_tensor(out=Sst[:], in0=Sst[:], scalar=egl[half][:, b, h:h + 1], in1=pd[:, 0:128],
                                                                           op0=ALU.mult, op1=ALU.add), reads=[Sst.name, egl[half].name, pd.name], writes=[Sst.name])
                        S.op('pool', lambda: g.tensor_copy(out=nxt[:], in_=Sst[:]), reads=[Sst.name], writes=[nxt.name])
                    self.head_norm(po_, stat, junk, b, gate, ogh)
                c0 = O_DN + h * 128
                S.dma('sp', self.og[:, c0:c0 + 128].rearrange("(n p) c -> p n c", p=128), ogh[:], reads=[ogh.name], pwrites=[self.ogk(b_) for b_ in range(NB)])


def kernel(x, meta_tokens, norm_w, w_in, conv_w, a_log, dt_bias, dn_norm_w, w_gk2, b_gk,
           gla_norm_w, b_f, w_out, final_norm_w):
    f = lambda a: np.ascontiguousarray(np.asarray(a, dtype=np.float32))
    x = f(x)
    B, S_LEN, _ = x.shape
    L = int(np.asarray(w_in).shape[0])
    prog = Prog(S_LEN, L)
    nc = prog.build()
    common = dict(meta_tokens=f(meta_tokens), norm_w=f(norm_w), w_in=f(w_in), conv_w=f(conv_w), a_log=f(a_log),
                  dt_bias=f(dt_bias), dn_norm_w=f(dn_norm_w), w_gk2=f(w_gk2), b_gk=f(b_gk), gla_norm_w=f(gla_norm_w),
                  b_f=f(b_f), w_out=f(w_out), final_norm_w=f(final_norm_w).reshape(1, -1))
    in_maps = [dict(common, x=np.ascontiguousarray(x[b])) for b in range(B)]
    res = run_bass_kernel_spmd(nc, in_maps, core_ids=list(range(B)))
    return np.stack([np.asarray(r["out"], dtype=np.float32) for r in res.results], 0)
```
